# Optimizing a Trainium2 kernel written in Bass

```python
import math
import jax, jax.numpy as jnp
from jax import lax
import numpy as np

D_MODEL = 2048
BATCH = 1
SEQ = 8192
DEPTH = 2

EPS = 1e-6
CONV_DIM = D_MODEL // 2
CONV_WIDTH = 31
SGU_DIM = D_MODEL // 2
SGU_GROUPS = 8
SGU_GROUP_DIM = SGU_DIM // SGU_GROUPS
CHUNK = 128
SB_HEADS = 16
SB_HEAD_DIM = 64
SB_DIM = SB_HEADS * SB_HEAD_DIM
Q_BLOCK = 128
N_BRANCH = 3
A_COLS = 2 * CONV_DIM
B_COLS = 2 * SGU_DIM
C_COLS = 3 * SB_DIM
G_COLS = N_BRANCH * D_MODEL
IN_DIM = A_COLS + B_COLS + C_COLS + G_COLS
D_FF = 4 * D_MODEL

kernel_name = "hybrid_conv_sgu_stickbreaking_gated_block"


def rms_norm(x, g):
    xf = x.astype(jnp.float32)
    var = jnp.mean(xf * xf, axis=-1, keepdims=True)
    return (xf * lax.rsqrt(var + EPS) * g.astype(jnp.float32)).astype(x.dtype)


def layer_norm(x, g, b):
    xf = x.astype(jnp.float32)
    mu = jnp.mean(xf, axis=-1, keepdims=True)
    var = jnp.mean(jnp.square(xf - mu), axis=-1, keepdims=True)
    y = (xf - mu) * lax.rsqrt(var + EPS) * g.astype(jnp.float32) + b.astype(jnp.float32)
    return y.astype(x.dtype)


def conformer_conv(pa, conv_w, conv_b, ln_g, ln_b):
    val, gate = jnp.split(pa, 2, axis=-1)
    z = val * jax.nn.sigmoid(gate)
    z = lax.conv_general_dilated(
        z, conv_w[:, None, :].astype(z.dtype),
        window_strides=(1,), padding=[(CONV_WIDTH - 1, 0)],
        dimension_numbers=("NWC", "WIO", "NWC"),
        feature_group_count=CONV_DIM) + conv_b
    z = layer_norm(z, ln_g, ln_b)
    return jax.nn.silu(z)


def spatial_gating(pb, ln_g, ln_b, w_s, b_s):
    bsz, seq, _ = pb.shape
    uv = jax.nn.gelu(pb)
    u, v = jnp.split(uv, 2, axis=-1)
    v = layer_norm(v, ln_g, ln_b)
    v = v.reshape(bsz, seq // CHUNK, CHUNK, SGU_GROUPS, SGU_GROUP_DIM)
    tri = jnp.tril(jnp.ones((CHUNK, CHUNK), dtype=bool))
    w = jnp.where(tri[None], w_s, jnp.zeros_like(w_s))
    s = jnp.einsum("gts,bnsgc->bntgc", w.astype(v.dtype), v)
    s = s + jnp.transpose(b_s)[None, None, :, :, None]
    return u * s.reshape(bsz, seq, SGU_DIM)


def stick_breaking_attention(q, k, v):
    seq = q.shape[2]
    scale = 1.0 / math.sqrt(SB_HEAD_DIM)
    outs = []
    for i in range(seq // Q_BLOCK):
        q0, kend = i * Q_BLOCK, (i + 1) * Q_BLOCK
        qb = q[:, :, q0:kend]
        kb = k[:, :, :kend]
        vb = v[:, :, :kend]
        z = jnp.einsum("bhtd,bhsd->bhts", qb, kb).astype(jnp.float32) * scale
        t_pos = q0 + jnp.arange(Q_BLOCK)
        s_pos = jnp.arange(kend)
        causal = s_pos[None, :] < t_pos[:, None]
        log_beta = jax.nn.log_sigmoid(z)
        log_1mb = jnp.where(causal, jax.nn.log_sigmoid(-z), 0.0)
        suffix = lax.cumsum(log_1mb, axis=3, reverse=True) - log_1mb
        a = jnp.where(causal, jnp.exp(log_beta + suffix), 0.0)
        outs.append(jnp.einsum("bhts,bhsd->bhtd", a.astype(vb.dtype), vb))
    return jnp.concatenate(outs, axis=2)


def setup_inputs(seed: int = 0) -> dict:
    key = jax.random.key(seed)
    ks = jax.random.split(key, 20)
    f32 = jnp.float32
    nrm = lambda k, shape, s: jax.random.normal(k, shape, f32) * s
    return {
        "x": jax.random.normal(ks[0], (BATCH, SEQ, D_MODEL), f32),
        "attn_norm_g": 1.0 + nrm(ks[1], (DEPTH, D_MODEL), 0.01),
        "w_in": nrm(ks[2], (DEPTH, D_MODEL, IN_DIM), D_MODEL ** -0.5),
        "b_gate": nrm(ks[3], (DEPTH, G_COLS), 0.02),
        "conv_w": nrm(ks[4], (DEPTH, CONV_WIDTH, CONV_DIM), CONV_WIDTH ** -0.5),
        "conv_b": nrm(ks[5], (DEPTH, CONV_DIM), 0.02),
        "conv_ln_g": 1.0 + nrm(ks[6], (DEPTH, CONV_DIM), 0.01),
        "conv_ln_b": nrm(ks[7], (DEPTH, CONV_DIM), 0.02),
        "sgu_ln_g": 1.0 + nrm(ks[8], (DEPTH, SGU_DIM), 0.01),
        "sgu_ln_b": nrm(ks[9], (DEPTH, SGU_DIM), 0.02),
        "sgu_w": nrm(ks[10], (DEPTH, SGU_GROUPS, CHUNK, CHUNK), CHUNK ** -0.5),
        "sgu_b": 1.0 + nrm(ks[11], (DEPTH, SGU_GROUPS, CHUNK), 0.02),
        "w_out_conv": nrm(ks[12], (DEPTH, CONV_DIM, D_MODEL), CONV_DIM ** -0.5),
        "w_out_sgu": nrm(ks[13], (DEPTH, SGU_DIM, D_MODEL), SGU_DIM ** -0.5),
        "w_out_sb": nrm(ks[14], (DEPTH, SB_DIM, D_MODEL), SB_DIM ** -0.5),
        "w_o": nrm(ks[15], (DEPTH, D_MODEL, D_MODEL), D_MODEL ** -0.5),
        "mlp_norm_g": 1.0 + nrm(ks[16], (DEPTH, D_MODEL), 0.01),
        "w_ff1": nrm(ks[17], (DEPTH, D_MODEL, D_FF), D_MODEL ** -0.5),
        "w_ff2": nrm(ks[18], (DEPTH, D_FF, D_MODEL), D_FF ** -0.5),
        "final_norm_g": 1.0 + nrm(ks[19], (D_MODEL,), 0.01),
    }


def reference(x, attn_norm_g, w_in, b_gate, conv_w, conv_b, conv_ln_g, conv_ln_b,
              sgu_ln_g, sgu_ln_b, sgu_w, sgu_b, w_out_conv, w_out_sgu, w_out_sb,
              w_o, mlp_norm_g, w_ff1, w_ff2, final_norm_g):
    bsz, seq, _ = x.shape
    for l in range(DEPTH):
        h = rms_norm(x, attn_norm_g[l])
        p = jnp.einsum("bsd,de->bse", h, w_in[l])
        pa = p[..., :A_COLS]
        pb = p[..., A_COLS:A_COLS + B_COLS]
        pc = p[..., A_COLS + B_COLS:A_COLS + B_COLS + C_COLS]
        pg = p[..., A_COLS + B_COLS + C_COLS:]

        ya = conformer_conv(pa, conv_w[l], conv_b[l], conv_ln_g[l], conv_ln_b[l])
        ya = jnp.einsum("bsc,cd->bsd", ya, w_out_conv[l])

        yb = spatial_gating(pb, sgu_ln_g[l], sgu_ln_b[l], sgu_w[l], sgu_b[l])
        yb = jnp.einsum("bsc,cd->bsd", yb, w_out_sgu[l])

        qkv = pc.reshape(bsz, seq, 3, SB_HEADS, SB_HEAD_DIM)
        q, k, v = (jnp.transpose(qkv[:, :, j], (0, 2, 1, 3)) for j in range(3))
        yc = stick_breaking_attention(q, k, v)
        yc = jnp.transpose(yc, (0, 2, 1, 3)).reshape(bsz, seq, SB_DIM)
        yc = jnp.einsum("bsc,cd->bsd", yc, w_out_sb[l])

        gates = jax.nn.sigmoid(pg + b_gate[l]).reshape(bsz, seq, N_BRANCH, D_MODEL)
        mixed = gates[:, :, 0] * ya + gates[:, :, 1] * yb + gates[:, :, 2] * yc
        x = x + jnp.einsum("bsd,de->bse", mixed, w_o[l])

        h = rms_norm(x, mlp_norm_g[l])
        f = jnp.square(jax.nn.relu(jnp.einsum("bsd,df->bsf", h, w_ff1[l])))
        x = x + jnp.einsum("bsf,fd->bsd", f, w_ff2[l])
    return rms_norm(x, final_norm_g)
```

```python
import os
import numpy as np
import ml_dtypes
import concourse.bass as bass
import concourse.mybir as mybir
from concourse.bass_utils import run_bass_kernel_spmd

F32 = mybir.dt.float32
BF16 = mybir.dt.bfloat16
I32 = mybir.dt.int32
AF = mybir.ActivationFunctionType
ALU = mybir.AluOpType

NCORES = 8
D = 2048
KC = 16
T = 1024
S = 8192
IN_DIM = 13312
DFF = 8192
EPS = 1e-6
CW = 31
HALO = 30
ZW = T + HALO
ZWP = 1056
C_PA, C_PB, C_Q, C_K, C_V, C_G = 0, 2048, 4096, 5120, 6144, 7168

PV_G1, PV_G2, PV_BG, PV_CW, PV_CB, PV_LG, PV_LB = 0, 16, 32, 80, 80 + 248, 80 + 256, 80 + 264
PV_L = 80 + 272
PV_FG = 2 * PV_L
PV_N = PV_FG + 16

X1 = 3 * 8 * 128 * 1024
XH = 128 * 8 * HALO
X2 = 8 * 128 * 1024


class Tile:
    __slots__ = ("ap", "w", "r")

    def __init__(self, ap, inherit=None):
        self.ap = ap
        self.w = None
        self.r = list(inherit) if inherit else []

    def toks(self):
        return ([self.w] if self.w else []) + list(self.r)


class Prog:
    def __init__(self, nc, sems):
        self.nc = nc
        self.eng = {"pe": nc.tensor, "act": nc.scalar, "dve": nc.vector, "pool": nc.gpsimd, "sp": nc.sync}
        self.sem = sems
        self.cnt = {k: 0 for k in sems}
        self.seen = {e: {} for e in self.eng}
        self.rr = {"sp": 0, "pool": 0}
        self.ndma = {"sp": [k for k in sems if k.startswith("dsp")], "pool": [k for k in sems if k.startswith("dpl")]}

    def _wait(self, e, toks):
        need = {}
        for (s, v) in toks:
            if need.get(s, 0) < v:
                need[s] = v
        for s, v in need.items():
            if s == e and e == "pe":
                continue
            if self.seen[e].get(s, 0) >= v:
                continue
            self.eng[e].wait_ge(self.sem[s], v)
            self.seen[e][s] = v

    def _deps(self, reads, writes):
        toks = []
        for t in reads:
            if t.w:
                toks.append(t.w)
        for t in writes:
            toks.extend(t.toks())
        return toks

    def _commit(self, tok, reads, writes):
        for t in reads:
            t.r.append(tok)
        for t in writes:
            t.w = tok
            t.r = []

    def op(self, e, fn, reads=(), writes=()):
        self._wait(e, self._deps(reads, writes))
        ins = fn(self.eng[e])
        self.cnt[e] += 1
        ins.then_inc(self.sem[e], 1)
        tok = (e, self.cnt[e])
        self._commit(tok, reads, writes)
        return tok

    def dma(self, q, fns, reads=(), writes=()):
        names = self.ndma[q]
        s = names[self.rr[q] % len(names)]
        self.rr[q] += 1
        toks = self._deps(reads, writes)
        toks.append((s, self.cnt[s]))
        self._wait(q, toks)
        for fn in fns:
            ins = fn(self.eng[q])
            ins.then_inc(self.sem[s], 16)
            self.cnt[s] += 16
        tok = (s, self.cnt[s])
        self._commit(tok, reads, writes)
        return tok

    def cc(self, fn, reads=(), writes=()):
        self._wait("pool", self._deps(reads, writes))
        ins = fn(self.eng["pool"])
        self.cnt["cc"] += 1
        ins.then_inc(self.sem["cc"], 1)
        tok = ("cc", self.cnt["cc"])
        self._commit(tok, reads, writes)
        return tok

    def finish(self, e, tiles):
        toks = []
        for t in tiles:
            toks.extend(t.toks())
        self._wait(e, toks)


def _ctx_enter(stack, cm):
    return stack.enter_context(cm)


def build(mode, layer=0, last=False):
    import contextlib
    nc = bass.Bass("TRN2", target_bir_lowering=False)
    fused = mode == "F"
    layers = [0, 1] if fused else [layer]
    stack = contextlib.ExitStack()

    def din(name, shape, dt):
        return nc.dram_tensor(name, list(shape), dt, kind="ExternalInput").ap()

    def dout(name, shape, dt):
        return nc.dram_tensor(name, list(shape), dt, kind="ExternalOutput").ap()

    def dint(name, shape, dt):
        return nc.dram_tensor(name, list(shape), dt).ap()

    W = {}
    need_w = {"A": ["w_in"], "B": ["w_in"], "C": ["w_in", "w_out_conv", "w_out_sgu", "w_out_sb", "w_o", "w_ff1", "w_ff2"]}
    wshapes = {"w_in": (D, IN_DIM), "w_out_conv": (1024, D), "w_out_sgu": (1024, D), "w_out_sb": (1024, D),
               "w_o": (D, D), "w_ff1": (D, DFF), "w_ff2": (DFF, D)}
    win_cols = {"A": 5120, "B": 2048, "C": 6144, "F": IN_DIM}[mode]
    wshapes["w_in"] = (D, win_cols)
    wc = {"A": (lambda c: c if c < 2048 else c - 2048), "B": (lambda c: c - 2048), "C": (lambda c: c - C_G), "F": (lambda c: c)}[mode]
    for l in layers:
        for nm in (need_w["C"] if fused else need_w[mode]):
            W[(nm, l)] = din(f"{nm}{l}", wshapes[nm], F32)
    pv_d = din("pv", (128, PV_N), F32)
    cst_d = din("cst", (128, 5 * 128), F32)
    idx_d = din("idx", (128, 2), I32)
    flag_d = din("flag", (128, 1), F32)
    sgu_d = {}
    if fused or mode == "B":
        for l in layers:
            sgu_d[l] = din(f"sgu{l}", (128, 1024 * 4), F32)

    if fused or mode in ("A", "C"):
        xT_d = din("xT", (D, T), F32)
    if fused:
        out_d = dout("out", (D, T), F32)
        send1 = dint("send1", (16, X1 // 16), BF16)
        g1 = dint("g1", (128, X1 // 16), BF16)
        sendh = dint("sendh", (16, XH // 16), F32)
        gh = dint("gh", (128, XH // 16), F32)
        send2 = dint("send2", (16, X2 // 16), BF16)
        g2 = dint("g2", (128, X2 // 16), BF16)
        xsp = dint("xsp", (128, KC * T), F32)
    else:
        if mode == "A":
            send1 = dout("send1", (16, X1 // 16), BF16)
            sendh = dout("sendh", (16, XH // 16), F32)
            hT_o = dout("hT_o", (128, KC * T), BF16)
            zT_o = dout("zT_o", (128, 8 * ZWP), F32)
        if mode == "B":
            g1 = din("g1", (128, X1 // 16), BF16)
            gh = din("gh", (128, XH // 16), F32)
            hT_i = din("hT_i", (128, KC * T), BF16)
            zT_i = din("zT_i", (128, 8 * ZWP), F32)
            send2 = dout("send2", (16, X2 // 16), BF16)
            if os.environ.get("DBGQ") or os.environ.get("DBGT"):
                dbg_q = dout("dbg_q", (128, 8192), BF16)
                dbg_k = dout("dbg_k", (128, 8192), BF16)
                dbg_v = dout("dbg_v", (128, 8192), BF16)
                dbg_t = dout("dbg_t", (128, 3 * 4 * 512), F32)
            ya_o = dout("ya_o", (128, 8 * T), BF16)
            yb_o = dout("yb_o", (128, 8 * T), BF16)
        if mode == "C":
            g2 = din("g2", (128, X2 // 16), BF16)
            hT_i = din("hT_i", (128, KC * T), BF16)
            ya_i = din("ya_i", (128, 8 * T), BF16)
            yb_i = din("yb_i", (128, 8 * T), BF16)
            out_d = dout("out", (D, T), F32)

    def sb(name, shape, dt):
        return _ctx_enter(stack, nc.sbuf_tensor(name, list(shape), dt))

    XA = sb("XA", (128, KC * T), F32)
    HA = sb("HA", (128, KC * T), BF16)
    WA = sb("WA", (128, 3 * 8192), BF16)
    MA = sb("MA", (128, 16384), BF16)
    TM = sb("TM", (128, 6 * 512), F32)
    PV = sb("PV", (128, PV_N), F32)
    CF = sb("CF", (128, 5 * 128), F32)
    CB = sb("CB", (128, 5 * 128), BF16)
    IDX = sb("IDX", (128, 2), I32)
    FLG = sb("FLG", (128, 1), F32)
    SM = sb("SM", (128, 64), F32)
    HL = sb("HL", (128, 8 * HALO), F32)
    PS = [_ctx_enter(stack, nc.psum_tensor(f"PS{i}", [128, 1024], F32)) for i in range(4)]

    sem_names = ["pe", "act", "dve", "pool", "sp", "cc"] + [f"dsp{i}" for i in range(12)] + [f"dpl{i}" for i in range(12)]
    sems = {n: _ctx_enter(stack, nc.semaphore(n)) for n in sem_names}
    pg = Prog(nc, sems)

    xv = XA[:].rearrange("p (k t) -> p k t", k=KC)
    hv = HA[:].rearrange("p (k t) -> p k t", k=KC)
    xt = [[Tile(xv[:, k, hf * 512:(hf + 1) * 512]) for hf in range(2)] for k in range(KC)]
    ht = [[Tile(hv[:, k, hf * 512:(hf + 1) * 512]) for hf in range(2)] for k in range(KC)]
    bank = [Tile(PS[i // 2][:, (i % 2) * 512:(i % 2 + 1) * 512]) for i in range(8)]
    tmp = [Tile(TM[:, i * 512:(i + 1) * 512]) for i in range(6)]
    pvt = Tile(PV[:])
    cft = Tile(CF[:])
    cbt = Tile(CB[:])
    idxt = Tile(IDX[:])
    flgt = Tile(FLG[:])
    smt = Tile(SM[:])
    hlt = Tile(HL[:])
    onesF = CF[:, 0:128]
    maskLE = CF[:, 128:256]
    maskLT_b = CB[:, 256:384]
    negtri_b = CB[:, 384:512]
    negones_b = CB[:, 512:640]
    zeros_b = CB[:, 0:128]

    state = {"bank": 0, "tmp": 0}

    def nb():
        b = bank[state["bank"] % 8]
        state["bank"] += 1
        return b

    def nt():
        t = tmp[state["tmp"] % 3]
        state["tmp"] += 1
        return t

    LT = [tmp[3], tmp[4], tmp[5]]

    def alltiles(tt):
        return [t for row in tt for t in row]

    XAb = XA[:].bitcast(BF16)
    zv = XA[:, 0:8 * ZWP].rearrange("p (m t) -> p m t", m=8)
    ybv = XAb[:, 0:8192].rearrange("p (m t) -> p m t", m=8)
    ycv = XAb[:, 8192:16384].rearrange("p (m t) -> p m t", m=8)
    yav = XAb[:, 17408:25600].rearrange("p (m t) -> p m t", m=8)
    XFREE = 12800
    accv = [XA[:, XFREE + i * 1024: XFREE + (i + 1) * 1024] for i in range(2)]
    sguc = XA[:, XFREE: XFREE + 3584]

    def load_consts():
        pg.dma("sp", [lambda e: e.dma_start(out=PV[:], in_=pv_d[:, :])], writes=[pvt])
        pg.dma("sp", [lambda e: e.dma_start(out=CF[:], in_=cst_d[:, :])], writes=[cft])
        pg.dma("sp", [lambda e: e.dma_start(out=IDX[:], in_=idx_d[:, :])], writes=[idxt])
        pg.dma("sp", [lambda e: e.dma_start(out=FLG[:], in_=flag_d[:, :])], writes=[flgt])
        pg.op("dve", lambda e: e.tensor_copy(out=CB[:], in_=CF[:]), reads=[cft], writes=[cbt])
        pg.op("dve", lambda e: e.memset(CB[:, 0:128], 0.0), writes=[cbt])

    def wview(wap, rows_kc, c0, ncols):
        return wap.rearrange("(k p) c -> p k c", p=128)[:, 0:rows_kc, c0:c0 + ncols]

    wslots = {}

    def carve_w(n, size, inherit):
        sl = []
        for i in range(n):
            sl.append(Tile(WA[:, i * size:(i + 1) * size], inherit=inherit))
        return sl

    def arena_toks(tiles):
        toks = []
        for t in tiles:
            toks.extend(t.toks())
        best = {}
        for s, v in toks:
            if best.get(s, 0) < v:
                best[s] = v
        return list(best.items())

    def load_w(slot, src, kcs, ncols):
        dst = slot.ap[:, 0:kcs * ncols].rearrange("p (k c) -> p k c", k=kcs)
        step = max(1, 512 // 128 * 1)
        step = 4
        fns = []
        for k0 in range(0, kcs, step):
            k1 = min(kcs, k0 + step)
            fns.append(lambda e, k0=k0, k1=k1: e.dma_start(out=dst[:, k0:k1, :], in_=src[:, k0:k1, :]))
        pg.dma("pool", fns, writes=[slot])
        return dst

    def rmsnorm(gcol0, dst_tiles, dst_f32_out=None):
        for hf in range(2):
            ps = nb()
            for k in range(KC):
                sq = nt()
                pg.op("act", lambda e, k=k, sq=sq: e.activation(out=sq.ap, in_=xt[k][hf].ap, func=AF.Square),
                      reads=[xt[k][hf]], writes=[sq])
                pg.op("pe", lambda e, k=k, sq=sq: e.matmul(ps.ap, lhsT=onesF, rhs=sq.ap, start=(k == 0), stop=(k == KC - 1)),
                      reads=[sq, cft], writes=[ps])
            rs = LT[0]
            pg.op("act", lambda e: e.activation(out=rs.ap, in_=ps.ap, func=AF.Sqrt, bias=EPS, scale=1.0 / D),
                  reads=[ps], writes=[rs])
            pg.op("dve", lambda e: e.reciprocal(out=rs.ap, in_=rs.ap), reads=[rs], writes=[rs])
            for k in range(KC):
                if dst_f32_out is None:
                    pg.op("dve", lambda e, k=k: e.scalar_tensor_tensor(
                        out=dst_tiles[k][hf].ap, in0=xt[k][hf].ap, scalar=PV[:, gcol0 + k:gcol0 + k + 1],
                        in1=rs.ap, op0=ALU.mult, op1=ALU.mult),
                        reads=[xt[k][hf], rs, pvt], writes=[dst_tiles[k][hf]])
                else:
                    o = nt()
                    pg.op("dve", lambda e, k=k, o=o: e.scalar_tensor_tensor(
                        out=o.ap, in0=xt[k][hf].ap, scalar=PV[:, gcol0 + k:gcol0 + k + 1],
                        in1=rs.ap, op0=ALU.mult, op1=ALU.mult),
                        reads=[xt[k][hf], rs, pvt], writes=[o])
                    final_toks.append(pg.dma("sp", [lambda e, k=k, o=o: e.dma_start(
                        out=dst_f32_out[k * 128:(k + 1) * 128, hf * 512:(hf + 1) * 512], in_=o.ap)],
                        reads=[o]))

    def load_x(src):
        if isinstance(src, tuple):
            fns = [lambda e, k0=k0: e.dma_start(out=XA[:, k0 * T:(k0 + 4) * T], in_=xsp[:, k0 * T:(k0 + 4) * T]) for k0 in range(0, KC, 4)]
            pg.dma("sp", fns, reads=[src[1]], writes=alltiles(xt))
            return
        v = src.rearrange("(k p) t -> p k t", p=128)
        fns = [lambda e, k0=k0: e.dma_start(out=xv[:, k0:k0 + 4, :], in_=v[:, k0:k0 + 4, :]) for k0 in range(0, KC, 4)]
        pg.dma("sp", fns, writes=alltiles(xt))

    xa_tiles = []
    final_toks = []
    ma_last = []

    def nb2():
        if state["bank"] % 2:
            state["bank"] += 1
        b0 = bank[state["bank"] % 8]
        b1 = bank[(state["bank"] + 1) % 8]
        state["bank"] += 2
        return b0, b1

    def mm_fm(ps, wdst, kcs, col0, rhs_tiles_fn):
        for k in range(kcs):
            rt, rap = rhs_tiles_fn(k)
            pg.op("pe", lambda e, k=k, rap=rap: e.matmul(ps.ap, lhsT=wdst[:, k, col0:col0 + 128], rhs=rap,
                                                          start=(k == 0), stop=(k == kcs - 1)),
                  reads=[rt, wdst_tile[0]], writes=[ps])

    wdst_tile = [None]

    def phase_A(l, send1_ap, sendh_ap, carve_inherit):
        win = W[("w_in", l)]
        rmsnorm(l * PV_L + PV_G1, ht)
        if fused:
            fns = [lambda e, k0=k0: e.dma_start(out=xsp[:, k0 * T:(k0 + 4) * T], in_=XA[:, k0 * T:(k0 + 4) * T]) for k0 in range(0, KC, 4)]
            xsp_t = Tile(None)
            pg.dma("sp", fns, reads=alltiles(xt), writes=[xsp_t])
        else:
            xsp_t = None
        slots = carve_w(3, 8192, carve_inherit)
        s1f = send1_ap.rearrange("a b -> (a b)")
        sec = [s1f[i * 1048576:(i + 1) * 1048576] for i in range(3)]
        qsec = [sec[i].rearrange("(j p t) -> j p t", j=8, p=128) for i in range(2)]
        vsec = sec[2].rearrange("(j p b f) -> j p b f", j=8, p=128, b=8)
        send1_t = Tile(None)
        minh = arena_toks(ma_last)
        qst = [Tile(MA[:, i * 1024:(i + 1) * 1024], inherit=minh) for i in range(2)]
        vst = [Tile(MA[:, 2048 + i * 4096: 2048 + (i + 1) * 4096], inherit=minh) for i in range(2)]
        chunks = [("q", C_Q, 0), ("q", C_Q + 512, 1), ("k", C_K, 0), ("k", C_K + 512, 1), ("v", C_V, 0), ("v", C_V + 512, 1),
                  ("pv", C_PA, 0), ("pg", C_PA + 1024, 0), ("pv", C_PA + 512, 1), ("pg", C_PA + 1536, 1)]
        xinh = arena_toks(alltiles(xt))
        zt = [[Tile(zv[:, m, HALO + hf * 512: HALO + (hf + 1) * 512], inherit=xinh) for hf in range(2)] for m in range(8)]
        zhalo_t = [Tile(zv[:, m, 0:HALO], inherit=xinh) for m in range(8)]
        xa_tiles.extend(alltiles(zt) + zhalo_t)
        dsts = {}

        def issue(i):
            kind, c0, ix = chunks[i]
            dsts[i] = (slots[i % 3], load_w(slots[i % 3], wview(win, KC, wc(c0), 512), KC, 512))

        issue(0)
        issue(1)
        qn = 0
        for i, (kind, c0, ix) in enumerate(chunks):
            if i + 2 < len(chunks) and kind != "pg":
                issue(i + 2)
            slot, wd = dsts[i]
            wdst_tile[0] = slot
            if kind in ("q", "k"):
                for jb in range(4):
                    j = ix * 4 + jb
                    st = qst[qn % 2]
                    qn += 1
                    for hf in range(2):
                        ps = nb()
                        mm_fm(ps, wd, KC, jb * 128, lambda k: (ht[k][hf], ht[k][hf].ap))
                        pg.op("act", lambda e, ps=ps, st=st, hf=hf: e.activation(
                            out=st.ap[:, hf * 512:(hf + 1) * 512], in_=ps.ap, func=AF.Copy,
                            scale=(0.125 if kind == "q" else 1.0)), reads=[ps], writes=[st])
                    sidx = 0 if kind == "q" else 1
                    pg.dma("sp", [lambda e, st=st, j=j, sidx=sidx: e.dma_start(out=qsec[sidx][j], in_=st.ap)],
                           reads=[st], writes=[send1_t])
            elif kind == "v":
                st = vst[ix % 2]
                stv = st.ap.rearrange("p (b f) -> p b f", b=8)
                for tb in range(8):
                    ps = nb()
                    hf = tb // 4
                    for k in range(KC):
                        pg.op("pe", lambda e, k=k, ps=ps, tb=tb: e.matmul(
                            ps.ap, lhsT=hv[:, k, tb * 128:(tb + 1) * 128], rhs=wd[:, k, :],
                            start=(k == 0), stop=(k == KC - 1)), reads=[ht[k][hf], slot], writes=[ps])
                    pg.op("dve", lambda e, ps=ps, tb=tb: e.tensor_copy(out=stv[:, tb, :], in_=ps.ap), reads=[ps], writes=[st])
                for jj in range(4):
                    j = ix * 4 + jj
                    pg.dma("sp", [lambda e, j=j, jj=jj: e.dma_start(out=vsec[j], in_=stv[:, :, jj * 128:(jj + 1) * 128])],
                           reads=[st], writes=[send1_t])
            elif kind == "pv":
                pass
            elif kind == "pg":
                slot_v, wd_v = dsts[i - 1]
                for mb in range(4):
                    m = ix * 4 + mb
                    for hf in range(2):
                        psv = nb()
                        psg = nb()
                        wdst_tile[0] = slot_v
                        mm_fm(psv, wd_v, KC, mb * 128, lambda k: (ht[k][hf], ht[k][hf].ap))
                        wdst_tile[0] = slot
                        mm_fm(psg, wd, KC, mb * 128, lambda k: (ht[k][hf], ht[k][hf].ap))
                        sg = nt()
                        pg.op("act", lambda e, psg=psg, sg=sg: e.activation(out=sg.ap, in_=psg.ap, func=AF.Sigmoid),
                              reads=[psg], writes=[sg])
                        pg.op("dve", lambda e, psv=psv, sg=sg, m=m, hf=hf: e.tensor_tensor(
                            out=zt[m][hf].ap, in0=psv.ap, in1=sg.ap, op=ALU.mult),
                            reads=[psv, sg], writes=[zt[m][hf]])
                if i + 2 < len(chunks):
                    issue(i + 2)
        shv = sendh_ap.rearrange("a b -> (a b)").rearrange("(p m t) -> p m t", p=128, m=8)
        sendh_t = Tile(None)
        pg.dma("sp", [lambda e: e.dma_start(out=shv, in_=zv[:, :, T:T + HALO])], reads=[zt[m][1] for m in range(8)], writes=[sendh_t])
        return dict(zt=zt, zhalo_t=zhalo_t, send1_t=send1_t, sendh_t=sendh_t, slots=slots, xsp_t=xsp_t, qst=qst, vst=vst)

    def phase_B(l, A, g1_ap, gh_ap, g1_t, gh_t, send2_ap):
        win = W[("w_in", l)]
        zt, zhalo_t = A["zt"], A["zhalo_t"]
        pvl = l * PV_L
        xinh = arena_toks(alltiles(xt))
        yat = [[Tile(yav[:, m, hf * 512:(hf + 1) * 512], inherit=xinh) for hf in range(2)] for m in range(8)]
        xa_tiles.extend(alltiles(yat))
        ghv = gh_ap.rearrange("a b -> (a b)").rearrange("(n c) -> n c", c=8 * HALO)
        pg.dma("pool", [lambda e: e.indirect_dma_start(
            out=HL[:], out_offset=None, in_=ghv, in_offset=bass.IndirectOffsetOnAxis(ap=IDX[:, 1:2], axis=0))],
            reads=[gh_t, idxt], writes=[hlt])
        hlv = HL[:].rearrange("p (m t) -> p m t", m=8)
        pg.op("dve", lambda e: e.tensor_scalar(out=zv[:, :, 0:HALO], in0=hlv, scalar1=FLG[:, 0:1], scalar2=None, op0=ALU.mult),
              reads=[hlt, flgt], writes=zhalo_t)
        acct = [Tile(accv[i], inherit=xinh) for i in range(2)]
        xa_tiles.extend(acct)
        s1 = [nb(), nb()]
        s2 = [nb(), nb()]
        cw0 = pvl + PV_CW
        for m in range(8):
            acc = acct[m % 2]
            zall = [zt[m][0], zt[m][1], zhalo_t[m]]
            pg.op("dve", lambda e, m=m, acc=acc: e.tensor_scalar(
                out=acc.ap, in0=zv[:, m, 0:T], scalar1=PV[:, cw0 + m * CW: cw0 + m * CW + 1],
                scalar2=PV[:, pvl + PV_CB + m: pvl + PV_CB + m + 1], op0=ALU.mult, op1=ALU.add),
                reads=zall + [pvt], writes=[acc])
            for k in range(1, CW):
                pg.op("dve", lambda e, m=m, acc=acc, k=k: e.scalar_tensor_tensor(
                    out=acc.ap, in0=zv[:, m, k:k + T], scalar=PV[:, cw0 + m * CW + k: cw0 + m * CW + k + 1],
                    in1=acc.ap, op0=ALU.mult, op1=ALU.add), reads=zall + [acc, pvt], writes=[acc])
            for hf in range(2):
                sq = nt()
                pg.op("act", lambda e, acc=acc, sq=sq, hf=hf: e.activation(out=sq.ap, in_=acc.ap[:, hf * 512:(hf + 1) * 512], func=AF.Square),
                      reads=[acc], writes=[sq])
                pg.op("pe", lambda e, acc=acc, hf=hf, m=m: e.matmul(s1[hf].ap, lhsT=onesF, rhs=acc.ap[:, hf * 512:(hf + 1) * 512],
                                                                   start=(m == 0), stop=(m == 7)), reads=[acc, cft], writes=[s1[hf]])
                pg.op("pe", lambda e, sq=sq, hf=hf, m=m: e.matmul(s2[hf].ap, lhsT=onesF, rhs=sq.ap,
                                                                  start=(m == 0), stop=(m == 7)), reads=[sq, cft], writes=[s2[hf]])
                pg.op("act", lambda e, acc=acc, hf=hf, m=m: e.activation(out=zt[m][hf].ap, in_=acc.ap[:, hf * 512:(hf + 1) * 512], func=AF.Copy),
                      reads=[acc], writes=[zt[m][hf]])
        for hf in range(2):
            mean, var, rstd = LT[0], LT[1], LT[2]
            pg.op("act", lambda e: e.activation(out=mean.ap, in_=s1[hf].ap, func=AF.Copy, scale=1.0 / 1024), reads=[s1[hf]], writes=[mean])
            pg.op("dve", lambda e: e.tensor_tensor(out=var.ap, in0=mean.ap, in1=mean.ap, op=ALU.mult), reads=[mean], writes=[var])
            pg.op("dve", lambda e: e.scalar_tensor_tensor(out=var.ap, in0=s2[hf].ap, scalar=1.0 / 1024, in1=var.ap,
                                                          op0=ALU.mult, op1=ALU.subtract), reads=[s2[hf], var], writes=[var])
            pg.op("act", lambda e: e.activation(out=rstd.ap, in_=var.ap, func=AF.Sqrt, bias=EPS, scale=1.0), reads=[var], writes=[rstd])
            pg.op("dve", lambda e: e.reciprocal(out=rstd.ap, in_=rstd.ap), reads=[rstd], writes=[rstd])
            for m in range(8):
                t1 = nt()
                pg.op("dve", lambda e, m=m, t1=t1: e.tensor_tensor(out=t1.ap, in0=zt[m][hf].ap, in1=mean.ap, op=ALU.subtract),
                      reads=[zt[m][hf], mean], writes=[t1])
                pg.op("dve", lambda e, t1=t1: e.tensor_tensor(out=t1.ap, in0=t1.ap, in1=rstd.ap, op=ALU.mult),
                      reads=[t1, rstd], writes=[t1])
                pg.op("act", lambda e, m=m, t1=t1: e.activation(
                    out=yat[m][hf].ap, in_=t1.ap, func=AF.Silu,
                    scale=PV[:, pvl + PV_LG + m: pvl + PV_LG + m + 1], bias=PV[:, pvl + PV_LB + m: pvl + PV_LB + m + 1]),
                    reads=[t1, pvt], writes=[yat[m][hf]])
        zdead = [t for row in zt for t in row] + zhalo_t
        ybt = [Tile(ybv[:, :, tb * 128:(tb + 1) * 128], inherit=arena_toks(zdead)) for tb in range(8)]
        sgc_t = Tile(sguc, inherit=arena_toks(acct))
        xa_tiles.extend(ybt + [sgc_t])
        pg.dma("sp", [lambda e: e.dma_start(out=sguc[:, 0:3072], in_=sgu_d[l][:, 0:3072])], writes=[sgc_t])
        lng_bc = sguc[:, 0:1024]
        lnb_bc = sguc[:, 1024:2048]
        bs_bc = sguc[:, 2048:3072].rearrange("p (g t) -> p g t", g=8)
        wTm = sguc[:, 3072:3584].bitcast(BF16).rearrange("p (g t) -> p g t", g=8)
        wtmp = [LT[0], LT[1]]
        for i in range(2):
            pg.dma("sp", [lambda e, i=i: e.dma_start(out=wtmp[i].ap, in_=sgu_d[l][:, 3072 + i * 512: 3072 + (i + 1) * 512])], writes=[wtmp[i]])
            for gg in range(4):
                g = i * 4 + gg
                pg.op("dve", lambda e, i=i, gg=gg, g=g: e.tensor_tensor(out=wTm[:, g, :], in0=wtmp[i].ap[:, gg * 128:(gg + 1) * 128],
                                                                        in1=maskLE, op=ALU.mult), reads=[wtmp[i], cft], writes=[sgc_t])
        slots = A["slots"]
        uT = MA[:, 0:8192].rearrange("p (m t) -> p m t", m=8)
        stg = A["qst"] + A["vst"]
        ut = [[Tile(uT[:, m, hf * 512:(hf + 1) * 512], inherit=arena_toks(stg)) for hf in range(2)] for m in range(8)]
        vn_t = [Tile(MA[:, 8192 + i * 1024: 8192 + (i + 1) * 1024], inherit=arena_toks(stg)) for i in range(2)]
        vg_t = [Tile(MA[:, 10240 + i * 2048: 10240 + (i + 1) * 2048], inherit=arena_toks(stg)) for i in range(2)]
        chunks = [C_PB, C_PB + 512, C_PB + 1024, C_PB + 1536]
        dsts = {}
        for i in range(3):
            dsts[i] = (slots[i % 3], load_w(slots[i % 3], wview(win, KC, wc(chunks[i]), 512), KC, 512))
        for i in range(2):
            slot, wd = dsts[i]
            wdst_tile[0] = slot
            for mb in range(4):
                m = i * 4 + mb
                for hf in range(2):
                    ps = nb()
                    mm_fm(ps, wd, KC, mb * 128, lambda k: (ht[k][hf], ht[k][hf].ap))
                    pg.op("act", lambda e, ps=ps, m=m, hf=hf: e.activation(out=ut[m][hf].ap, in_=ps.ap, func=AF.Gelu_apprx_tanh),
                          reads=[ps], writes=[ut[m][hf]])
        dsts[3] = (slots[0], load_w(slots[0], wview(win, KC, wc(chunks[3]), 512), KC, 512))
        for tb in range(8):
            hf = tb // 4
            vg = vg_t[tb % 2]
            vgf = vg.ap.bitcast(F32)
            vn = vn_t[tb % 2]
            for i in range(2):
                slot, wd = dsts[2 + i]
                ps = nb()
                for k in range(KC):
                    pg.op("pe", lambda e, k=k, ps=ps, wd=wd: e.matmul(ps.ap, lhsT=hv[:, k, tb * 128:(tb + 1) * 128], rhs=wd[:, k, :],
                                                                      start=(k == 0), stop=(k == KC - 1)), reads=[ht[k][hf], slot], writes=[ps])
                pg.op("act", lambda e, ps=ps, i=i: e.activation(out=vgf[:, i * 512:(i + 1) * 512], in_=ps.ap, func=AF.Gelu_apprx_tanh,
                                                              accum_out=SM[:, i:i + 1]), reads=[ps], writes=[vg, smt])
                junk = nt()
                pg.op("act", lambda e, i=i, junk=junk: e.activation(out=junk.ap, in_=vgf[:, i * 512:(i + 1) * 512], func=AF.Square,
                                                                    accum_out=SM[:, 2 + i:3 + i]), reads=[vg], writes=[junk, smt])
            pg.op("dve", lambda e: e.tensor_tensor(out=SM[:, 4:5], in0=SM[:, 0:1], in1=SM[:, 1:2], op=ALU.add), reads=[smt], writes=[smt])
            pg.op("dve", lambda e: e.tensor_tensor(out=SM[:, 5:6], in0=SM[:, 2:3], in1=SM[:, 3:4], op=ALU.add), reads=[smt], writes=[smt])
            pg.op("dve", lambda e: e.tensor_scalar(out=SM[:, 4:6], in0=SM[:, 4:6], scalar1=1.0 / 1024, scalar2=None, op0=ALU.mult), reads=[smt], writes=[smt])
            pg.op("dve", lambda e: e.tensor_tensor(out=SM[:, 6:7], in0=SM[:, 4:5], in1=SM[:, 4:5], op=ALU.mult), reads=[smt], writes=[smt])
            pg.op("dve", lambda e: e.tensor_tensor(out=SM[:, 6:7], in0=SM[:, 5:6], in1=SM[:, 6:7], op=ALU.subtract), reads=[smt], writes=[smt])
            pg.op("act", lambda e: e.activation(out=SM[:, 7:8], in_=SM[:, 6:7], func=AF.Sqrt, bias=EPS, scale=1.0), reads=[smt], writes=[smt])
            pg.op("dve", lambda e: e.reciprocal(out=SM[:, 7:8], in_=SM[:, 7:8]), reads=[smt], writes=[smt])
            pg.op("dve", lambda e: e.tensor_scalar(out=vgf, in0=vgf, scalar1=SM[:, 4:5], scalar2=SM[:, 7:8], op0=ALU.subtract, op1=ALU.mult),
                  reads=[vg, smt], writes=[vg])
            pg.op("dve", lambda e: e.tensor_tensor(out=vgf, in0=vgf, in1=lng_bc, op=ALU.mult), reads=[vg, sgc_t], writes=[vg])
            pg.op("dve", lambda e: e.tensor_tensor(out=vn.ap, in0=vgf, in1=lnb_bc, op=ALU.add), reads=[vg, sgc_t], writes=[vn])
            b0, b1 = nb2()
            psq = PS[bank.index(b0) // 2][:, :].rearrange("p (g t) -> p g t", g=8)
            for g in range(8):
                bt = b0 if g < 4 else b1
                pg.op("pe", lambda e, g=g: e.matmul(psq[:, g, :], lhsT=vn.ap[:, g * 128:(g + 1) * 128], rhs=wTm[:, g, :], start=True, stop=True),
                      reads=[vn, sgc_t], writes=[bt])
            vg3 = vgf.rearrange("p (g t) -> p g t", g=8)
            pg.op("dve", lambda e: e.tensor_tensor(out=vg3, in0=psq, in1=bs_bc, op=ALU.add), reads=[b0, b1, sgc_t], writes=[vg])
            pg.op("dve", lambda e, tb=tb: e.tensor_tensor(out=ybt[tb].ap, in0=vg3, in1=uT[:, :, tb * 128:(tb + 1) * 128], op=ALU.mult),
                  reads=[vg] + [ut[m][hf] for m in range(8)], writes=[ybt[tb]])
        wold = arena_toks(slots)
        qT = WA[:, 0:8192]
        kT = WA[:, 8192:16384]
        Vv = WA[:, 16384:24576].rearrange("p (b f) -> p b f", b=64)
        qt_ = [Tile(qT[:, g * 512:(g + 1) * 512], inherit=wold) for g in range(16)]
        kt_ = [Tile(kT[:, r * 1024:(r + 1) * 1024], inherit=wold) for r in range(8)]
        vt_ = [Tile(Vv[:, r * 8:(r + 1) * 8, :], inherit=wold) for r in range(8)]
        g1f = g1_ap.rearrange("a b -> (a b)")
        off = bass.IndirectOffsetOnAxis(ap=IDX[:, 0:1], axis=0)
        qsrc0 = g1f[0:1048576].rearrange("(n t) -> n t", t=1024)
        vsrc0 = g1f[0:1048576].rearrange("(n b f) -> n b f", b=8, f=128)
        for r in range(8):
            base = r * X1
            pg.dma("pool", [lambda e, r=r: e.indirect_dma_start(out=qT[:, r * 1024:(r + 1) * 1024], out_offset=None, in_=qsrc0, in_offset=off,
                                                                 element_offset=base)],
                   reads=[g1_t, idxt], writes=[qt_[2 * r], qt_[2 * r + 1]])
            pg.dma("pool", [lambda e, r=r: e.indirect_dma_start(out=kT[:, r * 1024:(r + 1) * 1024], out_offset=None, in_=qsrc0, in_offset=off,
                                                                 element_offset=base + 1048576)],
                   reads=[g1_t, idxt], writes=[kt_[r]])
            pg.dma("pool", [lambda e, r=r: e.indirect_dma_start(out=WA[:, 16384 + r * 1024: 16384 + (r + 1) * 1024], out_offset=None, in_=qsrc0, in_offset=off,
                                                                 element_offset=base + 2 * 1048576)],
                   reads=[g1_t, idxt], writes=[vt_[r]])
        if not fused and os.environ.get('DBGQ'):
            final_toks.append(pg.dma("sp", [lambda e: e.dma_start(out=dbg_q[:, :], in_=qT)], reads=qt_))
            final_toks.append(pg.dma("sp", [lambda e: e.dma_start(out=dbg_k[:, :], in_=kT)], reads=kt_))
            final_toks.append(pg.dma("sp", [lambda e: e.dma_start(out=dbg_v[:, :], in_=WA[:, 16384:24576])], reads=vt_))
        mold = arena_toks(alltiles(ut) + vn_t + vg_t)
        SPb = [Tile(MA[:, i * 512:(i + 1) * 512], inherit=mold) for i in range(3)]
        Ab = [Tile(MA[:, 1536 + i * 512: 1536 + (i + 1) * 512], inherit=mold) for i in range(3)]
        Rb = [[Tile(MA[:, 3072 + (pp * 2 + hh) * 512: 3072 + (pp * 2 + hh + 1) * 512], inherit=mold) for hh in range(2)] for pp in range(2)]
        Eb = [tmp[0], tmp[1], tmp[2]]
        psZ = [bank[i] for i in range(5)]
        psO = [bank[5], bank[6]]
        tiles = []
        for qg in range(16):
            for kb in range(4 * qg + 3, -1, -1):
                for hh in range(2):
                    c0 = max(0, kb - 4 * qg) * 128
                    tiles.append(dict(qg=qg, kb=kb, hh=hh, c0=c0, n=512 - c0, first=(kb == 4 * qg + 3), last=(kb == 0), diag=(kb >= 4 * qg)))
        NT = len(tiles)

        def S0(i):
            t = tiles[i]
            qg, kb, hh, c0, n = t["qg"], t["kb"], t["hh"], t["c0"], t["n"]
            if t["first"] and hh == 0:
                po = psO[qg % 2]
                pg.op("pe", lambda e: e.matmul(po.ap, lhsT=zeros_b, rhs=qT[:, qg * 512:(qg + 1) * 512], start=True, stop=False),
                      reads=[cbt, qt_[qg]], writes=[po])
                for h2 in range(2):
                    pg.op("pool", lambda e, h2=h2: e.memset(Rb[qg % 2][h2].ap, 0.0), writes=[Rb[qg % 2][h2]])
            pz = psZ[i % 5]
            pg.op("pe", lambda e: e.matmul(pz.ap[:, 0:n], lhsT=kT[hh * 64:(hh + 1) * 64, kb * 128:(kb + 1) * 128],
                                           rhs=qT[hh * 64:(hh + 1) * 64, qg * 512 + c0:(qg + 1) * 512], start=True, stop=False),
                  reads=[kt_[kb // 8], qt_[qg]], writes=[pz])

        def S1(i):
            t = tiles[i]
            n = t["n"]
            pz, E, SP = psZ[i % 5], Eb[i % 3], SPb[i % 3]
            pg.op("act", lambda e: e.activation(out=E.ap[:, 0:n], in_=pz.ap[:, 0:n], func=AF.Exp), reads=[pz], writes=[E])
            pg.op("act", lambda e: e.activation(out=SP.ap[:, 0:n], in_=E.ap[:, 0:n], func=AF.Ln, bias=1.0, scale=1.0), reads=[E], writes=[SP])
            if t["diag"]:
                pg.op("dve", lambda e: e.tensor_tensor(out=SP.ap[:, 0:128], in0=SP.ap[:, 0:128], in1=maskLT_b, op=ALU.mult),
                      reads=[SP, cbt], writes=[SP])

        def S2(i):
            t = tiles[i]
            qg, hh, c0, n = t["qg"], t["hh"], t["c0"], t["n"]
            pz, SP = psZ[i % 5], SPb[i % 3]
            R = Rb[qg % 2][hh]
            pg.op("pe", lambda e: e.matmul(pz.ap[:, 0:n], lhsT=negtri_b, rhs=SP.ap[:, 0:n], start=False, stop=t["first"]),
                  reads=[SP, cbt], writes=[pz])
            if not t["first"]:
                pg.op("pe", lambda e: e.matmul(pz.ap[:, 0:n], lhsT=negones_b, rhs=R.ap[:, c0:512], start=False, stop=True),
                      reads=[R, cbt], writes=[pz])
            if not t["last"]:
                pg.op("pool", lambda e: e.tensor_tensor(out=R.ap[:, c0:512], in0=R.ap[:, c0:512], in1=SP.ap[:, 0:n], op=ALU.add),
                      reads=[R, SP], writes=[R])

        def S3(i):
            t = tiles[i]
            n = t["n"]
            pz, A_ = psZ[i % 5], Ab[i % 3]
            pg.op("act", lambda e: e.activation(out=A_.ap[:, 0:n], in_=pz.ap[:, 0:n], func=AF.Exp), reads=[pz], writes=[A_])
            if t["diag"]:
                pg.op("dve", lambda e: e.tensor_tensor(out=A_.ap[:, 0:128], in0=A_.ap[:, 0:128], in1=maskLT_b, op=ALU.mult),
                      reads=[A_, cbt], writes=[A_])

        def S4(i):
            t = tiles[i]
            qg, kb, hh, c0, n = t["qg"], t["kb"], t["hh"], t["c0"], t["n"]
            A_ = Ab[i % 3]
            po = psO[qg % 2]
            pg.op("pe", lambda e: e.matmul(po.ap[hh * 64:(hh + 1) * 64, c0:512], lhsT=Vv[:, kb, hh * 64:(hh + 1) * 64], rhs=A_.ap[:, 0:n],
                                           start=False, stop=(t["last"])), reads=[A_, vt_[kb // 8]], writes=[po])
            if t["last"] and hh == 1:
                pg.op("act", lambda e: e.activation(out=qt_[qg].ap, in_=po.ap, func=AF.Copy), reads=[po], writes=[qt_[qg]])

        DBG = {0: 0, 6: 1, 16: 2} if (not fused and os.environ.get('DBGT')) else {}
        dscr = Tile(HL[:])

        def dump(i, which, src_tile, src_ap, is_bf):
            slot = DBG[i]
            col = (slot * 4 + which) * 512
            if is_bf:
                o = LT[which % 3]
                pg.op("dve", lambda e: e.tensor_copy(out=o.ap, in_=src_ap), reads=[src_tile], writes=[o])
                final_toks.append(pg.dma("sp", [lambda e: e.dma_start(out=dbg_t[:, col:col + 512], in_=o.ap)], reads=[o]))
            else:
                final_toks.append(pg.dma("sp", [lambda e: e.dma_start(out=dbg_t[:, col:col + 512], in_=src_ap)], reads=[src_tile]))

        for it in range(NT + 4):
            if it < NT:
                S0(it)
            if 0 <= it - 1 < NT:
                S1(it - 1)
                if (it - 1) in DBG:
                    dump(it - 1, 0, Eb[(it - 1) % 3], Eb[(it - 1) % 3].ap, False)
                    dump(it - 1, 1, SPb[(it - 1) % 3], SPb[(it - 1) % 3].ap, True)
            if 0 <= it - 2 < NT:
                S2(it - 2)
            if 0 <= it - 3 < NT:
                if (it - 3) in DBG:
                    o = LT[2]
                    pzz = psZ[(it - 3) % 5]
                    pg.op("dve", lambda e: e.tensor_copy(out=o.ap, in_=pzz.ap), reads=[pzz], writes=[o])
                    dump(it - 3, 2, o, o.ap, False)
                S3(it - 3)
                if (it - 3) in DBG:
                    dump(it - 3, 3, Ab[(it - 3) % 3], Ab[(it - 3) % 3].ap, True)
            if 0 <= it - 4 < NT:
                S4(it - 4)
        s2v = send2_ap.rearrange("a b -> (a b)").rearrange("(r p t) -> r p t", r=8, p=128)
        send2_t = Tile(None)
        for r in range(8):
            pg.dma("sp", [lambda e, r=r: e.dma_start(out=s2v[r], in_=qT[:, r * 1024:(r + 1) * 1024])],
                   reads=[qt_[2 * r], qt_[2 * r + 1]], writes=[send2_t])
        return dict(yat=yat, ybt=ybt, send2_t=send2_t, wa_tiles=qt_ + kt_ + vt_, ma_tiles=SPb + Ab + Rb[0] + Rb[1], yc_inherit=arena_toks(zdead))

    def phase_C(l, Bst, g2_ap, g2_t, x_src, final):
        win = W[("w_in", l)]
        pvl = l * PV_L
        yat, ybt = Bst["yat"], Bst["ybt"]
        yct = [Tile(ycv[:, j, :], inherit=Bst.get("yc_inherit")) for j in range(8)]
        xa_tiles.extend(yct)
        g2f = g2_ap.rearrange("a b -> (a b)")
        off = bass.IndirectOffsetOnAxis(ap=IDX[:, 0:1], axis=0)
        src0 = g2f[0:X2].rearrange("(n t) -> n t", t=1024)
        for j in range(8):
            pg.dma("pool", [lambda e, j=j: e.indirect_dma_start(out=ycv[:, j, :], out_offset=None, in_=src0, in_offset=off, element_offset=j * X2)],
                   reads=[g2_t, idxt], writes=[yct[j]])
        wold = arena_toks(Bst["wa_tiles"])
        slots = [Tile(WA[:, i * 4096:(i + 1) * 4096], inherit=wold) for i in range(4)]
        gt = Tile(WA[:, 16384:20480], inherit=wold)
        mt = Tile(WA[:, 20480:24576], inherit=wold)
        gv = WA[:, 16384:20480].bitcast(F32).rearrange("p (m t) -> p m t", m=2)
        mv = WA[:, 20480:24576].bitcast(F32).rearrange("p (m t) -> p m t", m=2)
        mold = arena_toks(Bst["ma_tiles"])
        mxv = MA[:].rearrange("p (k t) -> p k t", k=KC)
        mxt = [[Tile(mxv[:, k, hf * 512:(hf + 1) * 512], inherit=mold) for hf in range(2)] for k in range(KC)]
        ysrc = [
            (lambda k, hf: (yat[k][hf], yat[k][hf].ap)),
            (lambda k, hf: (ybt[(hf * 4)], ybv[:, k, hf * 512:(hf + 1) * 512])),
            (lambda k, hf: (yct[k], ycv[:, k, hf * 512:(hf + 1) * 512])),
        ]
        yb_all = ybt
        wouts = [W[("w_out_conv", l)], W[("w_out_sgu", l)], W[("w_out_sb", l)]]
        chunks = []
        for mg in range(8):
            for br in range(3):
                chunks.append(("g", br, mg))
                chunks.append(("o", br, mg))
        dsts = {}

        def issue(i):
            kind, br, mg = chunks[i]
            if kind == "g":
                dsts[i] = (slots[i % 4], load_w(slots[i % 4], wview(win, KC, wc(C_G + br * 2048 + mg * 256), 256), KC, 256))
            else:
                dsts[i] = (slots[i % 4], load_w(slots[i % 4], wview(wouts[br], 8, mg * 256, 256), 8, 256))

        for i in range(3):
            issue(i)
        for i, (kind, br, mg) in enumerate(chunks):
            if i + 3 < len(chunks):
                issue(i + 3)
            slot, wd = dsts[i]
            wdst_tile[0] = slot
            for mb in range(2):
                m = mg * 2 + mb
                for hf in range(2):
                    ps = nb()
                    if kind == "g":
                        mm_fm(ps, wd, KC, mb * 128, lambda k: (ht[k][hf], ht[k][hf].ap))
                        bcol = pvl + PV_BG + br * 16 + m
                        pg.op("act", lambda e, ps=ps, mb=mb, hf=hf, bcol=bcol: e.activation(
                            out=gv[:, mb, hf * 512:(hf + 1) * 512], in_=ps.ap, func=AF.Sigmoid, bias=PV[:, bcol:bcol + 1], scale=1.0),
                            reads=[ps, pvt], writes=[gt])
                    else:
                        if br == 1:
                            for k in range(8):
                                pg.op("pe", lambda e, k=k, ps=ps: e.matmul(ps.ap, lhsT=wd[:, k, mb * 128:(mb + 1) * 128],
                                                                           rhs=ybv[:, k, hf * 512:(hf + 1) * 512], start=(k == 0), stop=(k == 7)),
                                      reads=[slot] + yb_all[hf * 4:(hf + 1) * 4], writes=[ps])
                        else:
                            mm_fm(ps, wd, 8, mb * 128, lambda k: ysrc[br](k, hf))
                        gsl = gv[:, mb, hf * 512:(hf + 1) * 512]
                        msl = mv[:, mb, hf * 512:(hf + 1) * 512]
                        if br == 0:
                            pg.op("dve", lambda e, ps=ps, gsl=gsl, msl=msl: e.tensor_tensor(out=msl, in0=ps.ap, in1=gsl, op=ALU.mult),
                                  reads=[ps, gt], writes=[mt])
                        else:
                            tt = nt()
                            pg.op("dve", lambda e, ps=ps, gsl=gsl, tt=tt: e.tensor_tensor(out=tt.ap, in0=ps.ap, in1=gsl, op=ALU.mult),
                                  reads=[ps, gt], writes=[tt])
                            if br == 1:
                                pg.op("dve", lambda e, msl=msl, tt=tt: e.tensor_tensor(out=msl, in0=msl, in1=tt.ap, op=ALU.add),
                                      reads=[mt, tt], writes=[mt])
                            else:
                                pg.op("dve", lambda e, msl=msl, tt=tt, m=m, hf=hf: e.tensor_tensor(out=mxt[m][hf].ap, in0=msl, in1=tt.ap, op=ALU.add),
                                      reads=[mt, tt], writes=[mxt[m][hf]])
        ydead = arena_toks(alltiles(yat) + ybt + yct + xa_tiles)
        for row in xt:
            for t_ in row:
                t_.r.extend(ydead)
        load_x(x_src)
        wold = arena_toks(slots + [gt, mt])
        slots = [Tile(WA[:, i * 8192:(i + 1) * 8192], inherit=wold) for i in range(3)]
        wo = W[("w_o", l)]
        dsts = {}
        for i in range(2):
            dsts[i] = (slots[i % 3], load_w(slots[i % 3], wview(wo, KC, i * 512, 512), KC, 512))
        for i in range(4):
            if i + 2 < 4:
                dsts[i + 2] = (slots[(i + 2) % 3], load_w(slots[(i + 2) % 3], wview(wo, KC, (i + 2) * 512, 512), KC, 512))
            slot, wd = dsts[i]
            wdst_tile[0] = slot
            for eb in range(4):
                e_ = i * 4 + eb
                for hf in range(2):
                    ps = nb()
                    mm_fm(ps, wd, KC, eb * 128, lambda k: (mxt[k][hf], mxt[k][hf].ap))
                    pg.op("dve", lambda e, ps=ps, e_=e_, hf=hf: e.tensor_tensor(out=xt[e_][hf].ap, in0=ps.ap, in1=xt[e_][hf].ap, op=ALU.add),
                          reads=[ps, xt[e_][hf]], writes=[xt[e_][hf]])
        rmsnorm(pvl + PV_G2, ht)
        w1, w2 = W[("w_ff1", l)], W[("w_ff2", l)]
        mdead = arena_toks(alltiles(mxt))
        fv = [MA[:, i * 4096:(i + 1) * 4096].rearrange("p (k t) -> p k t", k=4) for i in range(2)]
        ft = [[[Tile(fv[i][:, k, hf * 512:(hf + 1) * 512], inherit=mdead) for hf in range(2)] for k in range(4)] for i in range(2)]
        NG = 16
        dsts = {}

        def issue_f(i):
            fg, which = i // 2, i % 2
            if which == 0:
                dsts[i] = (slots[i % 3], load_w(slots[i % 3], wview(w1, KC, fg * 512, 512), KC, 512))
            else:
                src = w2.rearrange("(k p) c -> p k c", p=128)[:, fg * 4:(fg + 1) * 4, :]
                dsts[i] = (slots[i % 3], load_w(slots[i % 3], src, 4, 2048))

        issue_f(0)
        issue_f(1)
        for i in range(2 * NG):
            if i + 2 < 2 * NG:
                issue_f(i + 2)
            fg, which = i // 2, i % 2
            slot, wd = dsts[i]
            wdst_tile[0] = slot
            fcur = ft[fg % 2]
            if which == 0:
                for fb in range(4):
                    for hf in range(2):
                        ps = nb()
                        mm_fm(ps, wd, KC, fb * 128, lambda k: (ht[k][hf], ht[k][hf].ap))
                        rl = nt()
                        pg.op("act", lambda e, ps=ps, rl=rl: e.activation(out=rl.ap, in_=ps.ap, func=AF.Relu), reads=[ps], writes=[rl])
                        pg.op("dve", lambda e, ps=ps, rl=rl, fb=fb, hf=hf: e.tensor_tensor(out=fcur[fb][hf].ap, in0=ps.ap, in1=rl.ap, op=ALU.mult),
                              reads=[ps, rl], writes=[fcur[fb][hf]])
            else:
                for e_ in range(KC):
                    for hf in range(2):
                        ps = nb()
                        mm_fm(ps, wd, 4, e_ * 128, lambda k: (fcur[k][hf], fcur[k][hf].ap))
                        pg.op("dve", lambda e, ps=ps, e_=e_, hf=hf: e.tensor_tensor(out=xt[e_][hf].ap, in0=ps.ap, in1=xt[e_][hf].ap, op=ALU.add),
                              reads=[ps, xt[e_][hf]], writes=[xt[e_][hf]])
        if final:
            rmsnorm(PV_FG, None, dst_f32_out=out_d)
        del ma_last[:]
        ma_last.extend([t_ for a_ in ft for b_ in a_ for t_ in b_])
        return dict(slots=slots)

    def store(dst, src_ap, reads):
        n = src_ap.shape[1]
        q = n // 4
        fns = [lambda e, a=a: e.dma_start(out=dst[:, a * q:(a + 1) * q], in_=src_ap[:, a * q:(a + 1) * q]) for a in range(4)]
        final_toks.append(pg.dma("sp", fns, reads=reads))

    def load(dst_ap, src, writes):
        n = dst_ap.shape[1]
        q = n // 4
        fns = [lambda e, a=a: e.dma_start(out=dst_ap[:, a * q:(a + 1) * q], in_=src[:, a * q:(a + 1) * q]) for a in range(4)]
        pg.dma("sp", fns, writes=writes)

    def allgather(src, dst, src_t):
        dst_t = Tile(None)
        pg.cc(lambda e: e.collective_compute("AllGather", ALU.bypass, replica_groups=[list(range(NCORES))],
                                             ins=[src.opt()], outs=[dst.opt()]), reads=[src_t], writes=[dst_t])
        return dst_t

    load_consts()
    if fused:
        load_x(xT_d)
        inherit = []
        for li, l in enumerate(layers):
            A = phase_A(l, send1, sendh, inherit)
            g1_t = allgather(send1, g1, A["send1_t"])
            gh_t = allgather(sendh, gh, A["sendh_t"])
            Bst = phase_B(l, A, g1, gh, g1_t, gh_t, send2)
            g2_t = allgather(send2, g2, Bst["send2_t"])
            Cst = phase_C(l, Bst, g2, g2_t, ("xsp", A["xsp_t"]), final=(li == len(layers) - 1))
            inherit = arena_toks(Cst["slots"])
    elif mode == "A":
        load_x(xT_d)
        A = phase_A(layer, send1, sendh, [])
        final_toks.extend(A["send1_t"].toks() + A["sendh_t"].toks())
        store(hT_o, HA[:], alltiles(ht))
        store(zT_o, XA[:, 0:8 * ZWP], alltiles(A["zt"]))
    elif mode == "B":
        load(HA[:], hT_i, alltiles(ht))
        zt = [[Tile(zv[:, m, HALO + hf * 512: HALO + (hf + 1) * 512]) for hf in range(2)] for m in range(8)]
        zhalo_t = [Tile(zv[:, m, 0:HALO]) for m in range(8)]
        load(XA[:, 0:8 * ZWP], zT_i, alltiles(zt) + zhalo_t)
        A = dict(zt=zt, zhalo_t=zhalo_t, slots=carve_w(3, 8192, []),
                 qst=[Tile(MA[:, i * 1024:(i + 1) * 1024]) for i in range(2)],
                 vst=[Tile(MA[:, 2048 + i * 4096: 2048 + (i + 1) * 4096]) for i in range(2)])
        g1_t, gh_t = Tile(None), Tile(None)
        Bst = phase_B(layer, A, g1, gh, g1_t, gh_t, send2)
        final_toks.extend(Bst["send2_t"].toks())
        store(ya_o, XAb[:, 17408:25600], alltiles(Bst["yat"]))
        store(yb_o, XAb[:, 0:8192], Bst["ybt"])
    elif mode == "C":
        load(HA[:], hT_i, alltiles(ht))
        yat = [[Tile(yav[:, m, hf * 512:(hf + 1) * 512]) for hf in range(2)] for m in range(8)]
        ybt = [Tile(ybv[:, :, tb * 128:(tb + 1) * 128]) for tb in range(8)]
        load(XAb[:, 17408:25600], ya_i, alltiles(yat))
        load(XAb[:, 0:8192], yb_i, ybt)
        Bst = dict(yat=yat, ybt=ybt, wa_tiles=[], ma_tiles=[], yc_inherit=[])
        g2_t = Tile(None)
        phase_C(layer, Bst, g2, g2_t, xT_d, final=last)
        if not last:
            xo = out_d.rearrange("(k p) t -> p k t", p=128)
            fns = [lambda e, k0=k0: e.dma_start(out=xo[:, k0:k0 + 4, :], in_=xv[:, k0:k0 + 4, :]) for k0 in range(0, KC, 4)]
            final_toks.append(pg.dma("sp", fns, reads=alltiles(xt)))
    pg._wait("sp", final_toks)
    stack.close()
    return nc


def _pack_pv(inp, lmap):
    pv = np.zeros((128, PV_N), np.float32)
    for slot, l in lmap.items():
        o = slot * PV_L
        pv[:, o + PV_G1:o + PV_G1 + 16] = np.asarray(inp["attn_norm_g"][l]).reshape(16, 128).T
        pv[:, o + PV_G2:o + PV_G2 + 16] = np.asarray(inp["mlp_norm_g"][l]).reshape(16, 128).T
        pv[:, o + PV_BG:o + PV_BG + 48] = np.asarray(inp["b_gate"][l]).reshape(48, 128).T
        cw = np.asarray(inp["conv_w"][l])
        pv[:, o + PV_CW:o + PV_CW + 248] = cw.reshape(CW, 8, 128).transpose(2, 1, 0).reshape(128, 248)
        pv[:, o + PV_CB:o + PV_CB + 8] = np.asarray(inp["conv_b"][l]).reshape(8, 128).T
        pv[:, o + PV_LG:o + PV_LG + 8] = np.asarray(inp["conv_ln_g"][l]).reshape(8, 128).T
        pv[:, o + PV_LB:o + PV_LB + 8] = np.asarray(inp["conv_ln_b"][l]).reshape(8, 128).T
    pv[:, PV_FG:PV_FG + 16] = np.asarray(inp["final_norm_g"]).reshape(16, 128).T
    return pv


def _consts():
    i = np.arange(128)
    c = np.zeros((128, 5 * 128), np.float32)
    c[:, 0:128] = 1.0
    c[:, 128:256] = (i[:, None] <= i[None, :])
    c[:, 256:384] = (i[:, None] < i[None, :])
    c[:, 384:512] = -1.0 * (i[:, None] >= i[None, :])
    c[:, 512:640] = -1.0
    return c


def _pack_sgu(inp, l):
    a = np.zeros((128, 4096), np.float32)
    a[:, 0:1024] = np.asarray(inp["sgu_ln_g"][l])[None, :]
    a[:, 1024:2048] = np.asarray(inp["sgu_ln_b"][l])[None, :]
    a[:, 2048:3072] = np.asarray(inp["sgu_b"][l]).reshape(1, 1024)
    a[:, 3072:4096] = np.asarray(inp["sgu_w"][l]).transpose(2, 0, 1).reshape(128, 1024)
    return a


def _core_small(c):
    p = np.arange(128, dtype=np.int32)
    idx = np.stack([c * 128 + p, max(c - 1, 0) * 128 + p], axis=1).astype(np.int32)
    flag = np.full((128, 1), 0.0 if c == 0 else 1.0, np.float32)
    return idx, flag


_CACHE = {}


def _get(mode, last=False):
    key = (mode, last)
    if key not in _CACHE:
        _CACHE[key] = build(mode, 0, last)
    return _CACHE[key]


WNAMES = ["w_in", "w_out_conv", "w_out_sgu", "w_out_sb", "w_o", "w_ff1", "w_ff2"]


def _run(nc, maps):
    res = run_bass_kernel_spmd(nc, maps, core_ids=list(range(NCORES)))
    return res.results


def kernel_unfused(inp, nlayers=2, debug=None):
    x = np.asarray(inp["x"], np.float32)[0]
    xs = [np.ascontiguousarray(x[c * T:(c + 1) * T].T) for c in range(NCORES)]
    cst = _consts()
    small = [_core_small(c) for c in range(NCORES)]
    for l in range(nlayers):
        pv = _pack_pv(inp, {0: l})
        sgu = _pack_sgu(inp, l)
        wl = {nm: np.ascontiguousarray(np.asarray(inp[nm][l], np.float32)) for nm in WNAMES}
        base = [dict(pv=pv, cst=cst, idx=small[c][0], flag=small[c][1]) for c in range(NCORES)]
        w_in = wl["w_in"]
        w_inA = np.ascontiguousarray(np.concatenate([w_in[:, 0:2048], w_in[:, 4096:7168]], axis=1))
        w_inB = np.ascontiguousarray(w_in[:, 2048:4096])
        w_inC = np.ascontiguousarray(w_in[:, C_G:])
        maps = [dict(base[c], xT=xs[c], w_in0=w_inA) for c in range(NCORES)]
        ra = _run(_get("A"), maps)
        g1 = np.concatenate([ra[c]["send1"] for c in range(NCORES)], axis=0)
        gh = np.concatenate([ra[c]["sendh"] for c in range(NCORES)], axis=0)
        if debug is not None:
            debug[f"A{l}"] = ra
        maps = [dict(base[c], w_in0=w_inB, sgu0=sgu, g1=g1, gh=gh, hT_i=ra[c]["hT_o"], zT_i=ra[c]["zT_o"]) for c in range(NCORES)]
        rb = _run(_get("B"), maps)
        g2 = np.concatenate([rb[c]["send2"] for c in range(NCORES)], axis=0)
        if debug is not None:
            debug[f"B{l}"] = rb
        last = (l == nlayers - 1)
        maps = [dict(base[c], xT=xs[c], g2=g2, hT_i=ra[c]["hT_o"], ya_i=rb[c]["ya_o"], yb_i=rb[c]["yb_o"],
                     **{f"{nm}0": (w_inC if nm == "w_in" else wl[nm]) for nm in WNAMES}) for c in range(NCORES)]
        rc = _run(_get("C", last), maps)
        xs = [rc[c]["out"] for c in range(NCORES)]
    out = np.concatenate([xs[c].T for c in range(NCORES)], axis=0)[None]
    return np.ascontiguousarray(out.astype(np.float32))


def kernel_fused(inp):
    x = np.asarray(inp["x"], np.float32)[0]
    cst = _consts()
    pv = _pack_pv(inp, {0: 0, 1: 1})
    shared = dict(pv=pv, cst=cst)
    for l in range(2):
        shared[f"sgu{l}"] = _pack_sgu(inp, l)
        for nm in WNAMES:
            shared[f"{nm}{l}"] = np.ascontiguousarray(np.asarray(inp[nm][l], np.float32))
    maps = []
    for c in range(NCORES):
        idx, flag = _core_small(c)
        maps.append(dict(shared, idx=idx, flag=flag, xT=np.ascontiguousarray(x[c * T:(c + 1) * T].T)))
    res = _run(_get("F"), maps)
    out = np.concatenate([res[c]["out"].T for c in range(NCORES)], axis=0)[None]
    return np.ascontiguousarray(out.astype(np.float32))


def kernel(**inputs):
    if os.environ.get("KFUSED", "1") == "1":
        return kernel_fused(inputs)
    return kernel_unfused(inputs)
```

```python
import os
import numpy as np
import ml_dtypes
import concourse.bass as bass
import concourse.mybir as mybir
from concourse.bass_utils import run_bass_kernel_spmd

F32 = mybir.dt.float32
BF16 = mybir.dt.bfloat16
I32 = mybir.dt.int32
AF = mybir.ActivationFunctionType
ALU = mybir.AluOpType

NCORES = 8
D = 2048
KC = 16
T = 1024
S = 8192
IN_DIM = 13312
DFF = 8192
EPS = 1e-6
CW = 31
HALO = 30
ZW = T + HALO
ZWP = 1056
C_PA, C_PB, C_Q, C_K, C_V, C_G = 0, 2048, 4096, 5120, 6144, 7168

PV_G1, PV_G2, PV_BG, PV_CW, PV_CB, PV_LG, PV_LB = 0, 16, 32, 80, 80 + 248, 80 + 256, 80 + 264
PV_L = 80 + 272
PV_FG = 2 * PV_L
PV_N = PV_FG + 16

X1 = 3 * 8 * 128 * 1024
XH = 128 * 8 * HALO
X2 = 8 * 128 * 1024


class Tile:
    __slots__ = ("ap", "w", "r")

    def __init__(self, ap, inherit=None):
        self.ap = ap
        self.w = None
        self.r = list(inherit) if inherit else []

    def toks(self):
        return ([self.w] if self.w else []) + list(self.r)


class Prog:
    def __init__(self, nc, sems):
        self.nc = nc
        self.eng = {"pe": nc.tensor, "act": nc.scalar, "dve": nc.vector, "pool": nc.gpsimd, "sp": nc.sync}
        self.sem = sems
        self.cnt = {k: 0 for k in sems}
        self.seen = {e: {} for e in self.eng}
        self.rr = {"sp": 0, "pool": 0}
        self.ndma = {"sp": [k for k in sems if k.startswith("dsp")], "pool": [k for k in sems if k.startswith("dpl")]}

    def _wait(self, e, toks):
        need = {}
        for (s, v) in toks:
            if need.get(s, 0) < v:
                need[s] = v
        for s, v in need.items():
            if s == e and e == "pe":
                continue
            if self.seen[e].get(s, 0) >= v:
                continue
            self.eng[e].wait_ge(self.sem[s], v)
            self.seen[e][s] = v

    def _deps(self, reads, writes):
        toks = []
        for t in reads:
            if t.w:
                toks.append(t.w)
        for t in writes:
            toks.extend(t.toks())
        return toks

    def _commit(self, tok, reads, writes):
        for t in reads:
            t.r.append(tok)
        for t in writes:
            t.w = tok
            t.r = []

    def op(self, e, fn, reads=(), writes=()):
        self._wait(e, self._deps(reads, writes))
        ins = fn(self.eng[e])
        self.cnt[e] += 1
        ins.then_inc(self.sem[e], 1)
        tok = (e, self.cnt[e])
        self._commit(tok, reads, writes)
        return tok

    def dma(self, q, fns, reads=(), writes=()):
        names = self.ndma[q]
        s = names[self.rr[q] % len(names)]
        self.rr[q] += 1
        toks = self._deps(reads, writes)
        toks.append((s, self.cnt[s]))
        self._wait(q, toks)
        for fn in fns:
            ins = fn(self.eng[q])
            ins.then_inc(self.sem[s], 16)
            self.cnt[s] += 16
        tok = (s, self.cnt[s])
        self._commit(tok, reads, writes)
        return tok

    def cc(self, fn, reads=(), writes=()):
        self._wait("pool", self._deps(reads, writes))
        ins = fn(self.eng["pool"])
        self.cnt["cc"] += 1
        ins.then_inc(self.sem["cc"], 1)
        tok = ("cc", self.cnt["cc"])
        self._commit(tok, reads, writes)
        return tok

    def finish(self, e, tiles):
        toks = []
        for t in tiles:
            toks.extend(t.toks())
        self._wait(e, toks)


def _ctx_enter(stack, cm):
    return stack.enter_context(cm)


def build(mode, layer=0, last=False):
    import contextlib
    nc = bass.Bass("TRN2", target_bir_lowering=False)
    fused = mode == "F"
    layers = [0, 1] if fused else [layer]
    stack = contextlib.ExitStack()

    def din(name, shape, dt):
        return nc.dram_tensor(name, list(shape), dt, kind="ExternalInput").ap()

    def dout(name, shape, dt):
        return nc.dram_tensor(name, list(shape), dt, kind="ExternalOutput").ap()

    def dint(name, shape, dt):
        return nc.dram_tensor(name, list(shape), dt).ap()

    W = {}
    need_w = {"A": ["w_in"], "B": ["w_in"], "C": ["w_in", "w_out_conv", "w_out_sgu", "w_out_sb", "w_o", "w_ff1", "w_ff2"]}
    wshapes = {"w_in": (D, IN_DIM), "w_out_conv": (1024, D), "w_out_sgu": (1024, D), "w_out_sb": (1024, D),
               "w_o": (D, D), "w_ff1": (D, DFF), "w_ff2": (DFF, D)}
    win_cols = {"A": 5120, "B": 2048, "C": 6144, "F": IN_DIM}[mode]
    wshapes["w_in"] = (D, win_cols)
    wc = {"A": (lambda c: c if c < 2048 else c - 2048), "B": (lambda c: c - 2048), "C": (lambda c: c - C_G), "F": (lambda c: c)}[mode]
    for l in layers:
        for nm in (need_w["C"] if fused else need_w[mode]):
            W[(nm, l)] = din(f"{nm}{l}", wshapes[nm], F32)
    pv_d = din("pv", (128, PV_N), F32)
    cst_d = din("cst", (128, 5 * 128), F32)
    idx_d = din("idx", (128, 2), I32)
    flag_d = din("flag", (128, 1), F32)
    sgu_d = {}
    if fused or mode == "B":
        for l in layers:
            sgu_d[l] = din(f"sgu{l}", (128, 1024 * 4), F32)

    if fused or mode in ("A", "C"):
        xT_d = din("xT", (D, T), F32)
    if fused:
        out_d = dout("out", (D, T), F32)
        send1 = dint("send1", (16, X1 // 16), BF16)
        g1 = dint("g1", (128, X1 // 16), BF16)
        sendh = dint("sendh", (16, XH // 16), F32)
        gh = dint("gh", (128, XH // 16), F32)
        send2 = dint("send2", (16, X2 // 16), BF16)
        g2 = dint("g2", (128, X2 // 16), BF16)
        xsp = dint("xsp", (128, KC * T), F32)
    else:
        if mode == "A":
            send1 = dout("send1", (16, X1 // 16), BF16)
            sendh = dout("sendh", (16, XH // 16), F32)
            hT_o = dout("hT_o", (128, KC * T), BF16)
            zT_o = dout("zT_o", (128, 8 * ZWP), F32)
        if mode == "B":
            g1 = din("g1", (128, X1 // 16), BF16)
            gh = din("gh", (128, XH // 16), F32)
            hT_i = din("hT_i", (128, KC * T), BF16)
            zT_i = din("zT_i", (128, 8 * ZWP), F32)
            send2 = dout("send2", (16, X2 // 16), BF16)
            if os.environ.get("DBGQ") or os.environ.get("DBGT"):
                dbg_q = dout("dbg_q", (128, 8192), BF16)
                dbg_k = dout("dbg_k", (128, 8192), BF16)
                dbg_v = dout("dbg_v", (128, 8192), BF16)
                dbg_t = dout("dbg_t", (128, 3 * 4 * 512), F32)
            ya_o = dout("ya_o", (128, 8 * T), BF16)
            yb_o = dout("yb_o", (128, 8 * T), BF16)
        if mode == "C":
            g2 = din("g2", (128, X2 // 16), BF16)
            hT_i = din("hT_i", (128, KC * T), BF16)
            ya_i = din("ya_i", (128, 8 * T), BF16)
            yb_i = din("yb_i", (128, 8 * T), BF16)
            out_d = dout("out", (D, T), F32)

    def sb(name, shape, dt):
        return _ctx_enter(stack, nc.sbuf_tensor(name, list(shape), dt))

    XA = sb("XA", (128, KC * T), F32)
    HA = sb("HA", (128, KC * T), BF16)
    WA = sb("WA", (128, 3 * 8192), BF16)
    MA = sb("MA", (128, 16384), BF16)
    TM = sb("TM", (128, 6 * 512), F32)
    PV = sb("PV", (128, PV_N), F32)
    CF = sb("CF", (128, 5 * 128), F32)
    CB = sb("CB", (128, 5 * 128), BF16)
    IDX = sb("IDX", (128, 2), I32)
    FLG = sb("FLG", (128, 1), F32)
    SM = sb("SM", (128, 64), F32)
    HL = sb("HL", (128, 8 * HALO), F32)
    PS = [_ctx_enter(stack, nc.psum_tensor(f"PS{i}", [128, 1024], F32)) for i in range(4)]

    sem_names = ["pe", "act", "dve", "pool", "sp", "cc"] + [f"dsp{i}" for i in range(12)] + [f"dpl{i}" for i in range(12)]
    sems = {n: _ctx_enter(stack, nc.semaphore(n)) for n in sem_names}
    pg = Prog(nc, sems)

    xv = XA[:].rearrange("p (k t) -> p k t", k=KC)
    hv = HA[:].rearrange("p (k t) -> p k t", k=KC)
    xt = [[Tile(xv[:, k, hf * 512:(hf + 1) * 512]) for hf in range(2)] for k in range(KC)]
    ht = [[Tile(hv[:, k, hf * 512:(hf + 1) * 512]) for hf in range(2)] for k in range(KC)]
    bank = [Tile(PS[i // 2][:, (i % 2) * 512:(i % 2 + 1) * 512]) for i in range(8)]
    tmp = [Tile(TM[:, i * 512:(i + 1) * 512]) for i in range(6)]
    pvt = Tile(PV[:])
    cft = Tile(CF[:])
    cbt = Tile(CB[:])
    idxt = Tile(IDX[:])
    flgt = Tile(FLG[:])
    smt = Tile(SM[:])
    hlt = Tile(HL[:])
    onesF = CF[:, 0:128]
    maskLE = CF[:, 128:256]
    maskLT_b = CB[:, 256:384]
    negtri_b = CB[:, 384:512]
    negones_b = CB[:, 512:640]
    zeros_b = CB[:, 0:128]

    state = {"bank": 0, "tmp": 0}

    def nb():
        b = bank[state["bank"] % 8]
        state["bank"] += 1
        return b

    def nt():
        t = tmp[state["tmp"] % 3]
        state["tmp"] += 1
        return t

    LT = [tmp[3], tmp[4], tmp[5]]

    def alltiles(tt):
        return [t for row in tt for t in row]

    XAb = XA[:].bitcast(BF16)
    zv = XA[:, 0:8 * ZWP].rearrange("p (m t) -> p m t", m=8)
    ybv = XAb[:, 0:8192].rearrange("p (m t) -> p m t", m=8)
    ycv = XAb[:, 8192:16384].rearrange("p (m t) -> p m t", m=8)
    yav = XAb[:, 17408:25600].rearrange("p (m t) -> p m t", m=8)
    XFREE = 12800
    accv = [XA[:, XFREE + i * 1024: XFREE + (i + 1) * 1024] for i in range(2)]
    sguc = XA[:, XFREE: XFREE + 3584]

    def load_consts():
        pg.dma("sp", [lambda e: e.dma_start(out=PV[:], in_=pv_d[:, :])], writes=[pvt])
        pg.dma("sp", [lambda e: e.dma_start(out=CF[:], in_=cst_d[:, :])], writes=[cft])
        pg.dma("sp", [lambda e: e.dma_start(out=IDX[:], in_=idx_d[:, :])], writes=[idxt])
        pg.dma("sp", [lambda e: e.dma_start(out=FLG[:], in_=flag_d[:, :])], writes=[flgt])
        pg.op("dve", lambda e: e.tensor_copy(out=CB[:], in_=CF[:]), reads=[cft], writes=[cbt])
        pg.op("dve", lambda e: e.memset(CB[:, 0:128], 0.0), writes=[cbt])

    def wview(wap, rows_kc, c0, ncols):
        return wap.rearrange("(k p) c -> p k c", p=128)[:, 0:rows_kc, c0:c0 + ncols]

    wslots = {}

    def carve_w(n, size, inherit):
        sl = []
        for i in range(n):
            sl.append(Tile(WA[:, i * size:(i + 1) * size], inherit=inherit))
        return sl

    def arena_toks(tiles):
        toks = []
        for t in tiles:
            toks.extend(t.toks())
        best = {}
        for s, v in toks:
            if best.get(s, 0) < v:
                best[s] = v
        return list(best.items())

    def load_w(slot, src, kcs, ncols):
        dst = slot.ap[:, 0:kcs * ncols].rearrange("p (k c) -> p k c", k=kcs)
        step = max(1, 512 // 128 * 1)
        step = 4
        fns = []
        for k0 in range(0, kcs, step):
            k1 = min(kcs, k0 + step)
            fns.append(lambda e, k0=k0, k1=k1: e.dma_start(out=dst[:, k0:k1, :], in_=src[:, k0:k1, :]))
        pg.dma("pool", fns, writes=[slot])
        return dst

    def rmsnorm(gcol0, dst_tiles, dst_f32_out=None):
        for hf in range(2):
            ps = nb()
            for k in range(KC):
                sq = nt()
                pg.op("act", lambda e, k=k, sq=sq: e.activation(out=sq.ap, in_=xt[k][hf].ap, func=AF.Square),
                      reads=[xt[k][hf]], writes=[sq])
                pg.op("pe", lambda e, k=k, sq=sq: e.matmul(ps.ap, lhsT=onesF, rhs=sq.ap, start=(k == 0), stop=(k == KC - 1)),
                      reads=[sq, cft], writes=[ps])
            rs = LT[0]
            pg.op("act", lambda e: e.activation(out=rs.ap, in_=ps.ap, func=AF.Sqrt, bias=EPS, scale=1.0 / D),
                  reads=[ps], writes=[rs])
            pg.op("dve", lambda e: e.reciprocal(out=rs.ap, in_=rs.ap), reads=[rs], writes=[rs])
            for k in range(KC):
                if dst_f32_out is None:
                    pg.op("dve", lambda e, k=k: e.scalar_tensor_tensor(
                        out=dst_tiles[k][hf].ap, in0=xt[k][hf].ap, scalar=PV[:, gcol0 + k:gcol0 + k + 1],
                        in1=rs.ap, op0=ALU.mult, op1=ALU.mult),
                        reads=[xt[k][hf], rs, pvt], writes=[dst_tiles[k][hf]])
                else:
                    o = nt()
                    pg.op("dve", lambda e, k=k, o=o: e.scalar_tensor_tensor(
                        out=o.ap, in0=xt[k][hf].ap, scalar=PV[:, gcol0 + k:gcol0 + k + 1],
                        in1=rs.ap, op0=ALU.mult, op1=ALU.mult),
                        reads=[xt[k][hf], rs, pvt], writes=[o])
                    final_toks.append(pg.dma("sp", [lambda e, k=k, o=o: e.dma_start(
                        out=dst_f32_out[k * 128:(k + 1) * 128, hf * 512:(hf + 1) * 512], in_=o.ap)],
                        reads=[o]))

    def load_x(src):
        if isinstance(src, tuple):
            fns = [lambda e, k0=k0: e.dma_start(out=XA[:, k0 * T:(k0 + 4) * T], in_=xsp[:, k0 * T:(k0 + 4) * T]) for k0 in range(0, KC, 4)]
            pg.dma("sp", fns, reads=[src[1]], writes=alltiles(xt))
            return
        v = src.rearrange("(k p) t -> p k t", p=128)
        fns = [lambda e, k0=k0: e.dma_start(out=xv[:, k0:k0 + 4, :], in_=v[:, k0:k0 + 4, :]) for k0 in range(0, KC, 4)]
        pg.dma("sp", fns, writes=alltiles(xt))

    xa_tiles = []
    final_toks = []
    ma_last = []

    def nb2():
        if state["bank"] % 2:
            state["bank"] += 1
        b0 = bank[state["bank"] % 8]
        b1 = bank[(state["bank"] + 1) % 8]
        state["bank"] += 2
        return b0, b1

    def mm_fm(ps, wdst, kcs, col0, rhs_tiles_fn):
        for k in range(kcs):
            rt, rap = rhs_tiles_fn(k)
            pg.op("pe", lambda e, k=k, rap=rap: e.matmul(ps.ap, lhsT=wdst[:, k, col0:col0 + 128], rhs=rap,
                                                          start=(k == 0), stop=(k == kcs - 1)),
                  reads=[rt, wdst_tile[0]], writes=[ps])

    wdst_tile = [None]


    def conv_taps(l, zt, zhalo_t, gh_ap, gh_t, xinh):
        pvl = l * PV_L
        ghv = gh_ap.rearrange("a b -> (a b)").rearrange("(n c) -> n c", c=8 * HALO)
        pg.dma("pool", [lambda e: e.indirect_dma_start(
            out=HL[:], out_offset=None, in_=ghv, in_offset=bass.IndirectOffsetOnAxis(ap=IDX[:, 1:2], axis=0))],
            reads=[gh_t, idxt], writes=[hlt])
        hlv = HL[:].rearrange("p (m t) -> p m t", m=8)
        pg.op("dve", lambda e: e.tensor_scalar(out=zv[:, :, 0:HALO], in0=hlv, scalar1=FLG[:, 0:1], scalar2=None, op0=ALU.mult),
              reads=[hlt, flgt], writes=zhalo_t)
        acct = [Tile(accv[i], inherit=xinh) for i in range(2)]
        xa_tiles.extend(acct)
        cw0 = pvl + PV_CW
        for m in range(8):
            acc = acct[m % 2]
            zall = [zt[m][0], zt[m][1], zhalo_t[m]]
            pg.op("dve", lambda e, m=m, acc=acc: e.tensor_scalar(
                out=acc.ap, in0=zv[:, m, 0:T], scalar1=PV[:, cw0 + m * CW: cw0 + m * CW + 1],
                scalar2=PV[:, pvl + PV_CB + m: pvl + PV_CB + m + 1], op0=ALU.mult, op1=ALU.add),
                reads=zall + [pvt], writes=[acc])
            for k in range(1, CW):
                pg.op("dve", lambda e, m=m, acc=acc, k=k: e.scalar_tensor_tensor(
                    out=acc.ap, in0=zv[:, m, k:k + T], scalar=PV[:, cw0 + m * CW + k: cw0 + m * CW + k + 1],
                    in1=acc.ap, op0=ALU.mult, op1=ALU.add), reads=zall + [acc, pvt], writes=[acc])
            for hf in range(2):
                pg.op("act", lambda e, acc=acc, hf=hf, m=m: e.activation(out=zt[m][hf].ap, in_=acc.ap[:, hf * 512:(hf + 1) * 512], func=AF.Copy),
                      reads=[acc], writes=[zt[m][hf]])
        return acct

    def conv_norm(l, zt, yat):
        pvl = l * PV_L
        s1 = [nb(), nb()]
        s2 = [nb(), nb()]
        for m in range(8):
            for hf in range(2):
                sq = nt()
                pg.op("act", lambda e, sq=sq, hf=hf, m=m: e.activation(out=sq.ap, in_=zt[m][hf].ap, func=AF.Square),
                      reads=[zt[m][hf]], writes=[sq])
                pg.op("pe", lambda e, hf=hf, m=m: e.matmul(s1[hf].ap, lhsT=onesF, rhs=zt[m][hf].ap,
                                                           start=(m == 0), stop=(m == 7)), reads=[zt[m][hf], cft], writes=[s1[hf]])
                pg.op("pe", lambda e, sq=sq, hf=hf, m=m: e.matmul(s2[hf].ap, lhsT=onesF, rhs=sq.ap,
                                                                  start=(m == 0), stop=(m == 7)), reads=[sq, cft], writes=[s2[hf]])
        for hf in range(2):
            mean, var, rstd = LT[0], LT[1], LT[2]
            pg.op("act", lambda e: e.activation(out=mean.ap, in_=s1[hf].ap, func=AF.Copy, scale=1.0 / 1024), reads=[s1[hf]], writes=[mean])
            pg.op("dve", lambda e: e.tensor_tensor(out=var.ap, in0=mean.ap, in1=mean.ap, op=ALU.mult), reads=[mean], writes=[var])
            pg.op("dve", lambda e: e.scalar_tensor_tensor(out=var.ap, in0=s2[hf].ap, scalar=1.0 / 1024, in1=var.ap,
                                                          op0=ALU.mult, op1=ALU.subtract), reads=[s2[hf], var], writes=[var])
            pg.op("act", lambda e: e.activation(out=rstd.ap, in_=var.ap, func=AF.Sqrt, bias=EPS, scale=1.0), reads=[var], writes=[rstd])
            pg.op("dve", lambda e: e.reciprocal(out=rstd.ap, in_=rstd.ap), reads=[rstd], writes=[rstd])
            for m in range(8):
                t1 = nt()
                pg.op("dve", lambda e, m=m, t1=t1: e.tensor_tensor(out=t1.ap, in0=zt[m][hf].ap, in1=mean.ap, op=ALU.subtract),
                      reads=[zt[m][hf], mean], writes=[t1])
                pg.op("dve", lambda e, t1=t1: e.tensor_tensor(out=t1.ap, in0=t1.ap, in1=rstd.ap, op=ALU.mult),
                      reads=[t1, rstd], writes=[t1])
                pg.op("act", lambda e, m=m, t1=t1: e.activation(
                    out=yat[m][hf].ap, in_=t1.ap, func=AF.Silu,
                    scale=PV[:, pvl + PV_LG + m: pvl + PV_LG + m + 1], bias=PV[:, pvl + PV_LB + m: pvl + PV_LB + m + 1]),
                    reads=[t1, pvt], writes=[yat[m][hf]])

    def phase_A(l, send1_ap, sendh_ap, carve_inherit, after_halo=None):
        win = W[("w_in", l)]
        rmsnorm(l * PV_L + PV_G1, ht)
        if fused:
            fns = [lambda e, k0=k0: e.dma_start(out=xsp[:, k0 * T:(k0 + 4) * T], in_=XA[:, k0 * T:(k0 + 4) * T]) for k0 in range(0, KC, 4)]
            xsp_t = Tile(None)
            pg.dma("sp", fns, reads=alltiles(xt), writes=[xsp_t])
        else:
            xsp_t = None
        slots = carve_w(3, 8192, carve_inherit)
        s1f = send1_ap.rearrange("a b -> (a b)")
        sec = [s1f[i * 1048576:(i + 1) * 1048576] for i in range(3)]
        qsec = [sec[i].rearrange("(j p t) -> j p t", j=8, p=128) for i in range(2)]
        vsec = sec[2].rearrange("(j p b f) -> j p b f", j=8, p=128, b=8)
        send1_t = Tile(None)
        minh = arena_toks(ma_last)
        qst = [Tile(MA[:, i * 1024:(i + 1) * 1024], inherit=minh) for i in range(2)]
        vst = [Tile(MA[:, 2048 + i * 4096: 2048 + (i + 1) * 4096], inherit=minh) for i in range(2)]
        chunks = [("pv", C_PA, 0), ("pg", C_PA + 1024, 0), ("pv", C_PA + 512, 1), ("pg", C_PA + 1536, 1),
                  ("q", C_Q, 0), ("q", C_Q + 512, 1), ("k", C_K, 0), ("k", C_K + 512, 1), ("v", C_V, 0), ("v", C_V + 512, 1)]
        shv = sendh_ap.rearrange("a b -> (a b)").rearrange("(p m t) -> p m t", p=128, m=8)
        sendh_t = Tile(None)
        ret = {}
        xinh = arena_toks(alltiles(xt))
        zt = [[Tile(zv[:, m, HALO + hf * 512: HALO + (hf + 1) * 512], inherit=xinh) for hf in range(2)] for m in range(8)]
        zhalo_t = [Tile(zv[:, m, 0:HALO], inherit=xinh) for m in range(8)]
        xa_tiles.extend(alltiles(zt) + zhalo_t)
        dsts = {}

        def issue(i):
            kind, c0, ix = chunks[i]
            dsts[i] = (slots[i % 3], load_w(slots[i % 3], wview(win, KC, wc(c0), 512), KC, 512))

        issue(0)
        issue(1)
        qn = 0
        for i, (kind, c0, ix) in enumerate(chunks):
            if i + 2 < len(chunks) and kind != "pg":
                issue(i + 2)
            slot, wd = dsts[i]
            wdst_tile[0] = slot
            if kind in ("q", "k"):
                for jb in range(4):
                    j = ix * 4 + jb
                    st = qst[qn % 2]
                    qn += 1
                    for hf in range(2):
                        ps = nb()
                        mm_fm(ps, wd, KC, jb * 128, lambda k: (ht[k][hf], ht[k][hf].ap))
                        pg.op("act", lambda e, ps=ps, st=st, hf=hf: e.activation(
                            out=st.ap[:, hf * 512:(hf + 1) * 512], in_=ps.ap, func=AF.Copy,
                            scale=(0.125 if kind == "q" else 1.0)), reads=[ps], writes=[st])
                    sidx = 0 if kind == "q" else 1
                    pg.dma("sp", [lambda e, st=st, j=j, sidx=sidx: e.dma_start(out=qsec[sidx][j], in_=st.ap)],
                           reads=[st], writes=[send1_t])
            elif kind == "v":
                st = vst[ix % 2]
                stv = st.ap.rearrange("p (b f) -> p b f", b=8)
                for tb in range(8):
                    ps = nb()
                    hf = tb // 4
                    for k in range(KC):
                        pg.op("pe", lambda e, k=k, ps=ps, tb=tb: e.matmul(
                            ps.ap, lhsT=hv[:, k, tb * 128:(tb + 1) * 128], rhs=wd[:, k, :],
                            start=(k == 0), stop=(k == KC - 1)), reads=[ht[k][hf], slot], writes=[ps])
                    pg.op("act", lambda e, ps=ps, tb=tb: e.activation(out=stv[:, tb, :], in_=ps.ap, func=AF.Copy), reads=[ps], writes=[st])
                for jj in range(4):
                    j = ix * 4 + jj
                    pg.dma("sp", [lambda e, j=j, jj=jj: e.dma_start(out=vsec[j], in_=stv[:, :, jj * 128:(jj + 1) * 128])],
                           reads=[st], writes=[send1_t])
            elif kind == "pv":
                pass
            elif kind == "pg":
                slot_v, wd_v = dsts[i - 1]
                for mb in range(4):
                    m = ix * 4 + mb
                    for hf in range(2):
                        psv = nb()
                        psg = nb()
                        wdst_tile[0] = slot_v
                        mm_fm(psv, wd_v, KC, mb * 128, lambda k: (ht[k][hf], ht[k][hf].ap))
                        wdst_tile[0] = slot
                        mm_fm(psg, wd, KC, mb * 128, lambda k: (ht[k][hf], ht[k][hf].ap))
                        sg = nt()
                        pg.op("act", lambda e, psg=psg, sg=sg: e.activation(out=sg.ap, in_=psg.ap, func=AF.Sigmoid),
                              reads=[psg], writes=[sg])
                        pg.op("dve", lambda e, psv=psv, sg=sg, m=m, hf=hf: e.tensor_tensor(
                            out=zt[m][hf].ap, in0=psv.ap, in1=sg.ap, op=ALU.mult),
                            reads=[psv, sg], writes=[zt[m][hf]])
                if i + 2 < len(chunks):
                    issue(i + 2)
                if ix == 1:
                    pg.dma("sp", [lambda e: e.dma_start(out=shv, in_=zv[:, :, T:T + HALO])], reads=[zt[m][1] for m in range(8)], writes=[sendh_t])
                    if after_halo is not None:
                        ret["gh_t"] = after_halo(sendh_t)
                        ret["acct"] = conv_taps(l, zt, zhalo_t, gh, ret["gh_t"], xinh)
        if after_halo is not None:
            yat_ = [[Tile(yav[:, m, hf * 512:(hf + 1) * 512], inherit=xinh) for hf in range(2)] for m in range(8)]
            xa_tiles.extend(alltiles(yat_))
            conv_norm(l, zt, yat_)
            ret["yat"] = yat_
        return dict(gh_t=ret.get("gh_t"), yat=ret.get("yat"), acct=ret.get("acct"), zt=zt, zhalo_t=zhalo_t, send1_t=send1_t, sendh_t=sendh_t, slots=slots, xsp_t=xsp_t, qst=qst, vst=vst)

    def phase_B(l, A, g1_ap, gh_ap, g1_t, gh_t, send2_ap):
        win = W[("w_in", l)]
        zt, zhalo_t = A["zt"], A["zhalo_t"]
        pvl = l * PV_L
        xinh = arena_toks(alltiles(xt))
        yat = [[Tile(yav[:, m, hf * 512:(hf + 1) * 512], inherit=xinh) for hf in range(2)] for m in range(8)]
        xa_tiles.extend(alltiles(yat))
        if A.get("yat") is not None:
            yat = A["yat"]
            acct = A["acct"]
        else:
            acct = conv_taps(l, zt, zhalo_t, gh_ap, gh_t, xinh)
            conv_norm(l, zt, yat)
        zdead = [t for row in zt for t in row] + zhalo_t
        ybt = [Tile(ybv[:, :, tb * 128:(tb + 1) * 128], inherit=arena_toks(zdead)) for tb in range(8)]
        sgc_t = Tile(sguc, inherit=arena_toks(acct))
        xa_tiles.extend(ybt + [sgc_t])
        pg.dma("sp", [lambda e: e.dma_start(out=sguc[:, 0:3072], in_=sgu_d[l][:, 0:3072])], writes=[sgc_t])
        lng_bc = sguc[:, 0:1024]
        lnb_bc = sguc[:, 1024:2048]
        bs_bc = sguc[:, 2048:3072].rearrange("p (g t) -> p g t", g=8)
        wTm = sguc[:, 3072:3584].bitcast(BF16).rearrange("p (g t) -> p g t", g=8)
        wtmp = [LT[0], LT[1]]
        for i in range(2):
            pg.dma("sp", [lambda e, i=i: e.dma_start(out=wtmp[i].ap, in_=sgu_d[l][:, 3072 + i * 512: 3072 + (i + 1) * 512])], writes=[wtmp[i]])
            for gg in range(4):
                g = i * 4 + gg
                pg.op("dve", lambda e, i=i, gg=gg, g=g: e.tensor_tensor(out=wTm[:, g, :], in0=wtmp[i].ap[:, gg * 128:(gg + 1) * 128],
                                                                        in1=maskLE, op=ALU.mult), reads=[wtmp[i], cft], writes=[sgc_t])
        slots = A["slots"]
        uT = MA[:, 0:8192].rearrange("p (m t) -> p m t", m=8)
        stg = A["qst"] + A["vst"]
        ut = [[Tile(uT[:, m, hf * 512:(hf + 1) * 512], inherit=arena_toks(stg)) for hf in range(2)] for m in range(8)]
        vn_t = [Tile(MA[:, 8192 + i * 1024: 8192 + (i + 1) * 1024], inherit=arena_toks(stg)) for i in range(2)]
        vg_t = [Tile(MA[:, 10240 + i * 2048: 10240 + (i + 1) * 2048], inherit=arena_toks(stg)) for i in range(2)]
        chunks = [C_PB, C_PB + 512, C_PB + 1024, C_PB + 1536]
        dsts = {}
        for i in range(3):
            dsts[i] = (slots[i % 3], load_w(slots[i % 3], wview(win, KC, wc(chunks[i]), 512), KC, 512))
        for i in range(2):
            slot, wd = dsts[i]
            wdst_tile[0] = slot
            for mb in range(4):
                m = i * 4 + mb
                for hf in range(2):
                    ps = nb()
                    mm_fm(ps, wd, KC, mb * 128, lambda k: (ht[k][hf], ht[k][hf].ap))
                    pg.op("act", lambda e, ps=ps, m=m, hf=hf: e.activation(out=ut[m][hf].ap, in_=ps.ap, func=AF.Gelu_apprx_tanh),
                          reads=[ps], writes=[ut[m][hf]])
        dsts[3] = (slots[0], load_w(slots[0], wview(win, KC, wc(chunks[3]), 512), KC, 512))
        for tb in range(8):
            hf = tb // 4
            vg = vg_t[tb % 2]
            vgf = vg.ap.bitcast(F32)
            vn = vn_t[tb % 2]
            for i in range(2):
                slot, wd = dsts[2 + i]
                ps = nb()
                for k in range(KC):
                    pg.op("pe", lambda e, k=k, ps=ps, wd=wd: e.matmul(ps.ap, lhsT=hv[:, k, tb * 128:(tb + 1) * 128], rhs=wd[:, k, :],
                                                                      start=(k == 0), stop=(k == KC - 1)), reads=[ht[k][hf], slot], writes=[ps])
                pg.op("act", lambda e, ps=ps, i=i: e.activation(out=vgf[:, i * 512:(i + 1) * 512], in_=ps.ap, func=AF.Gelu_apprx_tanh,
                                                              accum_out=SM[:, i:i + 1]), reads=[ps], writes=[vg, smt])
                junk = nt()
                pg.op("act", lambda e, i=i, junk=junk: e.activation(out=junk.ap, in_=vgf[:, i * 512:(i + 1) * 512], func=AF.Square,
                                                                    accum_out=SM[:, 2 + i:3 + i]), reads=[vg], writes=[junk, smt])
            pg.op("dve", lambda e: e.tensor_tensor(out=SM[:, 4:5], in0=SM[:, 0:1], in1=SM[:, 1:2], op=ALU.add), reads=[smt], writes=[smt])
            pg.op("dve", lambda e: e.tensor_tensor(out=SM[:, 5:6], in0=SM[:, 2:3], in1=SM[:, 3:4], op=ALU.add), reads=[smt], writes=[smt])
            pg.op("dve", lambda e: e.tensor_scalar(out=SM[:, 4:6], in0=SM[:, 4:6], scalar1=1.0 / 1024, scalar2=None, op0=ALU.mult), reads=[smt], writes=[smt])
            pg.op("dve", lambda e: e.tensor_tensor(out=SM[:, 6:7], in0=SM[:, 4:5], in1=SM[:, 4:5], op=ALU.mult), reads=[smt], writes=[smt])
            pg.op("dve", lambda e: e.tensor_tensor(out=SM[:, 6:7], in0=SM[:, 5:6], in1=SM[:, 6:7], op=ALU.subtract), reads=[smt], writes=[smt])
            pg.op("act", lambda e: e.activation(out=SM[:, 7:8], in_=SM[:, 6:7], func=AF.Sqrt, bias=EPS, scale=1.0), reads=[smt], writes=[smt])
            pg.op("dve", lambda e: e.reciprocal(out=SM[:, 7:8], in_=SM[:, 7:8]), reads=[smt], writes=[smt])
            pg.op("dve", lambda e: e.tensor_scalar(out=vgf, in0=vgf, scalar1=SM[:, 4:5], scalar2=SM[:, 7:8], op0=ALU.subtract, op1=ALU.mult),
                  reads=[vg, smt], writes=[vg])
            pg.op("dve", lambda e: e.tensor_tensor(out=vgf, in0=vgf, in1=lng_bc, op=ALU.mult), reads=[vg, sgc_t], writes=[vg])
            pg.op("dve", lambda e: e.tensor_tensor(out=vn.ap, in0=vgf, in1=lnb_bc, op=ALU.add), reads=[vg, sgc_t], writes=[vn])
            b0, b1 = nb2()
            psq = PS[bank.index(b0) // 2][:, :].rearrange("p (g t) -> p g t", g=8)
            for g in range(8):
                bt = b0 if g < 4 else b1
                pg.op("pe", lambda e, g=g: e.matmul(psq[:, g, :], lhsT=vn.ap[:, g * 128:(g + 1) * 128], rhs=wTm[:, g, :], start=True, stop=True),
                      reads=[vn, sgc_t], writes=[bt])
            vg3 = vgf.rearrange("p (g t) -> p g t", g=8)
            pg.op("dve", lambda e: e.tensor_tensor(out=vg3, in0=psq, in1=bs_bc, op=ALU.add), reads=[b0, b1, sgc_t], writes=[vg])
            pg.op("dve", lambda e, tb=tb: e.tensor_tensor(out=ybt[tb].ap, in0=vg3, in1=uT[:, :, tb * 128:(tb + 1) * 128], op=ALU.mult),
                  reads=[vg] + [ut[m][hf] for m in range(8)], writes=[ybt[tb]])
        wold = arena_toks(slots)
        qT = WA[:, 0:8192]
        kT = WA[:, 8192:16384]
        Vv = WA[:, 16384:24576].rearrange("p (b f) -> p b f", b=64)
        qt_ = [Tile(qT[:, g * 512:(g + 1) * 512], inherit=wold) for g in range(16)]
        kt_ = [Tile(kT[:, r * 1024:(r + 1) * 1024], inherit=wold) for r in range(8)]
        vt_ = [Tile(Vv[:, r * 8:(r + 1) * 8, :], inherit=wold) for r in range(8)]
        g1f = g1_ap.rearrange("a b -> (a b)")
        off = bass.IndirectOffsetOnAxis(ap=IDX[:, 0:1], axis=0)
        qsrc0 = g1f[0:1048576].rearrange("(n t) -> n t", t=1024)
        vsrc0 = g1f[0:1048576].rearrange("(n b f) -> n b f", b=8, f=128)
        for r in range(8):
            base = r * X1
            pg.dma("pool", [lambda e, r=r: e.indirect_dma_start(out=qT[:, r * 1024:(r + 1) * 1024], out_offset=None, in_=qsrc0, in_offset=off,
                                                                 element_offset=base)],
                   reads=[g1_t, idxt], writes=[qt_[2 * r], qt_[2 * r + 1]])
            pg.dma("pool", [lambda e, r=r: e.indirect_dma_start(out=kT[:, r * 1024:(r + 1) * 1024], out_offset=None, in_=qsrc0, in_offset=off,
                                                                 element_offset=base + 1048576)],
                   reads=[g1_t, idxt], writes=[kt_[r]])
            pg.dma("pool", [lambda e, r=r: e.indirect_dma_start(out=WA[:, 16384 + r * 1024: 16384 + (r + 1) * 1024], out_offset=None, in_=qsrc0, in_offset=off,
                                                                 element_offset=base + 2 * 1048576)],
                   reads=[g1_t, idxt], writes=[vt_[r]])
        if not fused and os.environ.get('DBGQ'):
            final_toks.append(pg.dma("sp", [lambda e: e.dma_start(out=dbg_q[:, :], in_=qT)], reads=qt_))
            final_toks.append(pg.dma("sp", [lambda e: e.dma_start(out=dbg_k[:, :], in_=kT)], reads=kt_))
            final_toks.append(pg.dma("sp", [lambda e: e.dma_start(out=dbg_v[:, :], in_=WA[:, 16384:24576])], reads=vt_))
        mold = arena_toks(alltiles(ut) + vn_t + vg_t)
        SP2 = [Tile(MA[:, i * 1024:(i + 1) * 1024], inherit=mold) for i in range(3)]
        A2 = [Tile(MA[:, 3072 + i * 1024: 3072 + (i + 1) * 1024], inherit=mold) for i in range(3)]
        R2 = [Tile(MA[:, 6144 + i * 1024: 6144 + (i + 1) * 1024], inherit=mold) for i in range(2)]
        v3 = lambda t_: t_.ap.rearrange("p (h c) -> p h c", h=2)
        E2t = [[tmp[2 * i], tmp[2 * i + 1]] for i in range(3)]
        E2v = [TM[:, i * 1024:(i + 1) * 1024].rearrange("p (h c) -> p h c", h=2) for i in range(3)]
        Z2t = [[bank[2 * i], bank[2 * i + 1]] for i in range(3)]
        Z2v = [PS[i][:, :].rearrange("p (h c) -> p h c", h=2) for i in range(3)]
        psO = [bank[6], bank[7]]
        tiles = []
        for qg in range(16):
            for kb in range(4 * qg + 3, -1, -1):
                c0 = max(0, kb - 4 * qg) * 128
                tiles.append(dict(qg=qg, kb=kb, c0=c0, n=512 - c0, first=(kb == 4 * qg + 3), last=(kb == 0), diag=(kb >= 4 * qg)))
        NT = len(tiles)

        def S0(i):
            t = tiles[i]
            qg, kb, c0, n = t["qg"], t["kb"], t["c0"], t["n"]
            if t["first"]:
                po = psO[qg % 2]
                pg.op("pe", lambda e: e.matmul(po.ap, lhsT=zeros_b, rhs=qT[:, qg * 512:(qg + 1) * 512], start=True, stop=False),
                      reads=[cbt, qt_[qg]], writes=[po])
                pg.op("dve", lambda e: e.memset(R2[qg % 2].ap, 0.0), writes=[R2[qg % 2]])
            zt_, zv_ = Z2t[i % 3], Z2v[i % 3]
            for hh in range(2):
                pg.op("pe", lambda e, hh=hh: e.matmul(zv_[:, hh, 0:n], lhsT=kT[hh * 64:(hh + 1) * 64, kb * 128:(kb + 1) * 128],
                                                       rhs=qT[hh * 64:(hh + 1) * 64, qg * 512 + c0:(qg + 1) * 512], start=True, stop=False),
                      reads=[kt_[kb // 8], qt_[qg]], writes=[zt_[hh]])

        def S1(i):
            t = tiles[i]
            n = t["n"]
            zt_, zv_ = Z2t[i % 3], Z2v[i % 3]
            et_, ev_ = E2t[i % 3], E2v[i % 3]
            SP = SP2[i % 3]
            pg.op("act", lambda e: e.activation(out=ev_[:, :, 0:n], in_=zv_[:, :, 0:n], func=AF.Exp), reads=zt_, writes=et_)
            pg.op("act", lambda e: e.activation(out=v3(SP)[:, :, 0:n], in_=ev_[:, :, 0:n], func=AF.Ln, bias=1.0, scale=1.0), reads=et_, writes=[SP])
            if t["diag"]:
                for hh in range(2):
                    pg.op("dve", lambda e, hh=hh: e.tensor_tensor(out=v3(SP)[:, hh, 0:128], in0=v3(SP)[:, hh, 0:128], in1=maskLT_b, op=ALU.mult),
                          reads=[SP, cbt], writes=[SP])

        def S2(i):
            t = tiles[i]
            qg, c0, n = t["qg"], t["c0"], t["n"]
            zt_, zv_ = Z2t[i % 3], Z2v[i % 3]
            SP = SP2[i % 3]
            R = R2[qg % 2]
            for hh in range(2):
                pg.op("pe", lambda e, hh=hh: e.matmul(zv_[:, hh, 0:n], lhsT=negtri_b, rhs=v3(SP)[:, hh, 0:n], start=False, stop=t["first"]),
                      reads=[SP, cbt], writes=[zt_[hh]])
                if not t["first"]:
                    pg.op("pe", lambda e, hh=hh: e.matmul(zv_[:, hh, 0:n], lhsT=negones_b, rhs=v3(R)[:, hh, c0:512], start=False, stop=True),
                          reads=[R, cbt], writes=[zt_[hh]])
            if not t["last"]:
                pg.op("dve", lambda e: e.tensor_tensor(out=v3(R)[:, :, c0:512], in0=v3(R)[:, :, c0:512], in1=v3(SP)[:, :, 0:n], op=ALU.add),
                      reads=[R, SP], writes=[R])

        def S3(i):
            t = tiles[i]
            n = t["n"]
            zt_, zv_ = Z2t[i % 3], Z2v[i % 3]
            A_ = A2[i % 3]
            pg.op("act", lambda e: e.activation(out=v3(A_)[:, :, 0:n], in_=zv_[:, :, 0:n], func=AF.Exp), reads=zt_, writes=[A_])
            if t["diag"]:
                for hh in range(2):
                    pg.op("dve", lambda e, hh=hh: e.tensor_tensor(out=v3(A_)[:, hh, 0:128], in0=v3(A_)[:, hh, 0:128], in1=maskLT_b, op=ALU.mult),
                          reads=[A_, cbt], writes=[A_])

        def S4(i):
            t = tiles[i]
            qg, kb, c0, n = t["qg"], t["kb"], t["c0"], t["n"]
            A_ = A2[i % 3]
            po = psO[qg % 2]
            for hh in range(2):
                pg.op("pe", lambda e, hh=hh: e.matmul(po.ap[hh * 64:(hh + 1) * 64, c0:512], lhsT=Vv[:, kb, hh * 64:(hh + 1) * 64], rhs=v3(A_)[:, hh, 0:n],
                                                       start=False, stop=(t["last"])), reads=[A_, vt_[kb // 8]], writes=[po])
            if t["last"]:
                pg.op("act", lambda e: e.activation(out=qt_[qg].ap, in_=po.ap, func=AF.Copy), reads=[po], writes=[qt_[qg]])

        for it in range(NT + 4):
            if 0 <= it - 4 < NT:
                S4(it - 4)
            if 0 <= it - 3 < NT:
                S3(it - 3)
            if 0 <= it - 2 < NT:
                S2(it - 2)
            if 0 <= it - 1 < NT:
                S1(it - 1)
            if it < NT:
                S0(it)
        s2v = send2_ap.rearrange("a b -> (a b)").rearrange("(r p t) -> r p t", r=8, p=128)
        send2_t = Tile(None)
        for r in range(8):
            pg.dma("sp", [lambda e, r=r: e.dma_start(out=s2v[r], in_=qT[:, r * 1024:(r + 1) * 1024])],
                   reads=[qt_[2 * r], qt_[2 * r + 1]], writes=[send2_t])
        return dict(yat=yat, ybt=ybt, send2_t=send2_t, wa_tiles=qt_ + kt_ + vt_, ma_tiles=SP2 + A2 + R2, yc_inherit=arena_toks(zdead))

    def phase_C(l, Bst, g2_ap, g2_t, x_src, final):
        win = W[("w_in", l)]
        pvl = l * PV_L
        yat, ybt = Bst["yat"], Bst["ybt"]
        yct = [Tile(ycv[:, j, :], inherit=Bst.get("yc_inherit")) for j in range(8)]
        xa_tiles.extend(yct)
        g2f = g2_ap.rearrange("a b -> (a b)")
        off = bass.IndirectOffsetOnAxis(ap=IDX[:, 0:1], axis=0)
        wold = arena_toks(Bst["wa_tiles"])
        slots = [Tile(WA[:, i * 4096:(i + 1) * 4096], inherit=wold) for i in range(4)]
        gt = Tile(WA[:, 16384:20480], inherit=wold)
        mt = Tile(WA[:, 20480:24576], inherit=wold)
        gv = WA[:, 16384:20480].bitcast(F32).rearrange("p (m t) -> p m t", m=2)
        mv = WA[:, 20480:24576].bitcast(F32).rearrange("p (m t) -> p m t", m=2)
        mold = arena_toks(Bst["ma_tiles"])
        mxv = MA[:].rearrange("p (k t) -> p k t", k=KC)
        mxt = [[Tile(mxv[:, k, hf * 512:(hf + 1) * 512], inherit=mold) for hf in range(2)] for k in range(KC)]
        ysrc = [
            (lambda k, hf: (yat[k][hf], yat[k][hf].ap)),
            (lambda k, hf: (ybt[(hf * 4)], ybv[:, k, hf * 512:(hf + 1) * 512])),
            (lambda k, hf: (yct[k], ycv[:, k, hf * 512:(hf + 1) * 512])),
        ]
        yb_all = ybt
        wouts = [W[("w_out_conv", l)], W[("w_out_sgu", l)], W[("w_out_sb", l)]]
        chunks = []
        for mg in range(8):
            for br in range(3):
                chunks.append(("g", br, mg))
                chunks.append(("o", br, mg))
        dsts = {}

        def issue(i):
            kind, br, mg = chunks[i]
            if kind == "g":
                dsts[i] = (slots[i % 4], load_w(slots[i % 4], wview(win, KC, wc(C_G + br * 2048 + mg * 256), 256), KC, 256))
            else:
                dsts[i] = (slots[i % 4], load_w(slots[i % 4], wview(wouts[br], 8, mg * 256, 256), 8, 256))

        for i in range(3):
            issue(i)
        src0 = g2f[0:X2].rearrange("(n t) -> n t", t=1024)
        for j in range(8):
            pg.dma("pool", [lambda e, j=j: e.indirect_dma_start(out=ycv[:, j, :], out_offset=None, in_=src0, in_offset=off, element_offset=j * X2)],
                   reads=[g2_t, idxt], writes=[yct[j]])
        for i, (kind, br, mg) in enumerate(chunks):
            if i + 3 < len(chunks):
                issue(i + 3)
            slot, wd = dsts[i]
            wdst_tile[0] = slot
            for mb in range(2):
                m = mg * 2 + mb
                for hf in range(2):
                    ps = nb()
                    if kind == "g":
                        mm_fm(ps, wd, KC, mb * 128, lambda k: (ht[k][hf], ht[k][hf].ap))
                        bcol = pvl + PV_BG + br * 16 + m
                        pg.op("act", lambda e, ps=ps, mb=mb, hf=hf, bcol=bcol: e.activation(
                            out=gv[:, mb, hf * 512:(hf + 1) * 512], in_=ps.ap, func=AF.Sigmoid, bias=PV[:, bcol:bcol + 1], scale=1.0),
                            reads=[ps, pvt], writes=[gt])
                    else:
                        if br == 1:
                            for k in range(8):
                                pg.op("pe", lambda e, k=k, ps=ps: e.matmul(ps.ap, lhsT=wd[:, k, mb * 128:(mb + 1) * 128],
                                                                           rhs=ybv[:, k, hf * 512:(hf + 1) * 512], start=(k == 0), stop=(k == 7)),
                                      reads=[slot] + yb_all[hf * 4:(hf + 1) * 4], writes=[ps])
                        else:
                            mm_fm(ps, wd, 8, mb * 128, lambda k: ysrc[br](k, hf))
                        gsl = gv[:, mb, hf * 512:(hf + 1) * 512]
                        msl = mv[:, mb, hf * 512:(hf + 1) * 512]
                        if br == 0:
                            pg.op("dve", lambda e, ps=ps, gsl=gsl, msl=msl: e.tensor_tensor(out=msl, in0=ps.ap, in1=gsl, op=ALU.mult),
                                  reads=[ps, gt], writes=[mt])
                        else:
                            tt = nt()
                            pg.op("dve", lambda e, ps=ps, gsl=gsl, tt=tt: e.tensor_tensor(out=tt.ap, in0=ps.ap, in1=gsl, op=ALU.mult),
                                  reads=[ps, gt], writes=[tt])
                            if br == 1:
                                pg.op("dve", lambda e, msl=msl, tt=tt: e.tensor_tensor(out=msl, in0=msl, in1=tt.ap, op=ALU.add),
                                      reads=[mt, tt], writes=[mt])
                            else:
                                pg.op("dve", lambda e, msl=msl, tt=tt, m=m, hf=hf: e.tensor_tensor(out=mxt[m][hf].ap, in0=msl, in1=tt.ap, op=ALU.add),
                                      reads=[mt, tt], writes=[mxt[m][hf]])
        ydead = arena_toks(alltiles(yat) + ybt + yct + xa_tiles)
        for row in xt:
            for t_ in row:
                t_.r.extend(ydead)
        load_x(x_src)
        wold = arena_toks(slots + [gt, mt])
        slots = [Tile(WA[:, i * 8192:(i + 1) * 8192], inherit=wold) for i in range(3)]
        wo = W[("w_o", l)]
        dsts = {}
        for i in range(2):
            dsts[i] = (slots[i % 3], load_w(slots[i % 3], wview(wo, KC, i * 512, 512), KC, 512))
        for i in range(4):
            if i + 2 < 4:
                dsts[i + 2] = (slots[(i + 2) % 3], load_w(slots[(i + 2) % 3], wview(wo, KC, (i + 2) * 512, 512), KC, 512))
            slot, wd = dsts[i]
            wdst_tile[0] = slot
            for eb in range(4):
                e_ = i * 4 + eb
                for hf in range(2):
                    ps = nb()
                    mm_fm(ps, wd, KC, eb * 128, lambda k: (mxt[k][hf], mxt[k][hf].ap))
                    pg.op("dve", lambda e, ps=ps, e_=e_, hf=hf: e.tensor_tensor(out=xt[e_][hf].ap, in0=ps.ap, in1=xt[e_][hf].ap, op=ALU.add),
                          reads=[ps, xt[e_][hf]], writes=[xt[e_][hf]])
        rmsnorm(pvl + PV_G2, ht)
        w1, w2 = W[("w_ff1", l)], W[("w_ff2", l)]
        mdead = arena_toks(alltiles(mxt))
        fv = [MA[:, i * 4096:(i + 1) * 4096].rearrange("p (k t) -> p k t", k=4) for i in range(2)]
        ft = [[[Tile(fv[i][:, k, hf * 512:(hf + 1) * 512], inherit=mdead) for hf in range(2)] for k in range(4)] for i in range(2)]
        NG = 16
        dsts = {}

        def issue_f(i):
            fg, which = i // 2, i % 2
            if which == 0:
                dsts[i] = (slots[i % 3], load_w(slots[i % 3], wview(w1, KC, fg * 512, 512), KC, 512))
            else:
                src = w2.rearrange("(k p) c -> p k c", p=128)[:, fg * 4:(fg + 1) * 4, :]
                dsts[i] = (slots[i % 3], load_w(slots[i % 3], src, 4, 2048))

        issue_f(0)
        issue_f(1)
        for i in range(2 * NG):
            if i + 2 < 2 * NG:
                issue_f(i + 2)
            fg, which = i // 2, i % 2
            slot, wd = dsts[i]
            wdst_tile[0] = slot
            fcur = ft[fg % 2]
            if which == 0:
                for fb in range(4):
                    for hf in range(2):
                        ps = nb()
                        mm_fm(ps, wd, KC, fb * 128, lambda k: (ht[k][hf], ht[k][hf].ap))
                        rl = nt()
                        pg.op("act", lambda e, ps=ps, rl=rl: e.activation(out=rl.ap, in_=ps.ap, func=AF.Relu), reads=[ps], writes=[rl])
                        pg.op("dve", lambda e, ps=ps, rl=rl, fb=fb, hf=hf: e.tensor_tensor(out=fcur[fb][hf].ap, in0=ps.ap, in1=rl.ap, op=ALU.mult),
                              reads=[ps, rl], writes=[fcur[fb][hf]])
            else:
                for e_ in range(KC):
                    for hf in range(2):
                        ps = nb()
                        mm_fm(ps, wd, 4, e_ * 128, lambda k: (fcur[k][hf], fcur[k][hf].ap))
                        pg.op("dve", lambda e, ps=ps, e_=e_, hf=hf: e.tensor_tensor(out=xt[e_][hf].ap, in0=ps.ap, in1=xt[e_][hf].ap, op=ALU.add),
                              reads=[ps, xt[e_][hf]], writes=[xt[e_][hf]])
        if final:
            rmsnorm(PV_FG, None, dst_f32_out=out_d)
        del ma_last[:]
        ma_last.extend([t_ for a_ in ft for b_ in a_ for t_ in b_])
        return dict(slots=slots)

    def store(dst, src_ap, reads):
        n = src_ap.shape[1]
        q = n // 4
        fns = [lambda e, a=a: e.dma_start(out=dst[:, a * q:(a + 1) * q], in_=src_ap[:, a * q:(a + 1) * q]) for a in range(4)]
        final_toks.append(pg.dma("sp", fns, reads=reads))

    def load(dst_ap, src, writes):
        n = dst_ap.shape[1]
        q = n // 4
        fns = [lambda e, a=a: e.dma_start(out=dst_ap[:, a * q:(a + 1) * q], in_=src[:, a * q:(a + 1) * q]) for a in range(4)]
        pg.dma("sp", fns, writes=writes)

    def allgather(src, dst, src_t):
        dst_t = Tile(None)
        pg.cc(lambda e: e.collective_compute("AllGather", ALU.bypass, replica_groups=[list(range(NCORES))],
                                             ins=[src.opt()], outs=[dst.opt()]), reads=[src_t], writes=[dst_t])
        return dst_t

    load_consts()
    if fused:
        load_x(xT_d)
        inherit = []
        for li, l in enumerate(layers):
            A = phase_A(l, send1, sendh, inherit, after_halo=lambda st: allgather(sendh, gh, st))
            gh_t = A["gh_t"]
            g1_t = allgather(send1, g1, A["send1_t"])
            Bst = phase_B(l, A, g1, gh, g1_t, gh_t, send2)
            g2_t = allgather(send2, g2, Bst["send2_t"])
            Cst = phase_C(l, Bst, g2, g2_t, ("xsp", A["xsp_t"]), final=(li == len(layers) - 1))
            inherit = arena_toks(Cst["slots"])
    elif mode == "A":
        load_x(xT_d)
        A = phase_A(layer, send1, sendh, [])
        final_toks.extend(A["send1_t"].toks() + A["sendh_t"].toks())
        store(hT_o, HA[:], alltiles(ht))
        store(zT_o, XA[:, 0:8 * ZWP], alltiles(A["zt"]))
    elif mode == "B":
        load(HA[:], hT_i, alltiles(ht))
        zt = [[Tile(zv[:, m, HALO + hf * 512: HALO + (hf + 1) * 512]) for hf in range(2)] for m in range(8)]
        zhalo_t = [Tile(zv[:, m, 0:HALO]) for m in range(8)]
        load(XA[:, 0:8 * ZWP], zT_i, alltiles(zt) + zhalo_t)
        A = dict(zt=zt, zhalo_t=zhalo_t, slots=carve_w(3, 8192, []),
                 qst=[Tile(MA[:, i * 1024:(i + 1) * 1024]) for i in range(2)],
                 vst=[Tile(MA[:, 2048 + i * 4096: 2048 + (i + 1) * 4096]) for i in range(2)])
        g1_t, gh_t = Tile(None), Tile(None)
        Bst = phase_B(layer, A, g1, gh, g1_t, gh_t, send2)
        final_toks.extend(Bst["send2_t"].toks())
        store(ya_o, XAb[:, 17408:25600], alltiles(Bst["yat"]))
        store(yb_o, XAb[:, 0:8192], Bst["ybt"])
    elif mode == "C":
        load(HA[:], hT_i, alltiles(ht))
        yat = [[Tile(yav[:, m, hf * 512:(hf + 1) * 512]) for hf in range(2)] for m in range(8)]
        ybt = [Tile(ybv[:, :, tb * 128:(tb + 1) * 128]) for tb in range(8)]
        load(XAb[:, 17408:25600], ya_i, alltiles(yat))
        load(XAb[:, 0:8192], yb_i, ybt)
        Bst = dict(yat=yat, ybt=ybt, wa_tiles=[], ma_tiles=[], yc_inherit=[])
        g2_t = Tile(None)
        phase_C(layer, Bst, g2, g2_t, xT_d, final=last)
        if not last:
            xo = out_d.rearrange("(k p) t -> p k t", p=128)
            fns = [lambda e, k0=k0: e.dma_start(out=xo[:, k0:k0 + 4, :], in_=xv[:, k0:k0 + 4, :]) for k0 in range(0, KC, 4)]
            final_toks.append(pg.dma("sp", fns, reads=alltiles(xt)))
    pg._wait("sp", final_toks)
    stack.close()
    return nc


def _pack_pv(inp, lmap):
    pv = np.zeros((128, PV_N), np.float32)
    for slot, l in lmap.items():
        o = slot * PV_L
        pv[:, o + PV_G1:o + PV_G1 + 16] = np.asarray(inp["attn_norm_g"][l]).reshape(16, 128).T
        pv[:, o + PV_G2:o + PV_G2 + 16] = np.asarray(inp["mlp_norm_g"][l]).reshape(16, 128).T
        pv[:, o + PV_BG:o + PV_BG + 48] = np.asarray(inp["b_gate"][l]).reshape(48, 128).T
        cw = np.asarray(inp["conv_w"][l])
        pv[:, o + PV_CW:o + PV_CW + 248] = cw.reshape(CW, 8, 128).transpose(2, 1, 0).reshape(128, 248)
        pv[:, o + PV_CB:o + PV_CB + 8] = np.asarray(inp["conv_b"][l]).reshape(8, 128).T
        pv[:, o + PV_LG:o + PV_LG + 8] = np.asarray(inp["conv_ln_g"][l]).reshape(8, 128).T
        pv[:, o + PV_LB:o + PV_LB + 8] = np.asarray(inp["conv_ln_b"][l]).reshape(8, 128).T
    pv[:, PV_FG:PV_FG + 16] = np.asarray(inp["final_norm_g"]).reshape(16, 128).T
    return pv


def _consts():
    i = np.arange(128)
    c = np.zeros((128, 5 * 128), np.float32)
    c[:, 0:128] = 1.0
    c[:, 128:256] = (i[:, None] <= i[None, :])
    c[:, 256:384] = (i[:, None] < i[None, :])
    c[:, 384:512] = -1.0 * (i[:, None] >= i[None, :])
    c[:, 512:640] = -1.0
    return c


def _pack_sgu(inp, l):
    a = np.zeros((128, 4096), np.float32)
    a[:, 0:1024] = np.asarray(inp["sgu_ln_g"][l])[None, :]
    a[:, 1024:2048] = np.asarray(inp["sgu_ln_b"][l])[None, :]
    a[:, 2048:3072] = np.asarray(inp["sgu_b"][l]).reshape(1, 1024)
    a[:, 3072:4096] = np.asarray(inp["sgu_w"][l]).transpose(2, 0, 1).reshape(128, 1024)
    return a


def _core_small(c):
    p = np.arange(128, dtype=np.int32)
    idx = np.stack([c * 128 + p, max(c - 1, 0) * 128 + p], axis=1).astype(np.int32)
    flag = np.full((128, 1), 0.0 if c == 0 else 1.0, np.float32)
    return idx, flag


_CACHE = {}


def _get(mode, last=False):
    key = (mode, last)
    if key not in _CACHE:
        _CACHE[key] = build(mode, 0, last)
    return _CACHE[key]


WNAMES = ["w_in", "w_out_conv", "w_out_sgu", "w_out_sb", "w_o", "w_ff1", "w_ff2"]


def _run(nc, maps):
    res = run_bass_kernel_spmd(nc, maps, core_ids=list(range(NCORES)))
    return res.results


def kernel_unfused(inp, nlayers=2, debug=None):
    x = np.asarray(inp["x"], np.float32)[0]
    xs = [np.ascontiguousarray(x[c * T:(c + 1) * T].T) for c in range(NCORES)]
    cst = _consts()
    small = [_core_small(c) for c in range(NCORES)]
    for l in range(nlayers):
        pv = _pack_pv(inp, {0: l})
        sgu = _pack_sgu(inp, l)
        wl = {nm: np.ascontiguousarray(np.asarray(inp[nm][l], np.float32)) for nm in WNAMES}
        base = [dict(pv=pv, cst=cst, idx=small[c][0], flag=small[c][1]) for c in range(NCORES)]
        w_in = wl["w_in"]
        w_inA = np.ascontiguousarray(np.concatenate([w_in[:, 0:2048], w_in[:, 4096:7168]], axis=1))
        w_inB = np.ascontiguousarray(w_in[:, 2048:4096])
        w_inC = np.ascontiguousarray(w_in[:, C_G:])
        maps = [dict(base[c], xT=xs[c], w_in0=w_inA) for c in range(NCORES)]
        ra = _run(_get("A"), maps)
        g1 = np.concatenate([ra[c]["send1"] for c in range(NCORES)], axis=0)
        gh = np.concatenate([ra[c]["sendh"] for c in range(NCORES)], axis=0)
        if debug is not None:
            debug[f"A{l}"] = ra
        maps = [dict(base[c], w_in0=w_inB, sgu0=sgu, g1=g1, gh=gh, hT_i=ra[c]["hT_o"], zT_i=ra[c]["zT_o"]) for c in range(NCORES)]
        rb = _run(_get("B"), maps)
        g2 = np.concatenate([rb[c]["send2"] for c in range(NCORES)], axis=0)
        if debug is not None:
            debug[f"B{l}"] = rb
        last = (l == nlayers - 1)
        maps = [dict(base[c], xT=xs[c], g2=g2, hT_i=ra[c]["hT_o"], ya_i=rb[c]["ya_o"], yb_i=rb[c]["yb_o"],
                     **{f"{nm}0": (w_inC if nm == "w_in" else wl[nm]) for nm in WNAMES}) for c in range(NCORES)]
        rc = _run(_get("C", last), maps)
        xs = [rc[c]["out"] for c in range(NCORES)]
    out = np.concatenate([xs[c].T for c in range(NCORES)], axis=0)[None]
    return np.ascontiguousarray(out.astype(np.float32))


def kernel_fused(inp):
    x = np.asarray(inp["x"], np.float32)[0]
    cst = _consts()
    pv = _pack_pv(inp, {0: 0, 1: 1})
    shared = dict(pv=pv, cst=cst)
    for l in range(2):
        shared[f"sgu{l}"] = _pack_sgu(inp, l)
        for nm in WNAMES:
            shared[f"{nm}{l}"] = np.ascontiguousarray(np.asarray(inp[nm][l], np.float32))
    maps = []
    for c in range(NCORES):
        idx, flag = _core_small(c)
        maps.append(dict(shared, idx=idx, flag=flag, xT=np.ascontiguousarray(x[c * T:(c + 1) * T].T)))
    res = _run(_get("F"), maps)
    out = np.concatenate([res[c]["out"].T for c in range(NCORES)], axis=0)[None]
    return np.ascontiguousarray(out.astype(np.float32))


def kernel(**inputs):
    if os.environ.get("KFUSED", "1") == "1":
        return kernel_fused(inputs)
    return kernel_unfused(inputs)
```

```python
import os
import numpy as np
import ml_dtypes
import concourse.bass as bass
import concourse.mybir as mybir
from concourse.bass_utils import run_bass_kernel_spmd

F32 = mybir.dt.float32
BF16 = mybir.dt.bfloat16
I32 = mybir.dt.int32
AF = mybir.ActivationFunctionType
ALU = mybir.AluOpType

NCORES = 8
D = 2048
KC = 16
T = 1024
S = 8192
IN_DIM = 13312
DFF = 8192
EPS = 1e-6
CW = 31
HALO = 30
ZW = T + HALO
ZWP = 1056
C_PA, C_PB, C_Q, C_K, C_V, C_G = 0, 2048, 4096, 5120, 6144, 7168

PV_G1, PV_G2, PV_BG, PV_CW, PV_CB, PV_LG, PV_LB = 0, 16, 32, 80, 80 + 248, 80 + 256, 80 + 264
PV_L = 80 + 272
PV_FG = 2 * PV_L
PV_N = PV_FG + 16

X1 = 3 * 8 * 128 * 1024
XH = 128 * 8 * HALO
X2 = 8 * 128 * 1024


class Tile:
    __slots__ = ("ap", "w", "r")

    def __init__(self, ap, inherit=None):
        self.ap = ap
        self.w = None
        self.r = list(inherit) if inherit else []

    def toks(self):
        return ([self.w] if self.w else []) + list(self.r)


class Prog:
    def __init__(self, nc, sems):
        self.nc = nc
        self.eng = {"pe": nc.tensor, "act": nc.scalar, "dve": nc.vector, "pool": nc.gpsimd, "sp": nc.sync}
        self.sem = sems
        self.cnt = {k: 0 for k in sems}
        self.seen = {e: {} for e in self.eng}
        self.rr = {"sp": 0, "pool": 0}
        self.ndma = {"sp": [k for k in sems if k.startswith("dsp")], "pool": [k for k in sems if k.startswith("dpl")]}

    def _wait(self, e, toks):
        need = {}
        for (s, v) in toks:
            if need.get(s, 0) < v:
                need[s] = v
        for s, v in need.items():
            if s == e and e == "pe":
                continue
            if self.seen[e].get(s, 0) >= v:
                continue
            self.eng[e].wait_ge(self.sem[s], v)
            self.seen[e][s] = v

    def _deps(self, reads, writes):
        toks = []
        for t in reads:
            if t.w:
                toks.append(t.w)
        for t in writes:
            toks.extend(t.toks())
        return toks

    def _commit(self, tok, reads, writes):
        for t in reads:
            t.r.append(tok)
        for t in writes:
            t.w = tok
            t.r = []

    def op(self, e, fn, reads=(), writes=()):
        self._wait(e, self._deps(reads, writes))
        ins = fn(self.eng[e])
        self.cnt[e] += 1
        ins.then_inc(self.sem[e], 1)
        tok = (e, self.cnt[e])
        self._commit(tok, reads, writes)
        return tok

    def dma(self, q, fns, reads=(), writes=()):
        names = self.ndma[q]
        s = names[self.rr[q] % len(names)]
        self.rr[q] += 1
        toks = self._deps(reads, writes)
        toks.append((s, self.cnt[s]))
        self._wait(q, toks)
        for fn in fns:
            ins = fn(self.eng[q])
            ins.then_inc(self.sem[s], 16)
            self.cnt[s] += 16
        tok = (s, self.cnt[s])
        self._commit(tok, reads, writes)
        return tok

    def cc(self, fn, reads=(), writes=()):
        self._wait("pool", self._deps(reads, writes))
        ins = fn(self.eng["pool"])
        self.cnt["cc"] += 1
        ins.then_inc(self.sem["cc"], 1)
        tok = ("cc", self.cnt["cc"])
        self._commit(tok, reads, writes)
        return tok

    def finish(self, e, tiles):
        toks = []
        for t in tiles:
            toks.extend(t.toks())
        self._wait(e, toks)


def _ctx_enter(stack, cm):
    return stack.enter_context(cm)


def build(mode, layer=0, last=False):
    import contextlib
    nc = bass.Bass("TRN2", target_bir_lowering=False)
    fused = mode == "F"
    layers = [0, 1] if fused else [layer]
    stack = contextlib.ExitStack()

    def din(name, shape, dt):
        return nc.dram_tensor(name, list(shape), dt, kind="ExternalInput").ap()

    def dout(name, shape, dt):
        return nc.dram_tensor(name, list(shape), dt, kind="ExternalOutput").ap()

    def dint(name, shape, dt):
        return nc.dram_tensor(name, list(shape), dt).ap()

    W = {}
    need_w = {"A": ["w_in"], "B": ["w_in"], "C": ["w_in", "w_out_conv", "w_out_sgu", "w_out_sb", "w_o", "w_ff1", "w_ff2"]}
    wshapes = {"w_in": (D, IN_DIM), "w_out_conv": (1024, D), "w_out_sgu": (1024, D), "w_out_sb": (1024, D),
               "w_o": (D, D), "w_ff1": (D, DFF), "w_ff2": (DFF, D)}
    win_cols = {"A": 5120, "B": 2048, "C": 6144, "F": IN_DIM}[mode]
    wshapes["w_in"] = (D, win_cols)
    wc = {"A": (lambda c: c if c < 2048 else c - 2048), "B": (lambda c: c - 2048), "C": (lambda c: c - C_G), "F": (lambda c: c)}[mode]
    for l in layers:
        for nm in (need_w["C"] if fused else need_w[mode]):
            W[(nm, l)] = din(f"{nm}{l}", wshapes[nm], F32)
    pv_d = din("pv", (128, PV_N), F32)
    cst_d = din("cst", (128, 5 * 128), F32)
    idx_d = din("idx", (128, 2), I32)
    flag_d = din("flag", (128, 1), F32)
    sgu_d = {}
    if fused or mode == "B":
        for l in layers:
            sgu_d[l] = din(f"sgu{l}", (128, 1024 * 4), F32)

    if fused or mode in ("A", "C"):
        xT_d = din("xT", (D, T), F32)
    if fused:
        out_d = dout("out", (D, T), F32)
        send1 = dint("send1", (16, X1 // 16), BF16)
        g1 = dint("g1", (128, X1 // 16), BF16)
        sendh = dint("sendh", (16, XH // 16), F32)
        gh = dint("gh", (128, XH // 16), F32)
        send2 = dint("send2", (16, X2 // 16), BF16)
        g2 = dint("g2", (128, X2 // 16), BF16)
        xsp = dint("xsp", (128, KC * T), F32)
    else:
        if mode == "A":
            send1 = dout("send1", (16, X1 // 16), BF16)
            sendh = dout("sendh", (16, XH // 16), F32)
            hT_o = dout("hT_o", (128, KC * T), BF16)
            zT_o = dout("zT_o", (128, 8 * ZWP), F32)
        if mode == "B":
            g1 = din("g1", (128, X1 // 16), BF16)
            gh = din("gh", (128, XH // 16), F32)
            hT_i = din("hT_i", (128, KC * T), BF16)
            zT_i = din("zT_i", (128, 8 * ZWP), F32)
            send2 = dout("send2", (16, X2 // 16), BF16)
            if os.environ.get("DBGQ") or os.environ.get("DBGT"):
                dbg_q = dout("dbg_q", (128, 8192), BF16)
                dbg_k = dout("dbg_k", (128, 8192), BF16)
                dbg_v = dout("dbg_v", (128, 8192), BF16)
                dbg_t = dout("dbg_t", (128, 3 * 4 * 512), F32)
            ya_o = dout("ya_o", (128, 8 * T), BF16)
            yb_o = dout("yb_o", (128, 8 * T), BF16)
        if mode == "C":
            g2 = din("g2", (128, X2 // 16), BF16)
            hT_i = din("hT_i", (128, KC * T), BF16)
            ya_i = din("ya_i", (128, 8 * T), BF16)
            yb_i = din("yb_i", (128, 8 * T), BF16)
            out_d = dout("out", (D, T), F32)

    def sb(name, shape, dt):
        return _ctx_enter(stack, nc.sbuf_tensor(name, list(shape), dt))

    XA = sb("XA", (128, KC * T), F32)
    HA = sb("HA", (128, KC * T), BF16)
    WA = sb("WA", (128, 3 * 8192), BF16)
    MA = sb("MA", (128, 16384), BF16)
    TM = sb("TM", (128, 6 * 512), F32)
    PV = sb("PV", (128, PV_N), F32)
    CF = sb("CF", (128, 5 * 128), F32)
    CB = sb("CB", (128, 5 * 128), BF16)
    IDX = sb("IDX", (128, 2), I32)
    FLG = sb("FLG", (128, 1), F32)
    SM = sb("SM", (128, 64), F32)
    HL = sb("HL", (128, 8 * HALO), F32)
    PS = [_ctx_enter(stack, nc.psum_tensor(f"PS{i}", [128, 1024], F32)) for i in range(4)]

    sem_names = ["pe", "act", "dve", "pool", "sp", "cc"] + [f"dsp{i}" for i in range(12)] + [f"dpl{i}" for i in range(12)]
    sems = {n: _ctx_enter(stack, nc.semaphore(n)) for n in sem_names}
    pg = Prog(nc, sems)

    xv = XA[:].rearrange("p (k t) -> p k t", k=KC)
    hv = HA[:].rearrange("p (k t) -> p k t", k=KC)
    xt = [[Tile(xv[:, k, hf * 512:(hf + 1) * 512]) for hf in range(2)] for k in range(KC)]
    ht = [[Tile(hv[:, k, hf * 512:(hf + 1) * 512]) for hf in range(2)] for k in range(KC)]
    bank = [Tile(PS[i // 2][:, (i % 2) * 512:(i % 2 + 1) * 512]) for i in range(8)]
    tmp = [Tile(TM[:, i * 512:(i + 1) * 512]) for i in range(6)]
    pvt = Tile(PV[:])
    cft = Tile(CF[:])
    cbt = Tile(CB[:])
    idxt = Tile(IDX[:])
    flgt = Tile(FLG[:])
    smt = Tile(SM[:])
    hlt = Tile(HL[:])
    onesF = CF[:, 0:128]
    maskLE = CF[:, 128:256]
    maskLT_b = CB[:, 256:384]
    negtri_b = CB[:, 384:512]
    negones_b = CB[:, 512:640]
    zeros_b = CB[:, 0:128]

    state = {"bank": 0, "tmp": 0}

    def nb():
        b = bank[state["bank"] % 8]
        state["bank"] += 1
        return b

    def nt():
        t = tmp[state["tmp"] % 3]
        state["tmp"] += 1
        return t

    LT = [tmp[3], tmp[4], tmp[5]]

    def alltiles(tt):
        return [t for row in tt for t in row]

    XAb = XA[:].bitcast(BF16)
    zv = XA[:, 0:8 * ZWP].rearrange("p (m t) -> p m t", m=8)
    ybv = XAb[:, 0:8192].rearrange("p (m t) -> p m t", m=8)
    ycv = XAb[:, 8192:16384].rearrange("p (m t) -> p m t", m=8)
    yav = XAb[:, 17408:25600].rearrange("p (m t) -> p m t", m=8)
    XFREE = 12800
    accv = [XA[:, XFREE + i * 1024: XFREE + (i + 1) * 1024] for i in range(2)]
    sguc = XA[:, XFREE: XFREE + 3584]

    def load_consts():
        pg.dma("sp", [lambda e: e.dma_start(out=PV[:], in_=pv_d[:, :])], writes=[pvt])
        pg.dma("sp", [lambda e: e.dma_start(out=CF[:], in_=cst_d[:, :])], writes=[cft])
        pg.dma("sp", [lambda e: e.dma_start(out=IDX[:], in_=idx_d[:, :])], writes=[idxt])
        pg.dma("sp", [lambda e: e.dma_start(out=FLG[:], in_=flag_d[:, :])], writes=[flgt])
        pg.op("dve", lambda e: e.tensor_copy(out=CB[:], in_=CF[:]), reads=[cft], writes=[cbt])
        pg.op("dve", lambda e: e.memset(CB[:, 0:128], 0.0), writes=[cbt])

    def wview(wap, rows_kc, c0, ncols):
        return wap.rearrange("(k p) c -> p k c", p=128)[:, 0:rows_kc, c0:c0 + ncols]

    wslots = {}

    def carve_w(n, size, inherit):
        sl = []
        for i in range(n):
            sl.append(Tile(WA[:, i * size:(i + 1) * size], inherit=inherit))
        return sl

    def arena_toks(tiles):
        toks = []
        for t in tiles:
            toks.extend(t.toks())
        best = {}
        for s, v in toks:
            if best.get(s, 0) < v:
                best[s] = v
        return list(best.items())

    def load_w(slot, src, kcs, ncols):
        dst = slot.ap[:, 0:kcs * ncols].rearrange("p (k c) -> p k c", k=kcs)
        step = max(1, 512 // 128 * 1)
        step = 4
        fns = []
        for k0 in range(0, kcs, step):
            k1 = min(kcs, k0 + step)
            fns.append(lambda e, k0=k0, k1=k1: e.dma_start(out=dst[:, k0:k1, :], in_=src[:, k0:k1, :]))
        pg.dma("pool", fns, writes=[slot])
        return dst

    def rmsnorm(gcol0, dst_tiles, dst_f32_out=None):
        for hf in range(2):
            ps = nb()
            for k in range(KC):
                sq = nt()
                pg.op("act", lambda e, k=k, sq=sq: e.activation(out=sq.ap, in_=xt[k][hf].ap, func=AF.Square),
                      reads=[xt[k][hf]], writes=[sq])
                pg.op("pe", lambda e, k=k, sq=sq: e.matmul(ps.ap, lhsT=onesF, rhs=sq.ap, start=(k == 0), stop=(k == KC - 1)),
                      reads=[sq, cft], writes=[ps])
            rs = LT[0]
            pg.op("act", lambda e: e.activation(out=rs.ap, in_=ps.ap, func=AF.Sqrt, bias=EPS, scale=1.0 / D),
                  reads=[ps], writes=[rs])
            pg.op("dve", lambda e: e.reciprocal(out=rs.ap, in_=rs.ap), reads=[rs], writes=[rs])
            for k in range(KC):
                if dst_f32_out is None:
                    pg.op("dve", lambda e, k=k: e.scalar_tensor_tensor(
                        out=dst_tiles[k][hf].ap, in0=xt[k][hf].ap, scalar=PV[:, gcol0 + k:gcol0 + k + 1],
                        in1=rs.ap, op0=ALU.mult, op1=ALU.mult),
                        reads=[xt[k][hf], rs, pvt], writes=[dst_tiles[k][hf]])
                else:
                    o = nt()
                    pg.op("dve", lambda e, k=k, o=o: e.scalar_tensor_tensor(
                        out=o.ap, in0=xt[k][hf].ap, scalar=PV[:, gcol0 + k:gcol0 + k + 1],
                        in1=rs.ap, op0=ALU.mult, op1=ALU.mult),
                        reads=[xt[k][hf], rs, pvt], writes=[o])
                    final_toks.append(pg.dma("sp", [lambda e, k=k, o=o: e.dma_start(
                        out=dst_f32_out[k * 128:(k + 1) * 128, hf * 512:(hf + 1) * 512], in_=o.ap)],
                        reads=[o]))

    def load_x(src):
        if isinstance(src, tuple):
            fns = [lambda e, k0=k0: e.dma_start(out=XA[:, k0 * T:(k0 + 4) * T], in_=xsp[:, k0 * T:(k0 + 4) * T]) for k0 in range(0, KC, 4)]
            pg.dma("sp", fns, reads=[src[1]], writes=alltiles(xt))
            return
        v = src.rearrange("(k p) t -> p k t", p=128)
        fns = [lambda e, k0=k0: e.dma_start(out=xv[:, k0:k0 + 4, :], in_=v[:, k0:k0 + 4, :]) for k0 in range(0, KC, 4)]
        pg.dma("sp", fns, writes=alltiles(xt))

    xa_tiles = []
    final_toks = []
    ma_last = []

    def nb2():
        if state["bank"] % 2:
            state["bank"] += 1
        b0 = bank[state["bank"] % 8]
        b1 = bank[(state["bank"] + 1) % 8]
        state["bank"] += 2
        return b0, b1

    def mm_fm(ps, wdst, kcs, col0, rhs_tiles_fn):
        for k in range(kcs):
            rt, rap = rhs_tiles_fn(k)
            pg.op("pe", lambda e, k=k, rap=rap: e.matmul(ps.ap, lhsT=wdst[:, k, col0:col0 + 128], rhs=rap,
                                                          start=(k == 0), stop=(k == kcs - 1)),
                  reads=[rt, wdst_tile[0]], writes=[ps])

    wdst_tile = [None]


    def conv_taps(l, zt, zhalo_t, gh_ap, gh_t, xinh):
        pvl = l * PV_L
        ghv = gh_ap.rearrange("a b -> (a b)").rearrange("(n c) -> n c", c=8 * HALO)
        pg.dma("pool", [lambda e: e.indirect_dma_start(
            out=HL[:], out_offset=None, in_=ghv, in_offset=bass.IndirectOffsetOnAxis(ap=IDX[:, 1:2], axis=0))],
            reads=[gh_t, idxt], writes=[hlt])
        hlv = HL[:].rearrange("p (m t) -> p m t", m=8)
        pg.op("dve", lambda e: e.tensor_scalar(out=zv[:, :, 0:HALO], in0=hlv, scalar1=FLG[:, 0:1], scalar2=None, op0=ALU.mult),
              reads=[hlt, flgt], writes=zhalo_t)
        acct = [Tile(accv[i], inherit=xinh) for i in range(2)]
        xa_tiles.extend(acct)
        cw0 = pvl + PV_CW
        for m in range(8):
            acc = acct[m % 2]
            zall = [zt[m][0], zt[m][1], zhalo_t[m]]
            pg.op("dve", lambda e, m=m, acc=acc: e.tensor_scalar(
                out=acc.ap, in0=zv[:, m, 0:T], scalar1=PV[:, cw0 + m * CW: cw0 + m * CW + 1],
                scalar2=PV[:, pvl + PV_CB + m: pvl + PV_CB + m + 1], op0=ALU.mult, op1=ALU.add),
                reads=zall + [pvt], writes=[acc])
            for k in range(1, CW - 1):
                pg.op("dve", lambda e, m=m, acc=acc, k=k: e.scalar_tensor_tensor(
                    out=acc.ap, in0=zv[:, m, k:k + T], scalar=PV[:, cw0 + m * CW + k: cw0 + m * CW + k + 1],
                    in1=acc.ap, op0=ALU.mult, op1=ALU.add), reads=zall + [acc, pvt], writes=[acc])
            k = CW - 1
            pg.op("dve", lambda e, m=m, acc=acc, k=k: e.scalar_tensor_tensor(
                out=zv[:, m, HALO:HALO + T], in0=zv[:, m, k:k + T], scalar=PV[:, cw0 + m * CW + k: cw0 + m * CW + k + 1],
                in1=acc.ap, op0=ALU.mult, op1=ALU.add), reads=zall + [acc, pvt], writes=[zt[m][0], zt[m][1]])
        return acct

    def conv_norm(l, zt, yat):
        pvl = l * PV_L
        s1 = [nb(), nb()]
        s2 = [nb(), nb()]
        for m in range(8):
            for hf in range(2):
                sq = nt()
                pg.op("act", lambda e, sq=sq, hf=hf, m=m: e.activation(out=sq.ap, in_=zt[m][hf].ap, func=AF.Square),
                      reads=[zt[m][hf]], writes=[sq])
                pg.op("pe", lambda e, hf=hf, m=m: e.matmul(s1[hf].ap, lhsT=onesF, rhs=zt[m][hf].ap,
                                                           start=(m == 0), stop=(m == 7)), reads=[zt[m][hf], cft], writes=[s1[hf]])
                pg.op("pe", lambda e, sq=sq, hf=hf, m=m: e.matmul(s2[hf].ap, lhsT=onesF, rhs=sq.ap,
                                                                  start=(m == 0), stop=(m == 7)), reads=[sq, cft], writes=[s2[hf]])
        for hf in range(2):
            mean, var, rstd = LT[0], LT[1], LT[2]
            pg.op("act", lambda e: e.activation(out=mean.ap, in_=s1[hf].ap, func=AF.Copy, scale=1.0 / 1024), reads=[s1[hf]], writes=[mean])
            pg.op("dve", lambda e: e.tensor_tensor(out=var.ap, in0=mean.ap, in1=mean.ap, op=ALU.mult), reads=[mean], writes=[var])
            pg.op("dve", lambda e: e.scalar_tensor_tensor(out=var.ap, in0=s2[hf].ap, scalar=1.0 / 1024, in1=var.ap,
                                                          op0=ALU.mult, op1=ALU.subtract), reads=[s2[hf], var], writes=[var])
            pg.op("act", lambda e: e.activation(out=rstd.ap, in_=var.ap, func=AF.Sqrt, bias=EPS, scale=1.0), reads=[var], writes=[rstd])
            pg.op("dve", lambda e: e.reciprocal(out=rstd.ap, in_=rstd.ap), reads=[rstd], writes=[rstd])
            for m in range(8):
                t1 = nt()
                pg.op("dve", lambda e, m=m, t1=t1: e.tensor_tensor(out=t1.ap, in0=zt[m][hf].ap, in1=mean.ap, op=ALU.subtract),
                      reads=[zt[m][hf], mean], writes=[t1])
                pg.op("dve", lambda e, t1=t1: e.tensor_tensor(out=t1.ap, in0=t1.ap, in1=rstd.ap, op=ALU.mult),
                      reads=[t1, rstd], writes=[t1])
                pg.op("act", lambda e, m=m, t1=t1: e.activation(
                    out=yat[m][hf].ap, in_=t1.ap, func=AF.Silu,
                    scale=PV[:, pvl + PV_LG + m: pvl + PV_LG + m + 1], bias=PV[:, pvl + PV_LB + m: pvl + PV_LB + m + 1]),
                    reads=[t1, pvt], writes=[yat[m][hf]])

    def phase_A(l, send1_ap, sendh_ap, carve_inherit, after_halo=None):
        win = W[("w_in", l)]
        rmsnorm(l * PV_L + PV_G1, ht)
        if fused:
            fns = [lambda e, k0=k0: e.dma_start(out=xsp[:, k0 * T:(k0 + 4) * T], in_=XA[:, k0 * T:(k0 + 4) * T]) for k0 in range(0, KC, 4)]
            xsp_t = Tile(None)
            pg.dma("sp", fns, reads=alltiles(xt), writes=[xsp_t])
        else:
            xsp_t = None
        slots = carve_w(3, 8192, carve_inherit)
        s1f = send1_ap.rearrange("a b -> (a b)")
        sec = [s1f[i * 1048576:(i + 1) * 1048576] for i in range(3)]
        qsec = [sec[i].rearrange("(j p t) -> j p t", j=8, p=128) for i in range(2)]
        vsec = sec[2].rearrange("(j p b f) -> j p b f", j=8, p=128, b=8)
        send1_t = Tile(None)
        minh = arena_toks(ma_last)
        qst = [Tile(MA[:, i * 1024:(i + 1) * 1024], inherit=minh) for i in range(2)]
        vst = [Tile(MA[:, 2048 + i * 4096: 2048 + (i + 1) * 4096], inherit=minh) for i in range(2)]
        chunks = [("pv", C_PA, 0), ("pg", C_PA + 1024, 0), ("pv", C_PA + 512, 1), ("pg", C_PA + 1536, 1),
                  ("q", C_Q, 0), ("q", C_Q + 512, 1), ("k", C_K, 0), ("k", C_K + 512, 1), ("v", C_V, 0), ("v", C_V + 512, 1)]
        shv = sendh_ap.rearrange("a b -> (a b)").rearrange("(p m t) -> p m t", p=128, m=8)
        sendh_t = Tile(None)
        ret = {}
        xinh = arena_toks(alltiles(xt))
        zt = [[Tile(zv[:, m, HALO + hf * 512: HALO + (hf + 1) * 512], inherit=xinh) for hf in range(2)] for m in range(8)]
        zhalo_t = [Tile(zv[:, m, 0:HALO], inherit=xinh) for m in range(8)]
        xa_tiles.extend(alltiles(zt) + zhalo_t)
        dsts = {}

        def issue(i):
            kind, c0, ix = chunks[i]
            dsts[i] = (slots[i % 3], load_w(slots[i % 3], wview(win, KC, wc(c0), 512), KC, 512))

        issue(0)
        issue(1)
        qn = 0
        for i, (kind, c0, ix) in enumerate(chunks):
            if i + 2 < len(chunks) and kind != "pg":
                issue(i + 2)
            slot, wd = dsts[i]
            wdst_tile[0] = slot
            if kind in ("q", "k"):
                for jb in range(4):
                    j = ix * 4 + jb
                    st = qst[qn % 2]
                    qn += 1
                    for hf in range(2):
                        ps = nb()
                        mm_fm(ps, wd, KC, jb * 128, lambda k: (ht[k][hf], ht[k][hf].ap))
                        pg.op("act", lambda e, ps=ps, st=st, hf=hf: e.activation(
                            out=st.ap[:, hf * 512:(hf + 1) * 512], in_=ps.ap, func=AF.Copy,
                            scale=(0.125 if kind == "q" else 1.0)), reads=[ps], writes=[st])
                    sidx = 0 if kind == "q" else 1
                    pg.dma("sp", [lambda e, st=st, j=j, sidx=sidx: e.dma_start(out=qsec[sidx][j], in_=st.ap)],
                           reads=[st], writes=[send1_t])
            elif kind == "v":
                st = vst[ix % 2]
                stv = st.ap.rearrange("p (b f) -> p b f", b=8)
                for tb in range(8):
                    ps = nb()
                    hf = tb // 4
                    for k in range(KC):
                        pg.op("pe", lambda e, k=k, ps=ps, tb=tb: e.matmul(
                            ps.ap, lhsT=hv[:, k, tb * 128:(tb + 1) * 128], rhs=wd[:, k, :],
                            start=(k == 0), stop=(k == KC - 1)), reads=[ht[k][hf], slot], writes=[ps])
                    pg.op("act", lambda e, ps=ps, tb=tb: e.activation(out=stv[:, tb, :], in_=ps.ap, func=AF.Copy), reads=[ps], writes=[st])
                for jj in range(4):
                    j = ix * 4 + jj
                    pg.dma("sp", [lambda e, j=j, jj=jj: e.dma_start(out=vsec[j], in_=stv[:, :, jj * 128:(jj + 1) * 128])],
                           reads=[st], writes=[send1_t])
            elif kind == "pv":
                pass
            elif kind == "pg":
                slot_v, wd_v = dsts[i - 1]
                for mb in range(4):
                    m = ix * 4 + mb
                    for hf in range(2):
                        psv = nb()
                        psg = nb()
                        wdst_tile[0] = slot_v
                        mm_fm(psv, wd_v, KC, mb * 128, lambda k: (ht[k][hf], ht[k][hf].ap))
                        wdst_tile[0] = slot
                        mm_fm(psg, wd, KC, mb * 128, lambda k: (ht[k][hf], ht[k][hf].ap))
                        sg = nt()
                        pg.op("act", lambda e, psg=psg, sg=sg: e.activation(out=sg.ap, in_=psg.ap, func=AF.Sigmoid),
                              reads=[psg], writes=[sg])
                        pg.op("dve", lambda e, psv=psv, sg=sg, m=m, hf=hf: e.tensor_tensor(
                            out=zt[m][hf].ap, in0=psv.ap, in1=sg.ap, op=ALU.mult),
                            reads=[psv, sg], writes=[zt[m][hf]])
                if i + 2 < len(chunks):
                    issue(i + 2)
                if ix == 1:
                    pg.dma("sp", [lambda e: e.dma_start(out=shv, in_=zv[:, :, T:T + HALO])], reads=[zt[m][1] for m in range(8)], writes=[sendh_t])
                    if after_halo is not None:
                        ret["gh_t"] = after_halo(sendh_t)
                        ret["acct"] = conv_taps(l, zt, zhalo_t, gh, ret["gh_t"], xinh)
        if after_halo is not None:
            yat_ = [[Tile(yav[:, m, hf * 512:(hf + 1) * 512], inherit=xinh) for hf in range(2)] for m in range(8)]
            xa_tiles.extend(alltiles(yat_))
            conv_norm(l, zt, yat_)
            ret["yat"] = yat_
        return dict(gh_t=ret.get("gh_t"), yat=ret.get("yat"), acct=ret.get("acct"), zt=zt, zhalo_t=zhalo_t, send1_t=send1_t, sendh_t=sendh_t, slots=slots, xsp_t=xsp_t, qst=qst, vst=vst)

    def phase_B(l, A, g1_ap, gh_ap, g1_t, gh_t, send2_ap):
        win = W[("w_in", l)]
        zt, zhalo_t = A["zt"], A["zhalo_t"]
        pvl = l * PV_L
        xinh = arena_toks(alltiles(xt))
        yat = [[Tile(yav[:, m, hf * 512:(hf + 1) * 512], inherit=xinh) for hf in range(2)] for m in range(8)]
        xa_tiles.extend(alltiles(yat))
        if A.get("yat") is not None:
            yat = A["yat"]
            acct = A["acct"]
        else:
            acct = conv_taps(l, zt, zhalo_t, gh_ap, gh_t, xinh)
            conv_norm(l, zt, yat)
        zdead = [t for row in zt for t in row] + zhalo_t
        ybt = [Tile(ybv[:, :, tb * 128:(tb + 1) * 128], inherit=arena_toks(zdead)) for tb in range(8)]
        sgc_t = Tile(sguc, inherit=arena_toks(acct))
        xa_tiles.extend(ybt + [sgc_t])
        pg.dma("sp", [lambda e: e.dma_start(out=sguc[:, 0:3072], in_=sgu_d[l][:, 0:3072])], writes=[sgc_t])
        lng_bc = sguc[:, 0:1024]
        lnb_bc = sguc[:, 1024:2048]
        bs_bc = sguc[:, 2048:3072].rearrange("p (g t) -> p g t", g=8)
        wTm = sguc[:, 3072:3584].bitcast(BF16).rearrange("p (g t) -> p g t", g=8)
        wtmp = [LT[0], LT[1]]
        for i in range(2):
            pg.dma("sp", [lambda e, i=i: e.dma_start(out=wtmp[i].ap, in_=sgu_d[l][:, 3072 + i * 512: 3072 + (i + 1) * 512])], writes=[wtmp[i]])
            for gg in range(4):
                g = i * 4 + gg
                pg.op("dve", lambda e, i=i, gg=gg, g=g: e.tensor_tensor(out=wTm[:, g, :], in0=wtmp[i].ap[:, gg * 128:(gg + 1) * 128],
                                                                        in1=maskLE, op=ALU.mult), reads=[wtmp[i], cft], writes=[sgc_t])
        slots = A["slots"]
        uT = MA[:, 0:8192].rearrange("p (m t) -> p m t", m=8)
        stg = A["qst"] + A["vst"]
        ut = [[Tile(uT[:, m, hf * 512:(hf + 1) * 512], inherit=arena_toks(stg)) for hf in range(2)] for m in range(8)]
        vn_t = [Tile(MA[:, 8192 + i * 1024: 8192 + (i + 1) * 1024], inherit=arena_toks(stg)) for i in range(2)]
        vg_t = [Tile(MA[:, 10240 + i * 2048: 10240 + (i + 1) * 2048], inherit=arena_toks(stg)) for i in range(2)]
        chunks = [C_PB, C_PB + 512, C_PB + 1024, C_PB + 1536]
        dsts = {}
        for i in range(3):
            dsts[i] = (slots[i % 3], load_w(slots[i % 3], wview(win, KC, wc(chunks[i]), 512), KC, 512))
        for i in range(2):
            slot, wd = dsts[i]
            wdst_tile[0] = slot
            for mb in range(4):
                m = i * 4 + mb
                for hf in range(2):
                    ps = nb()
                    mm_fm(ps, wd, KC, mb * 128, lambda k: (ht[k][hf], ht[k][hf].ap))
                    pg.op("act", lambda e, ps=ps, m=m, hf=hf: e.activation(out=ut[m][hf].ap, in_=ps.ap, func=AF.Gelu_apprx_tanh),
                          reads=[ps], writes=[ut[m][hf]])
        dsts[3] = (slots[0], load_w(slots[0], wview(win, KC, wc(chunks[3]), 512), KC, 512))
        for tb in range(8):
            hf = tb // 4
            vg = vg_t[tb % 2]
            vgf = vg.ap.bitcast(F32)
            vn = vn_t[tb % 2]
            for i in range(2):
                slot, wd = dsts[2 + i]
                ps = nb()
                for k in range(KC):
                    pg.op("pe", lambda e, k=k, ps=ps, wd=wd: e.matmul(ps.ap, lhsT=hv[:, k, tb * 128:(tb + 1) * 128], rhs=wd[:, k, :],
                                                                      start=(k == 0), stop=(k == KC - 1)), reads=[ht[k][hf], slot], writes=[ps])
                pg.op("act", lambda e, ps=ps, i=i: e.activation(out=vgf[:, i * 512:(i + 1) * 512], in_=ps.ap, func=AF.Gelu_apprx_tanh,
                                                              accum_out=SM[:, i:i + 1]), reads=[ps], writes=[vg, smt])
                junk = nt()
                pg.op("act", lambda e, i=i, junk=junk: e.activation(out=junk.ap, in_=vgf[:, i * 512:(i + 1) * 512], func=AF.Square,
                                                                    accum_out=SM[:, 2 + i:3 + i]), reads=[vg], writes=[junk, smt])
            pg.op("dve", lambda e: e.tensor_tensor(out=SM[:, 4:5], in0=SM[:, 0:1], in1=SM[:, 1:2], op=ALU.add), reads=[smt], writes=[smt])
            pg.op("dve", lambda e: e.tensor_tensor(out=SM[:, 5:6], in0=SM[:, 2:3], in1=SM[:, 3:4], op=ALU.add), reads=[smt], writes=[smt])
            pg.op("dve", lambda e: e.tensor_scalar(out=SM[:, 4:6], in0=SM[:, 4:6], scalar1=1.0 / 1024, scalar2=None, op0=ALU.mult), reads=[smt], writes=[smt])
            pg.op("dve", lambda e: e.tensor_tensor(out=SM[:, 6:7], in0=SM[:, 4:5], in1=SM[:, 4:5], op=ALU.mult), reads=[smt], writes=[smt])
            pg.op("dve", lambda e: e.tensor_tensor(out=SM[:, 6:7], in0=SM[:, 5:6], in1=SM[:, 6:7], op=ALU.subtract), reads=[smt], writes=[smt])
            pg.op("act", lambda e: e.activation(out=SM[:, 7:8], in_=SM[:, 6:7], func=AF.Sqrt, bias=EPS, scale=1.0), reads=[smt], writes=[smt])
            pg.op("dve", lambda e: e.reciprocal(out=SM[:, 7:8], in_=SM[:, 7:8]), reads=[smt], writes=[smt])
            pg.op("dve", lambda e: e.tensor_scalar(out=vgf, in0=vgf, scalar1=SM[:, 4:5], scalar2=SM[:, 7:8], op0=ALU.subtract, op1=ALU.mult),
                  reads=[vg, smt], writes=[vg])
            pg.op("dve", lambda e: e.tensor_tensor(out=vgf, in0=vgf, in1=lng_bc, op=ALU.mult), reads=[vg, sgc_t], writes=[vg])
            pg.op("dve", lambda e: e.tensor_tensor(out=vn.ap, in0=vgf, in1=lnb_bc, op=ALU.add), reads=[vg, sgc_t], writes=[vn])
            b0, b1 = nb2()
            psq = PS[bank.index(b0) // 2][:, :].rearrange("p (g t) -> p g t", g=8)
            for g in range(8):
                bt = b0 if g < 4 else b1
                pg.op("pe", lambda e, g=g: e.matmul(psq[:, g, :], lhsT=vn.ap[:, g * 128:(g + 1) * 128], rhs=wTm[:, g, :], start=True, stop=True),
                      reads=[vn, sgc_t], writes=[bt])
            vg3 = vgf.rearrange("p (g t) -> p g t", g=8)
            pg.op("dve", lambda e: e.tensor_tensor(out=vg3, in0=psq, in1=bs_bc, op=ALU.add), reads=[b0, b1, sgc_t], writes=[vg])
            pg.op("dve", lambda e, tb=tb: e.tensor_tensor(out=ybt[tb].ap, in0=vg3, in1=uT[:, :, tb * 128:(tb + 1) * 128], op=ALU.mult),
                  reads=[vg] + [ut[m][hf] for m in range(8)], writes=[ybt[tb]])
        wold = arena_toks(slots)
        qT = WA[:, 0:8192]
        kT = WA[:, 8192:16384]
        Vv = WA[:, 16384:24576].rearrange("p (b f) -> p b f", b=64)
        qt_ = [Tile(qT[:, g * 512:(g + 1) * 512], inherit=wold) for g in range(16)]
        kt_ = [Tile(kT[:, r * 1024:(r + 1) * 1024], inherit=wold) for r in range(8)]
        vt_ = [Tile(Vv[:, r * 8:(r + 1) * 8, :], inherit=wold) for r in range(8)]
        g1f = g1_ap.rearrange("a b -> (a b)")
        off = bass.IndirectOffsetOnAxis(ap=IDX[:, 0:1], axis=0)
        qsrc0 = g1f[0:1048576].rearrange("(n t) -> n t", t=1024)
        vsrc0 = g1f[0:1048576].rearrange("(n b f) -> n b f", b=8, f=128)
        for r in range(8):
            base = r * X1
            pg.dma("pool", [lambda e, r=r: e.indirect_dma_start(out=qT[:, r * 1024:(r + 1) * 1024], out_offset=None, in_=qsrc0, in_offset=off,
                                                                 element_offset=base)],
                   reads=[g1_t, idxt], writes=[qt_[2 * r], qt_[2 * r + 1]])
            pg.dma("pool", [lambda e, r=r: e.indirect_dma_start(out=kT[:, r * 1024:(r + 1) * 1024], out_offset=None, in_=qsrc0, in_offset=off,
                                                                 element_offset=base + 1048576)],
                   reads=[g1_t, idxt], writes=[kt_[r]])
            pg.dma("pool", [lambda e, r=r: e.indirect_dma_start(out=WA[:, 16384 + r * 1024: 16384 + (r + 1) * 1024], out_offset=None, in_=qsrc0, in_offset=off,
                                                                 element_offset=base + 2 * 1048576)],
                   reads=[g1_t, idxt], writes=[vt_[r]])
        if not fused and os.environ.get('DBGQ'):
            final_toks.append(pg.dma("sp", [lambda e: e.dma_start(out=dbg_q[:, :], in_=qT)], reads=qt_))
            final_toks.append(pg.dma("sp", [lambda e: e.dma_start(out=dbg_k[:, :], in_=kT)], reads=kt_))
            final_toks.append(pg.dma("sp", [lambda e: e.dma_start(out=dbg_v[:, :], in_=WA[:, 16384:24576])], reads=vt_))
        mold = arena_toks(alltiles(ut) + vn_t + vg_t)
        SP2 = [Tile(MA[:, i * 1024:(i + 1) * 1024], inherit=mold) for i in range(3)]
        A2 = [Tile(MA[:, 3072 + i * 1024: 3072 + (i + 1) * 1024], inherit=mold) for i in range(3)]
        R2 = [Tile(MA[:, 6144 + i * 1024: 6144 + (i + 1) * 1024], inherit=mold) for i in range(2)]
        v3 = lambda t_: t_.ap.rearrange("p (h c) -> p h c", h=2)
        E2t = [[tmp[2 * i], tmp[2 * i + 1]] for i in range(3)]
        E2v = [TM[:, i * 1024:(i + 1) * 1024].rearrange("p (h c) -> p h c", h=2) for i in range(3)]
        Z2t = [[bank[2 * i], bank[2 * i + 1]] for i in range(3)]
        Z2v = [PS[i][:, :].rearrange("p (h c) -> p h c", h=2) for i in range(3)]
        psO = [bank[6], bank[7]]
        tiles = []
        for qg in range(16):
            for kb in range(4 * qg + 3, -1, -1):
                c0 = max(0, kb - 4 * qg) * 128
                tiles.append(dict(qg=qg, kb=kb, c0=c0, n=512 - c0, first=(kb == 4 * qg + 3), last=(kb == 0), diag=(kb >= 4 * qg)))
        NT = len(tiles)

        def S0(i):
            t = tiles[i]
            qg, kb, c0, n = t["qg"], t["kb"], t["c0"], t["n"]
            if t["first"]:
                po = psO[qg % 2]
                pg.op("pe", lambda e: e.matmul(po.ap, lhsT=zeros_b, rhs=qT[:, qg * 512:(qg + 1) * 512], start=True, stop=False),
                      reads=[cbt, qt_[qg]], writes=[po])
                pg.op("dve", lambda e: e.memset(R2[qg % 2].ap, 0.0), writes=[R2[qg % 2]])
            zt_, zv_ = Z2t[i % 3], Z2v[i % 3]
            for hh in range(2):
                pg.op("pe", lambda e, hh=hh: e.matmul(zv_[:, hh, 0:n], lhsT=kT[hh * 64:(hh + 1) * 64, kb * 128:(kb + 1) * 128],
                                                       rhs=qT[hh * 64:(hh + 1) * 64, qg * 512 + c0:(qg + 1) * 512], start=True, stop=False),
                      reads=[kt_[kb // 8], qt_[qg]], writes=[zt_[hh]])

        def S1(i):
            t = tiles[i]
            n = t["n"]
            zt_, zv_ = Z2t[i % 3], Z2v[i % 3]
            et_, ev_ = E2t[i % 3], E2v[i % 3]
            SP = SP2[i % 3]
            pg.op("act", lambda e: e.activation(out=ev_[:, :, 0:n], in_=zv_[:, :, 0:n], func=AF.Exp), reads=zt_, writes=et_)
            pg.op("act", lambda e: e.activation(out=v3(SP)[:, :, 0:n], in_=ev_[:, :, 0:n], func=AF.Ln, bias=1.0, scale=1.0), reads=et_, writes=[SP])
            if t["diag"]:
                for hh in range(2):
                    pg.op("dve", lambda e, hh=hh: e.tensor_tensor(out=v3(SP)[:, hh, 0:128], in0=v3(SP)[:, hh, 0:128], in1=maskLT_b, op=ALU.mult),
                          reads=[SP, cbt], writes=[SP])

        def S2(i):
            t = tiles[i]
            qg, c0, n = t["qg"], t["c0"], t["n"]
            zt_, zv_ = Z2t[i % 3], Z2v[i % 3]
            SP = SP2[i % 3]
            R = R2[qg % 2]
            for hh in range(2):
                pg.op("pe", lambda e, hh=hh: e.matmul(zv_[:, hh, 0:n], lhsT=negtri_b, rhs=v3(SP)[:, hh, 0:n], start=False, stop=t["first"]),
                      reads=[SP, cbt], writes=[zt_[hh]])
                if not t["first"]:
                    pg.op("pe", lambda e, hh=hh: e.matmul(zv_[:, hh, 0:n], lhsT=negones_b, rhs=v3(R)[:, hh, c0:512], start=False, stop=True),
                          reads=[R, cbt], writes=[zt_[hh]])
            if not t["last"]:
                pg.op("dve", lambda e: e.tensor_tensor(out=v3(R)[:, :, c0:512], in0=v3(R)[:, :, c0:512], in1=v3(SP)[:, :, 0:n], op=ALU.add),
                      reads=[R, SP], writes=[R])

        def S3(i):
            t = tiles[i]
            n = t["n"]
            zt_, zv_ = Z2t[i % 3], Z2v[i % 3]
            A_ = A2[i % 3]
            pg.op("act", lambda e: e.activation(out=v3(A_)[:, :, 0:n], in_=zv_[:, :, 0:n], func=AF.Exp), reads=zt_, writes=[A_])
            if t["diag"]:
                for hh in range(2):
                    pg.op("dve", lambda e, hh=hh: e.tensor_tensor(out=v3(A_)[:, hh, 0:128], in0=v3(A_)[:, hh, 0:128], in1=maskLT_b, op=ALU.mult),
                          reads=[A_, cbt], writes=[A_])

        def S4(i):
            t = tiles[i]
            qg, kb, c0, n = t["qg"], t["kb"], t["c0"], t["n"]
            A_ = A2[i % 3]
            po = psO[qg % 2]
            for hh in range(2):
                pg.op("pe", lambda e, hh=hh: e.matmul(po.ap[hh * 64:(hh + 1) * 64, c0:512], lhsT=Vv[:, kb, hh * 64:(hh + 1) * 64], rhs=v3(A_)[:, hh, 0:n],
                                                       start=False, stop=(t["last"])), reads=[A_, vt_[kb // 8]], writes=[po])
            if t["last"]:
                pg.op("act", lambda e: e.activation(out=qt_[qg].ap, in_=po.ap, func=AF.Copy), reads=[po], writes=[qt_[qg]])

        for it in range(NT + 4):
            if 0 <= it - 4 < NT:
                S4(it - 4)
            if 0 <= it - 3 < NT:
                S3(it - 3)
            if 0 <= it - 2 < NT:
                S2(it - 2)
            if 0 <= it - 1 < NT:
                S1(it - 1)
            if it < NT:
                S0(it)
        s2v = send2_ap.rearrange("a b -> (a b)").rearrange("(r p t) -> r p t", r=8, p=128)
        send2_t = Tile(None)
        for r in range(8):
            pg.dma("sp", [lambda e, r=r: e.dma_start(out=s2v[r], in_=qT[:, r * 1024:(r + 1) * 1024])],
                   reads=[qt_[2 * r], qt_[2 * r + 1]], writes=[send2_t])
        return dict(yat=yat, ybt=ybt, send2_t=send2_t, wa_tiles=qt_ + kt_ + vt_, ma_tiles=SP2 + A2 + R2, yc_inherit=arena_toks(zdead))

    def phase_C(l, Bst, g2_ap, g2_t, x_src, final):
        win = W[("w_in", l)]
        pvl = l * PV_L
        yat, ybt = Bst["yat"], Bst["ybt"]
        yct = [Tile(ycv[:, j, :], inherit=Bst.get("yc_inherit")) for j in range(8)]
        xa_tiles.extend(yct)
        g2f = g2_ap.rearrange("a b -> (a b)")
        off = bass.IndirectOffsetOnAxis(ap=IDX[:, 0:1], axis=0)
        wold = arena_toks(Bst["wa_tiles"])
        slots = [Tile(WA[:, i * 4096:(i + 1) * 4096], inherit=wold) for i in range(4)]
        gt = Tile(WA[:, 16384:20480], inherit=wold)
        mt = Tile(WA[:, 20480:24576], inherit=wold)
        gv = WA[:, 16384:20480].bitcast(F32).rearrange("p (m t) -> p m t", m=2)
        mv = WA[:, 20480:24576].bitcast(F32).rearrange("p (m t) -> p m t", m=2)
        mold = arena_toks(Bst["ma_tiles"])
        mxv = MA[:].rearrange("p (k t) -> p k t", k=KC)
        mxt = [[Tile(mxv[:, k, hf * 512:(hf + 1) * 512], inherit=mold) for hf in range(2)] for k in range(KC)]
        ysrc = [
            (lambda k, hf: (yat[k][hf], yat[k][hf].ap)),
            (lambda k, hf: (ybt[(hf * 4)], ybv[:, k, hf * 512:(hf + 1) * 512])),
            (lambda k, hf: (yct[k], ycv[:, k, hf * 512:(hf + 1) * 512])),
        ]
        yb_all = ybt
        wouts = [W[("w_out_conv", l)], W[("w_out_sgu", l)], W[("w_out_sb", l)]]
        chunks = []
        for mg in range(8):
            for br in range(3):
                chunks.append(("g", br, mg))
                chunks.append(("o", br, mg))
        dsts = {}

        def issue(i):
            kind, br, mg = chunks[i]
            if kind == "g":
                dsts[i] = (slots[i % 4], load_w(slots[i % 4], wview(win, KC, wc(C_G + br * 2048 + mg * 256), 256), KC, 256))
            else:
                dsts[i] = (slots[i % 4], load_w(slots[i % 4], wview(wouts[br], 8, mg * 256, 256), 8, 256))

        for i in range(3):
            issue(i)
        src0 = g2f[0:X2].rearrange("(n t) -> n t", t=1024)
        for j in range(8):
            pg.dma("pool", [lambda e, j=j: e.indirect_dma_start(out=ycv[:, j, :], out_offset=None, in_=src0, in_offset=off, element_offset=j * X2)],
                   reads=[g2_t, idxt], writes=[yct[j]])
        for i, (kind, br, mg) in enumerate(chunks):
            if i + 3 < len(chunks):
                issue(i + 3)
            slot, wd = dsts[i]
            wdst_tile[0] = slot
            for mb in range(2):
                m = mg * 2 + mb
                for hf in range(2):
                    ps = nb()
                    if kind == "g":
                        mm_fm(ps, wd, KC, mb * 128, lambda k: (ht[k][hf], ht[k][hf].ap))
                        bcol = pvl + PV_BG + br * 16 + m
                        pg.op("act", lambda e, ps=ps, mb=mb, hf=hf, bcol=bcol: e.activation(
                            out=gv[:, mb, hf * 512:(hf + 1) * 512], in_=ps.ap, func=AF.Sigmoid, bias=PV[:, bcol:bcol + 1], scale=1.0),
                            reads=[ps, pvt], writes=[gt])
                    else:
                        if br == 1:
                            for k in range(8):
                                pg.op("pe", lambda e, k=k, ps=ps: e.matmul(ps.ap, lhsT=wd[:, k, mb * 128:(mb + 1) * 128],
                                                                           rhs=ybv[:, k, hf * 512:(hf + 1) * 512], start=(k == 0), stop=(k == 7)),
                                      reads=[slot] + yb_all[hf * 4:(hf + 1) * 4], writes=[ps])
                        else:
                            mm_fm(ps, wd, 8, mb * 128, lambda k: ysrc[br](k, hf))
                        gsl = gv[:, mb, hf * 512:(hf + 1) * 512]
                        msl = mv[:, mb, hf * 512:(hf + 1) * 512]
                        if br == 0:
                            pg.op("dve", lambda e, ps=ps, gsl=gsl, msl=msl: e.tensor_tensor(out=msl, in0=ps.ap, in1=gsl, op=ALU.mult),
                                  reads=[ps, gt], writes=[mt])
                        else:
                            tt = nt()
                            pg.op("dve", lambda e, ps=ps, gsl=gsl, tt=tt: e.tensor_tensor(out=tt.ap, in0=ps.ap, in1=gsl, op=ALU.mult),
                                  reads=[ps, gt], writes=[tt])
                            if br == 1:
                                pg.op("dve", lambda e, msl=msl, tt=tt: e.tensor_tensor(out=msl, in0=msl, in1=tt.ap, op=ALU.add),
                                      reads=[mt, tt], writes=[mt])
                            else:
                                pg.op("dve", lambda e, msl=msl, tt=tt, m=m, hf=hf: e.tensor_tensor(out=mxt[m][hf].ap, in0=msl, in1=tt.ap, op=ALU.add),
                                      reads=[mt, tt], writes=[mxt[m][hf]])
        ydead = arena_toks(alltiles(yat) + ybt + yct + xa_tiles)
        for row in xt:
            for t_ in row:
                t_.r.extend(ydead)
        load_x(x_src)
        wold = arena_toks(slots + [gt, mt])
        slots = [Tile(WA[:, i * 8192:(i + 1) * 8192], inherit=wold) for i in range(3)]
        wo = W[("w_o", l)]
        dsts = {}
        for i in range(2):
            dsts[i] = (slots[i % 3], load_w(slots[i % 3], wview(wo, KC, i * 512, 512), KC, 512))
        for i in range(4):
            if i + 2 < 4:
                dsts[i + 2] = (slots[(i + 2) % 3], load_w(slots[(i + 2) % 3], wview(wo, KC, (i + 2) * 512, 512), KC, 512))
            slot, wd = dsts[i]
            wdst_tile[0] = slot
            for eb in range(4):
                e_ = i * 4 + eb
                for hf in range(2):
                    ps = nb()
                    mm_fm(ps, wd, KC, eb * 128, lambda k: (mxt[k][hf], mxt[k][hf].ap))
                    pg.op("dve", lambda e, ps=ps, e_=e_, hf=hf: e.tensor_tensor(out=xt[e_][hf].ap, in0=ps.ap, in1=xt[e_][hf].ap, op=ALU.add),
                          reads=[ps, xt[e_][hf]], writes=[xt[e_][hf]])
        rmsnorm(pvl + PV_G2, ht)
        w1, w2 = W[("w_ff1", l)], W[("w_ff2", l)]
        mdead = arena_toks(alltiles(mxt))
        fv = [MA[:, i * 4096:(i + 1) * 4096].rearrange("p (k t) -> p k t", k=4) for i in range(2)]
        ft = [[[Tile(fv[i][:, k, hf * 512:(hf + 1) * 512], inherit=mdead) for hf in range(2)] for k in range(4)] for i in range(2)]
        NG = 16
        dsts = {}

        def issue_f(i):
            fg, which = i // 2, i % 2
            if which == 0:
                dsts[i] = (slots[i % 3], load_w(slots[i % 3], wview(w1, KC, fg * 512, 512), KC, 512))
            else:
                src = w2.rearrange("(k p) c -> p k c", p=128)[:, fg * 4:(fg + 1) * 4, :]
                dsts[i] = (slots[i % 3], load_w(slots[i % 3], src, 4, 2048))

        issue_f(0)
        issue_f(1)
        for i in range(2 * NG):
            if i + 2 < 2 * NG:
                issue_f(i + 2)
            fg, which = i // 2, i % 2
            slot, wd = dsts[i]
            wdst_tile[0] = slot
            fcur = ft[fg % 2]
            if which == 0:
                for fb in range(4):
                    for hf in range(2):
                        ps = nb()
                        mm_fm(ps, wd, KC, fb * 128, lambda k: (ht[k][hf], ht[k][hf].ap))
                        rl = nt()
                        pg.op("act", lambda e, ps=ps, rl=rl: e.activation(out=rl.ap, in_=ps.ap, func=AF.Relu), reads=[ps], writes=[rl])
                        pg.op("dve", lambda e, ps=ps, rl=rl, fb=fb, hf=hf: e.tensor_tensor(out=fcur[fb][hf].ap, in0=ps.ap, in1=rl.ap, op=ALU.mult),
                              reads=[ps, rl], writes=[fcur[fb][hf]])
            else:
                for e_ in range(KC):
                    for hf in range(2):
                        ps = nb()
                        mm_fm(ps, wd, 4, e_ * 128, lambda k: (fcur[k][hf], fcur[k][hf].ap))
                        pg.op("dve", lambda e, ps=ps, e_=e_, hf=hf: e.tensor_tensor(out=xt[e_][hf].ap, in0=ps.ap, in1=xt[e_][hf].ap, op=ALU.add),
                              reads=[ps, xt[e_][hf]], writes=[xt[e_][hf]])
        if final:
            rmsnorm(PV_FG, None, dst_f32_out=out_d)
        del ma_last[:]
        ma_last.extend([t_ for a_ in ft for b_ in a_ for t_ in b_])
        return dict(slots=slots)

    def store(dst, src_ap, reads):
        n = src_ap.shape[1]
        q = n // 4
        fns = [lambda e, a=a: e.dma_start(out=dst[:, a * q:(a + 1) * q], in_=src_ap[:, a * q:(a + 1) * q]) for a in range(4)]
        final_toks.append(pg.dma("sp", fns, reads=reads))

    def load(dst_ap, src, writes):
        n = dst_ap.shape[1]
        q = n // 4
        fns = [lambda e, a=a: e.dma_start(out=dst_ap[:, a * q:(a + 1) * q], in_=src[:, a * q:(a + 1) * q]) for a in range(4)]
        pg.dma("sp", fns, writes=writes)

    def allgather(src, dst, src_t):
        dst_t = Tile(None)
        pg.cc(lambda e: e.collective_compute("AllGather", ALU.bypass, replica_groups=[list(range(NCORES))],
                                             ins=[src.opt()], outs=[dst.opt()]), reads=[src_t], writes=[dst_t])
        return dst_t

    load_consts()
    if fused:
        load_x(xT_d)
        inherit = []
        for li, l in enumerate(layers):
            A = phase_A(l, send1, sendh, inherit, after_halo=lambda st: allgather(sendh, gh, st))
            gh_t = A["gh_t"]
            g1_t = allgather(send1, g1, A["send1_t"])
            Bst = phase_B(l, A, g1, gh, g1_t, gh_t, send2)
            g2_t = allgather(send2, g2, Bst["send2_t"])
            Cst = phase_C(l, Bst, g2, g2_t, ("xsp", A["xsp_t"]), final=(li == len(layers) - 1))
            inherit = arena_toks(Cst["slots"])
    elif mode == "A":
        load_x(xT_d)
        A = phase_A(layer, send1, sendh, [])
        final_toks.extend(A["send1_t"].toks() + A["sendh_t"].toks())
        store(hT_o, HA[:], alltiles(ht))
        store(zT_o, XA[:, 0:8 * ZWP], alltiles(A["zt"]))
    elif mode == "B":
        load(HA[:], hT_i, alltiles(ht))
        zt = [[Tile(zv[:, m, HALO + hf * 512: HALO + (hf + 1) * 512]) for hf in range(2)] for m in range(8)]
        zhalo_t = [Tile(zv[:, m, 0:HALO]) for m in range(8)]
        load(XA[:, 0:8 * ZWP], zT_i, alltiles(zt) + zhalo_t)
        A = dict(zt=zt, zhalo_t=zhalo_t, slots=carve_w(3, 8192, []),
                 qst=[Tile(MA[:, i * 1024:(i + 1) * 1024]) for i in range(2)],
                 vst=[Tile(MA[:, 2048 + i * 4096: 2048 + (i + 1) * 4096]) for i in range(2)])
        g1_t, gh_t = Tile(None), Tile(None)
        Bst = phase_B(layer, A, g1, gh, g1_t, gh_t, send2)
        final_toks.extend(Bst["send2_t"].toks())
        store(ya_o, XAb[:, 17408:25600], alltiles(Bst["yat"]))
        store(yb_o, XAb[:, 0:8192], Bst["ybt"])
    elif mode == "C":
        load(HA[:], hT_i, alltiles(ht))
        yat = [[Tile(yav[:, m, hf * 512:(hf + 1) * 512]) for hf in range(2)] for m in range(8)]
        ybt = [Tile(ybv[:, :, tb * 128:(tb + 1) * 128]) for tb in range(8)]
        load(XAb[:, 17408:25600], ya_i, alltiles(yat))
        load(XAb[:, 0:8192], yb_i, ybt)
        Bst = dict(yat=yat, ybt=ybt, wa_tiles=[], ma_tiles=[], yc_inherit=[])
        g2_t = Tile(None)
        phase_C(layer, Bst, g2, g2_t, xT_d, final=last)
        if not last:
            xo = out_d.rearrange("(k p) t -> p k t", p=128)
            fns = [lambda e, k0=k0: e.dma_start(out=xo[:, k0:k0 + 4, :], in_=xv[:, k0:k0 + 4, :]) for k0 in range(0, KC, 4)]
            final_toks.append(pg.dma("sp", fns, reads=alltiles(xt)))
    pg._wait("sp", final_toks)
    stack.close()
    return nc


def _pack_pv(inp, lmap):
    pv = np.zeros((128, PV_N), np.float32)
    for slot, l in lmap.items():
        o = slot * PV_L
        pv[:, o + PV_G1:o + PV_G1 + 16] = np.asarray(inp["attn_norm_g"][l]).reshape(16, 128).T
        pv[:, o + PV_G2:o + PV_G2 + 16] = np.asarray(inp["mlp_norm_g"][l]).reshape(16, 128).T
        pv[:, o + PV_BG:o + PV_BG + 48] = np.asarray(inp["b_gate"][l]).reshape(48, 128).T
        cw = np.asarray(inp["conv_w"][l])
        pv[:, o + PV_CW:o + PV_CW + 248] = cw.reshape(CW, 8, 128).transpose(2, 1, 0).reshape(128, 248)
        pv[:, o + PV_CB:o + PV_CB + 8] = np.asarray(inp["conv_b"][l]).reshape(8, 128).T
        pv[:, o + PV_LG:o + PV_LG + 8] = np.asarray(inp["conv_ln_g"][l]).reshape(8, 128).T
        pv[:, o + PV_LB:o + PV_LB + 8] = np.asarray(inp["conv_ln_b"][l]).reshape(8, 128).T
    pv[:, PV_FG:PV_FG + 16] = np.asarray(inp["final_norm_g"]).reshape(16, 128).T
    return pv


def _consts():
    i = np.arange(128)
    c = np.zeros((128, 5 * 128), np.float32)
    c[:, 0:128] = 1.0
    c[:, 128:256] = (i[:, None] <= i[None, :])
    c[:, 256:384] = (i[:, None] < i[None, :])
    c[:, 384:512] = -1.0 * (i[:, None] >= i[None, :])
    c[:, 512:640] = -1.0
    return c


def _pack_sgu(inp, l):
    a = np.zeros((128, 4096), np.float32)
    a[:, 0:1024] = np.asarray(inp["sgu_ln_g"][l])[None, :]
    a[:, 1024:2048] = np.asarray(inp["sgu_ln_b"][l])[None, :]
    a[:, 2048:3072] = np.asarray(inp["sgu_b"][l]).reshape(1, 1024)
    a[:, 3072:4096] = np.asarray(inp["sgu_w"][l]).transpose(2, 0, 1).reshape(128, 1024)
    return a


def _core_small(c):
    p = np.arange(128, dtype=np.int32)
    idx = np.stack([c * 128 + p, max(c - 1, 0) * 128 + p], axis=1).astype(np.int32)
    flag = np.full((128, 1), 0.0 if c == 0 else 1.0, np.float32)
    return idx, flag


_CACHE = {}


def _get(mode, last=False):
    key = (mode, last)
    if key not in _CACHE:
        _CACHE[key] = build(mode, 0, last)
    return _CACHE[key]


WNAMES = ["w_in", "w_out_conv", "w_out_sgu", "w_out_sb", "w_o", "w_ff1", "w_ff2"]


def _run(nc, maps):
    res = run_bass_kernel_spmd(nc, maps, core_ids=list(range(NCORES)))
    return res.results


def kernel_unfused(inp, nlayers=2, debug=None):
    x = np.asarray(inp["x"], np.float32)[0]
    xs = [np.ascontiguousarray(x[c * T:(c + 1) * T].T) for c in range(NCORES)]
    cst = _consts()
    small = [_core_small(c) for c in range(NCORES)]
    for l in range(nlayers):
        pv = _pack_pv(inp, {0: l})
        sgu = _pack_sgu(inp, l)
        wl = {nm: np.ascontiguousarray(np.asarray(inp[nm][l], np.float32)) for nm in WNAMES}
        base = [dict(pv=pv, cst=cst, idx=small[c][0], flag=small[c][1]) for c in range(NCORES)]
        w_in = wl["w_in"]
        w_inA = np.ascontiguousarray(np.concatenate([w_in[:, 0:2048], w_in[:, 4096:7168]], axis=1))
        w_inB = np.ascontiguousarray(w_in[:, 2048:4096])
        w_inC = np.ascontiguousarray(w_in[:, C_G:])
        maps = [dict(base[c], xT=xs[c], w_in0=w_inA) for c in range(NCORES)]
        ra = _run(_get("A"), maps)
        g1 = np.concatenate([ra[c]["send1"] for c in range(NCORES)], axis=0)
        gh = np.concatenate([ra[c]["sendh"] for c in range(NCORES)], axis=0)
        if debug is not None:
            debug[f"A{l}"] = ra
        maps = [dict(base[c], w_in0=w_inB, sgu0=sgu, g1=g1, gh=gh, hT_i=ra[c]["hT_o"], zT_i=ra[c]["zT_o"]) for c in range(NCORES)]
        rb = _run(_get("B"), maps)
        g2 = np.concatenate([rb[c]["send2"] for c in range(NCORES)], axis=0)
        if debug is not None:
            debug[f"B{l}"] = rb
        last = (l == nlayers - 1)
        maps = [dict(base[c], xT=xs[c], g2=g2, hT_i=ra[c]["hT_o"], ya_i=rb[c]["ya_o"], yb_i=rb[c]["yb_o"],
                     **{f"{nm}0": (w_inC if nm == "w_in" else wl[nm]) for nm in WNAMES}) for c in range(NCORES)]
        rc = _run(_get("C", last), maps)
        xs = [rc[c]["out"] for c in range(NCORES)]
    out = np.concatenate([xs[c].T for c in range(NCORES)], axis=0)[None]
    return np.ascontiguousarray(out.astype(np.float32))


def kernel_fused(inp):
    x = np.asarray(inp["x"], np.float32)[0]
    cst = _consts()
    pv = _pack_pv(inp, {0: 0, 1: 1})
    shared = dict(pv=pv, cst=cst)
    for l in range(2):
        shared[f"sgu{l}"] = _pack_sgu(inp, l)
        for nm in WNAMES:
            shared[f"{nm}{l}"] = np.ascontiguousarray(np.asarray(inp[nm][l], np.float32))
    maps = []
    for c in range(NCORES):
        idx, flag = _core_small(c)
        maps.append(dict(shared, idx=idx, flag=flag, xT=np.ascontiguousarray(x[c * T:(c + 1) * T].T)))
    res = _run(_get("F"), maps)
    out = np.concatenate([res[c]["out"].T for c in range(NCORES)], axis=0)[None]
    return np.ascontiguousarray(out.astype(np.float32))


def kernel(**inputs):
    if os.environ.get("KFUSED", "1") == "1":
        return kernel_fused(inputs)
    return kernel_unfused(inputs)
```

```python
import os
import numpy as np
import ml_dtypes
import concourse.bass as bass
import concourse.mybir as mybir
from concourse.bass_utils import run_bass_kernel_spmd

F32 = mybir.dt.float32
BF16 = mybir.dt.bfloat16
I32 = mybir.dt.int32
AF = mybir.ActivationFunctionType
ALU = mybir.AluOpType

NCORES = 8
D = 2048
KC = 16
T = 1024
S = 8192
IN_DIM = 13312
DFF = 8192
EPS = 1e-6
CW = 31
HALO = 30
ZW = T + HALO
ZWP = 1056
C_PA, C_PB, C_Q, C_K, C_V, C_G = 0, 2048, 4096, 5120, 6144, 7168

PV_G1, PV_G2, PV_BG, PV_CW, PV_CB, PV_LG, PV_LB = 0, 16, 32, 80, 80 + 248, 80 + 256, 80 + 264
PV_L = 80 + 272
PV_FG = 2 * PV_L
PV_N = PV_FG + 16

X1 = 3 * 8 * 128 * 1024
XH = 128 * 8 * HALO
X2 = 8 * 128 * 1024


class Tile:
    __slots__ = ("ap", "w", "r")

    def __init__(self, ap, inherit=None):
        self.ap = ap
        self.w = None
        self.r = list(inherit) if inherit else []

    def toks(self):
        return ([self.w] if self.w else []) + list(self.r)


class Prog:
    def __init__(self, nc, sems):
        self.nc = nc
        self.eng = {"pe": nc.tensor, "act": nc.scalar, "dve": nc.vector, "pool": nc.gpsimd, "sp": nc.sync}
        self.sem = sems
        self.cnt = {k: 0 for k in sems}
        self.seen = {e: {} for e in self.eng}
        self.rr = {"sp": 0, "pool": 0}
        self.ndma = {"sp": [k for k in sems if k.startswith("dsp")], "pool": [k for k in sems if k.startswith("dpl")]}

    def _wait(self, e, toks):
        need = {}
        for (s, v) in toks:
            if need.get(s, 0) < v:
                need[s] = v
        for s, v in need.items():
            if s == e and e == "pe":
                continue
            if self.seen[e].get(s, 0) >= v:
                continue
            self.eng[e].wait_ge(self.sem[s], v)
            self.seen[e][s] = v

    def _deps(self, reads, writes):
        toks = []
        for t in reads:
            if t.w:
                toks.append(t.w)
        for t in writes:
            toks.extend(t.toks())
        return toks

    def _commit(self, tok, reads, writes):
        for t in reads:
            t.r.append(tok)
        for t in writes:
            t.w = tok
            t.r = []

    def op(self, e, fn, reads=(), writes=()):
        self._wait(e, self._deps(reads, writes))
        ins = fn(self.eng[e])
        self.cnt[e] += 1
        ins.then_inc(self.sem[e], 1)
        tok = (e, self.cnt[e])
        self._commit(tok, reads, writes)
        return tok

    def dma(self, q, fns, reads=(), writes=()):
        names = self.ndma[q]
        s = names[self.rr[q] % len(names)]
        self.rr[q] += 1
        toks = self._deps(reads, writes)
        toks.append((s, self.cnt[s]))
        self._wait(q, toks)
        for fn in fns:
            ins = fn(self.eng[q])
            ins.then_inc(self.sem[s], 16)
            self.cnt[s] += 16
        tok = (s, self.cnt[s])
        self._commit(tok, reads, writes)
        return tok

    def cc(self, fn, reads=(), writes=()):
        self._wait("pool", self._deps(reads, writes))
        ins = fn(self.eng["pool"])
        self.cnt["cc"] += 1
        ins.then_inc(self.sem["cc"], 1)
        tok = ("cc", self.cnt["cc"])
        self._commit(tok, reads, writes)
        return tok

    def finish(self, e, tiles):
        toks = []
        for t in tiles:
            toks.extend(t.toks())
        self._wait(e, toks)


def _ctx_enter(stack, cm):
    return stack.enter_context(cm)


def build(mode, layer=0, last=False):
    import contextlib
    nc = bass.Bass("TRN2", target_bir_lowering=False)
    fused = mode == "F"
    layers = [0, 1] if fused else [layer]
    stack = contextlib.ExitStack()

    def din(name, shape, dt):
        return nc.dram_tensor(name, list(shape), dt, kind="ExternalInput").ap()

    def dout(name, shape, dt):
        return nc.dram_tensor(name, list(shape), dt, kind="ExternalOutput").ap()

    def dint(name, shape, dt):
        return nc.dram_tensor(name, list(shape), dt).ap()

    W = {}
    need_w = {"A": ["w_in"], "B": ["w_in"], "C": ["w_in", "w_out_conv", "w_out_sgu", "w_out_sb", "w_o", "w_ff1", "w_ff2"]}
    wshapes = {"w_in": (D, IN_DIM), "w_out_conv": (1024, D), "w_out_sgu": (1024, D), "w_out_sb": (1024, D),
               "w_o": (D, D), "w_ff1": (D, DFF), "w_ff2": (DFF, D)}
    win_cols = {"A": 5120, "B": 2048, "C": 6144, "F": IN_DIM}[mode]
    wshapes["w_in"] = (D, win_cols)
    wc = {"A": (lambda c: c if c < 2048 else c - 2048), "B": (lambda c: c - 2048), "C": (lambda c: c - C_G), "F": (lambda c: c)}[mode]
    for l in layers:
        for nm in (need_w["C"] if fused else need_w[mode]):
            W[(nm, l)] = din(f"{nm}{l}", wshapes[nm], F32)
    pv_d = din("pv", (128, PV_N), F32)
    cst_d = din("cst", (128, 5 * 128), F32)
    idx_d = din("idx", (128, 2), I32)
    flag_d = din("flag", (128, 1), F32)
    sgu_d = {}
    if fused or mode == "B":
        for l in layers:
            sgu_d[l] = din(f"sgu{l}", (128, 1024 * 4), F32)

    if fused or mode in ("A", "C"):
        xT_d = din("xT", (D, T), F32)
    if fused:
        out_d = dout("out", (D, T), F32)
        send1 = dint("send1", (16, 2 * 1048576 // 16), BF16)
        g1 = dint("g1", (128, 2 * 1048576 // 16), BF16)
        send1v = dint("send1v", (16, 1048576 // 16), BF16)
        g1v = dint("g1v", (128, 1048576 // 16), BF16)
        sendh = dint("sendh", (16, XH // 16), F32)
        gh = dint("gh", (128, XH // 16), F32)
        send2 = dint("send2", (16, X2 // 16), BF16)
        g2 = dint("g2", (128, X2 // 16), BF16)
        xsp = dint("xsp", (128, KC * T), F32)
    else:
        if mode == "A":
            send1 = dout("send1", (16, X1 // 16), BF16)
            sendh = dout("sendh", (16, XH // 16), F32)
            hT_o = dout("hT_o", (128, KC * T), BF16)
            zT_o = dout("zT_o", (128, 8 * ZWP), F32)
        if mode == "B":
            g1 = din("g1", (128, X1 // 16), BF16)
            gh = din("gh", (128, XH // 16), F32)
            hT_i = din("hT_i", (128, KC * T), BF16)
            zT_i = din("zT_i", (128, 8 * ZWP), F32)
            send2 = dout("send2", (16, X2 // 16), BF16)
            if os.environ.get("DBGQ") or os.environ.get("DBGT"):
                dbg_q = dout("dbg_q", (128, 8192), BF16)
                dbg_k = dout("dbg_k", (128, 8192), BF16)
                dbg_v = dout("dbg_v", (128, 8192), BF16)
                dbg_t = dout("dbg_t", (128, 3 * 4 * 512), F32)
            ya_o = dout("ya_o", (128, 8 * T), BF16)
            yb_o = dout("yb_o", (128, 8 * T), BF16)
        if mode == "C":
            g2 = din("g2", (128, X2 // 16), BF16)
            hT_i = din("hT_i", (128, KC * T), BF16)
            ya_i = din("ya_i", (128, 8 * T), BF16)
            yb_i = din("yb_i", (128, 8 * T), BF16)
            out_d = dout("out", (D, T), F32)

    def sb(name, shape, dt):
        return _ctx_enter(stack, nc.sbuf_tensor(name, list(shape), dt))

    XA = sb("XA", (128, KC * T), F32)
    HA = sb("HA", (128, KC * T), BF16)
    WA = sb("WA", (128, 3 * 8192), BF16)
    MA = sb("MA", (128, 16384), BF16)
    TM = sb("TM", (128, 6 * 512), F32)
    PV = sb("PV", (128, PV_N), F32)
    CF = sb("CF", (128, 5 * 128), F32)
    CB = sb("CB", (128, 5 * 128), BF16)
    IDX = sb("IDX", (128, 2), I32)
    FLG = sb("FLG", (128, 1), F32)
    SM = sb("SM", (128, 64), F32)
    HL = sb("HL", (128, 8 * HALO), F32)
    PS = [_ctx_enter(stack, nc.psum_tensor(f"PS{i}", [128, 1024], F32)) for i in range(4)]

    sem_names = ["pe", "act", "dve", "pool", "sp", "cc"] + [f"dsp{i}" for i in range(12)] + [f"dpl{i}" for i in range(12)]
    sems = {n: _ctx_enter(stack, nc.semaphore(n)) for n in sem_names}
    pg = Prog(nc, sems)

    xv = XA[:].rearrange("p (k t) -> p k t", k=KC)
    hv = HA[:].rearrange("p (k t) -> p k t", k=KC)
    xt = [[Tile(xv[:, k, hf * 512:(hf + 1) * 512]) for hf in range(2)] for k in range(KC)]
    ht = [[Tile(hv[:, k, hf * 512:(hf + 1) * 512]) for hf in range(2)] for k in range(KC)]
    bank = [Tile(PS[i // 2][:, (i % 2) * 512:(i % 2 + 1) * 512]) for i in range(8)]
    tmp = [Tile(TM[:, i * 512:(i + 1) * 512]) for i in range(6)]
    pvt = Tile(PV[:])
    cft = Tile(CF[:])
    cbt = Tile(CB[:])
    idxt = Tile(IDX[:])
    flgt = Tile(FLG[:])
    smt = Tile(SM[:])
    hlt = Tile(HL[:])
    onesF = CF[:, 0:128]
    maskLE = CF[:, 128:256]
    maskLT_b = CB[:, 256:384]
    negtri_b = CB[:, 384:512]
    negones_b = CB[:, 512:640]
    zeros_b = CB[:, 0:128]

    state = {"bank": 0, "tmp": 0}

    def nb():
        b = bank[state["bank"] % 8]
        state["bank"] += 1
        return b

    def nt():
        t = tmp[state["tmp"] % 3]
        state["tmp"] += 1
        return t

    LT = [tmp[3], tmp[4], tmp[5]]

    def alltiles(tt):
        return [t for row in tt for t in row]

    XAb = XA[:].bitcast(BF16)
    zv = XA[:, 0:8 * ZWP].rearrange("p (m t) -> p m t", m=8)
    ybv = XAb[:, 0:8192].rearrange("p (m t) -> p m t", m=8)
    ycv = XAb[:, 8192:16384].rearrange("p (m t) -> p m t", m=8)
    yav = XAb[:, 17408:25600].rearrange("p (m t) -> p m t", m=8)
    XFREE = 12800
    accv = [XA[:, XFREE + i * 1024: XFREE + (i + 1) * 1024] for i in range(2)]
    sguc = XA[:, XFREE: XFREE + 3584]

    def load_consts():
        pg.dma("sp", [lambda e: e.dma_start(out=PV[:], in_=pv_d[:, :])], writes=[pvt])
        pg.dma("sp", [lambda e: e.dma_start(out=CF[:], in_=cst_d[:, :])], writes=[cft])
        pg.dma("sp", [lambda e: e.dma_start(out=IDX[:], in_=idx_d[:, :])], writes=[idxt])
        pg.dma("sp", [lambda e: e.dma_start(out=FLG[:], in_=flag_d[:, :])], writes=[flgt])
        pg.op("dve", lambda e: e.tensor_copy(out=CB[:], in_=CF[:]), reads=[cft], writes=[cbt])
        pg.op("dve", lambda e: e.memset(CB[:, 0:128], 0.0), writes=[cbt])

    def wview(wap, rows_kc, c0, ncols):
        return wap.rearrange("(k p) c -> p k c", p=128)[:, 0:rows_kc, c0:c0 + ncols]

    wslots = {}

    def carve_w(n, size, inherit):
        sl = []
        for i in range(n):
            sl.append(Tile(WA[:, i * size:(i + 1) * size], inherit=inherit))
        return sl

    def arena_toks(tiles):
        toks = []
        for t in tiles:
            toks.extend(t.toks())
        best = {}
        for s, v in toks:
            if best.get(s, 0) < v:
                best[s] = v
        return list(best.items())

    def load_w(slot, src, kcs, ncols):
        dst = slot.ap[:, 0:kcs * ncols].rearrange("p (k c) -> p k c", k=kcs)
        step = max(1, 512 // 128 * 1)
        step = 4
        fns = []
        for k0 in range(0, kcs, step):
            k1 = min(kcs, k0 + step)
            fns.append(lambda e, k0=k0, k1=k1: e.dma_start(out=dst[:, k0:k1, :], in_=src[:, k0:k1, :]))
        pg.dma("pool", fns, writes=[slot])
        return dst

    def rmsnorm(gcol0, dst_tiles, dst_f32_out=None):
        for hf in range(2):
            ps = nb()
            for k in range(KC):
                sq = nt()
                pg.op("act", lambda e, k=k, sq=sq: e.activation(out=sq.ap, in_=xt[k][hf].ap, func=AF.Square),
                      reads=[xt[k][hf]], writes=[sq])
                pg.op("pe", lambda e, k=k, sq=sq: e.matmul(ps.ap, lhsT=onesF, rhs=sq.ap, start=(k == 0), stop=(k == KC - 1)),
                      reads=[sq, cft], writes=[ps])
            rs = LT[0]
            pg.op("act", lambda e: e.activation(out=rs.ap, in_=ps.ap, func=AF.Sqrt, bias=EPS, scale=1.0 / D),
                  reads=[ps], writes=[rs])
            pg.op("dve", lambda e: e.reciprocal(out=rs.ap, in_=rs.ap), reads=[rs], writes=[rs])
            for k in range(KC):
                if dst_f32_out is None:
                    pg.op("dve", lambda e, k=k: e.scalar_tensor_tensor(
                        out=dst_tiles[k][hf].ap, in0=xt[k][hf].ap, scalar=PV[:, gcol0 + k:gcol0 + k + 1],
                        in1=rs.ap, op0=ALU.mult, op1=ALU.mult),
                        reads=[xt[k][hf], rs, pvt], writes=[dst_tiles[k][hf]])
                else:
                    o = nt()
                    pg.op("dve", lambda e, k=k, o=o: e.scalar_tensor_tensor(
                        out=o.ap, in0=xt[k][hf].ap, scalar=PV[:, gcol0 + k:gcol0 + k + 1],
                        in1=rs.ap, op0=ALU.mult, op1=ALU.mult),
                        reads=[xt[k][hf], rs, pvt], writes=[o])
                    final_toks.append(pg.dma("sp", [lambda e, k=k, o=o: e.dma_start(
                        out=dst_f32_out[k * 128:(k + 1) * 128, hf * 512:(hf + 1) * 512], in_=o.ap)],
                        reads=[o]))

    def load_x(src):
        if isinstance(src, tuple):
            fns = [lambda e, k0=k0: e.dma_start(out=XA[:, k0 * T:(k0 + 4) * T], in_=xsp[:, k0 * T:(k0 + 4) * T]) for k0 in range(0, KC, 4)]
            pg.dma("sp", fns, reads=[src[1]], writes=alltiles(xt))
            return
        v = src.rearrange("(k p) t -> p k t", p=128)
        fns = [lambda e, k0=k0: e.dma_start(out=xv[:, k0:k0 + 4, :], in_=v[:, k0:k0 + 4, :]) for k0 in range(0, KC, 4)]
        pg.dma("sp", fns, writes=alltiles(xt))

    xa_tiles = []
    final_toks = []
    ma_last = []

    def nb2():
        if state["bank"] % 2:
            state["bank"] += 1
        b0 = bank[state["bank"] % 8]
        b1 = bank[(state["bank"] + 1) % 8]
        state["bank"] += 2
        return b0, b1

    def mm_fm(ps, wdst, kcs, col0, rhs_tiles_fn):
        for k in range(kcs):
            rt, rap = rhs_tiles_fn(k)
            pg.op("pe", lambda e, k=k, rap=rap: e.matmul(ps.ap, lhsT=wdst[:, k, col0:col0 + 128], rhs=rap,
                                                          start=(k == 0), stop=(k == kcs - 1)),
                  reads=[rt, wdst_tile[0]], writes=[ps])

    wdst_tile = [None]


    def conv_taps(l, zt, zhalo_t, gh_ap, gh_t, xinh):
        pvl = l * PV_L
        ghv = gh_ap.rearrange("a b -> (a b)").rearrange("(n c) -> n c", c=8 * HALO)
        pg.dma("pool", [lambda e: e.indirect_dma_start(
            out=HL[:], out_offset=None, in_=ghv, in_offset=bass.IndirectOffsetOnAxis(ap=IDX[:, 1:2], axis=0))],
            reads=[gh_t, idxt], writes=[hlt])
        hlv = HL[:].rearrange("p (m t) -> p m t", m=8)
        pg.op("dve", lambda e: e.tensor_scalar(out=zv[:, :, 0:HALO], in0=hlv, scalar1=FLG[:, 0:1], scalar2=None, op0=ALU.mult),
              reads=[hlt, flgt], writes=zhalo_t)
        acct = [Tile(accv[i], inherit=xinh) for i in range(2)]
        xa_tiles.extend(acct)
        cw0 = pvl + PV_CW
        for m in range(8):
            acc = acct[m % 2]
            zall = [zt[m][0], zt[m][1], zhalo_t[m]]
            pg.op("dve", lambda e, m=m, acc=acc: e.tensor_scalar(
                out=acc.ap, in0=zv[:, m, 0:T], scalar1=PV[:, cw0 + m * CW: cw0 + m * CW + 1],
                scalar2=PV[:, pvl + PV_CB + m: pvl + PV_CB + m + 1], op0=ALU.mult, op1=ALU.add),
                reads=zall + [pvt], writes=[acc])
            for k in range(1, CW - 1):
                pg.op("dve", lambda e, m=m, acc=acc, k=k: e.scalar_tensor_tensor(
                    out=acc.ap, in0=zv[:, m, k:k + T], scalar=PV[:, cw0 + m * CW + k: cw0 + m * CW + k + 1],
                    in1=acc.ap, op0=ALU.mult, op1=ALU.add), reads=zall + [acc, pvt], writes=[acc])
            k = CW - 1
            pg.op("dve", lambda e, m=m, acc=acc, k=k: e.scalar_tensor_tensor(
                out=zv[:, m, HALO:HALO + T], in0=zv[:, m, k:k + T], scalar=PV[:, cw0 + m * CW + k: cw0 + m * CW + k + 1],
                in1=acc.ap, op0=ALU.mult, op1=ALU.add), reads=zall + [acc, pvt], writes=[zt[m][0], zt[m][1]])
        return acct

    def conv_norm(l, zt, yat):
        pvl = l * PV_L
        s1 = [nb(), nb()]
        s2 = [nb(), nb()]
        for m in range(8):
            for hf in range(2):
                sq = nt()
                pg.op("act", lambda e, sq=sq, hf=hf, m=m: e.activation(out=sq.ap, in_=zt[m][hf].ap, func=AF.Square),
                      reads=[zt[m][hf]], writes=[sq])
                pg.op("pe", lambda e, hf=hf, m=m: e.matmul(s1[hf].ap, lhsT=onesF, rhs=zt[m][hf].ap,
                                                           start=(m == 0), stop=(m == 7)), reads=[zt[m][hf], cft], writes=[s1[hf]])
                pg.op("pe", lambda e, sq=sq, hf=hf, m=m: e.matmul(s2[hf].ap, lhsT=onesF, rhs=sq.ap,
                                                                  start=(m == 0), stop=(m == 7)), reads=[sq, cft], writes=[s2[hf]])
        for hf in range(2):
            mean, var, rstd = LT[0], LT[1], LT[2]
            pg.op("act", lambda e: e.activation(out=mean.ap, in_=s1[hf].ap, func=AF.Copy, scale=1.0 / 1024), reads=[s1[hf]], writes=[mean])
            pg.op("dve", lambda e: e.tensor_tensor(out=var.ap, in0=mean.ap, in1=mean.ap, op=ALU.mult), reads=[mean], writes=[var])
            pg.op("dve", lambda e: e.scalar_tensor_tensor(out=var.ap, in0=s2[hf].ap, scalar=1.0 / 1024, in1=var.ap,
                                                          op0=ALU.mult, op1=ALU.subtract), reads=[s2[hf], var], writes=[var])
            pg.op("act", lambda e: e.activation(out=rstd.ap, in_=var.ap, func=AF.Sqrt, bias=EPS, scale=1.0), reads=[var], writes=[rstd])
            pg.op("dve", lambda e: e.reciprocal(out=rstd.ap, in_=rstd.ap), reads=[rstd], writes=[rstd])
            for m in range(8):
                t1 = nt()
                pg.op("dve", lambda e, m=m, t1=t1: e.tensor_tensor(out=t1.ap, in0=zt[m][hf].ap, in1=mean.ap, op=ALU.subtract),
                      reads=[zt[m][hf], mean], writes=[t1])
                pg.op("dve", lambda e, t1=t1: e.tensor_tensor(out=t1.ap, in0=t1.ap, in1=rstd.ap, op=ALU.mult),
                      reads=[t1, rstd], writes=[t1])
                pg.op("act", lambda e, m=m, t1=t1: e.activation(
                    out=yat[m][hf].ap, in_=t1.ap, func=AF.Silu,
                    scale=PV[:, pvl + PV_LG + m: pvl + PV_LG + m + 1], bias=PV[:, pvl + PV_LB + m: pvl + PV_LB + m + 1]),
                    reads=[t1, pvt], writes=[yat[m][hf]])

    def phase_A(l, send1_ap, sendh_ap, carve_inherit, after_halo=None, after_qk=None):
        win = W[("w_in", l)]
        rmsnorm(l * PV_L + PV_G1, ht)
        if fused:
            fns = [lambda e, k0=k0: e.dma_start(out=xsp[:, k0 * T:(k0 + 4) * T], in_=XA[:, k0 * T:(k0 + 4) * T]) for k0 in range(0, KC, 4)]
            xsp_t = Tile(None)
            pg.dma("sp", fns, reads=alltiles(xt), writes=[xsp_t])
        else:
            xsp_t = None
        slots = carve_w(3, 8192, carve_inherit)
        s1f = send1_ap.rearrange("a b -> (a b)")
        if fused:
            s1vf = send1v.rearrange("a b -> (a b)")
            sec = [s1f[0:1048576], s1f[1048576:2 * 1048576], s1vf[0:1048576]]
        else:
            sec = [s1f[i * 1048576:(i + 1) * 1048576] for i in range(3)]
        send1v_t = Tile(None)
        qsec = [sec[i].rearrange("(j p t) -> j p t", j=8, p=128) for i in range(2)]
        vsec = sec[2].rearrange("(j p b f) -> j p b f", j=8, p=128, b=8)
        send1_t = Tile(None)
        minh = arena_toks(ma_last)
        qst = [Tile(MA[:, i * 1024:(i + 1) * 1024], inherit=minh) for i in range(2)]
        vst = [Tile(MA[:, 2048 + i * 4096: 2048 + (i + 1) * 4096], inherit=minh) for i in range(2)]
        chunks = [("pv", C_PA, 0), ("pg", C_PA + 1024, 0), ("pv", C_PA + 512, 1), ("pg", C_PA + 1536, 1),
                  ("q", C_Q, 0), ("q", C_Q + 512, 1), ("k", C_K, 0), ("k", C_K + 512, 1), ("v", C_V, 0), ("v", C_V + 512, 1)]
        shv = sendh_ap.rearrange("a b -> (a b)").rearrange("(p m t) -> p m t", p=128, m=8)
        sendh_t = Tile(None)
        ret = {}
        xinh = arena_toks(alltiles(xt))
        zt = [[Tile(zv[:, m, HALO + hf * 512: HALO + (hf + 1) * 512], inherit=xinh) for hf in range(2)] for m in range(8)]
        zhalo_t = [Tile(zv[:, m, 0:HALO], inherit=xinh) for m in range(8)]
        xa_tiles.extend(alltiles(zt) + zhalo_t)
        dsts = {}

        def issue(i):
            kind, c0, ix = chunks[i]
            dsts[i] = (slots[i % 3], load_w(slots[i % 3], wview(win, KC, wc(c0), 512), KC, 512))

        issue(0)
        issue(1)
        qn = 0
        for i, (kind, c0, ix) in enumerate(chunks):
            if i + 2 < len(chunks) and kind != "pg":
                issue(i + 2)
            slot, wd = dsts[i]
            wdst_tile[0] = slot
            if kind in ("q", "k"):
                for jb in range(4):
                    j = ix * 4 + jb
                    st = qst[qn % 2]
                    qn += 1
                    for hf in range(2):
                        ps = nb()
                        mm_fm(ps, wd, KC, jb * 128, lambda k: (ht[k][hf], ht[k][hf].ap))
                        pg.op("act", lambda e, ps=ps, st=st, hf=hf: e.activation(
                            out=st.ap[:, hf * 512:(hf + 1) * 512], in_=ps.ap, func=AF.Copy,
                            scale=(0.125 if kind == "q" else 1.0)), reads=[ps], writes=[st])
                    sidx = 0 if kind == "q" else 1
                    pg.dma("sp", [lambda e, st=st, j=j, sidx=sidx: e.dma_start(out=qsec[sidx][j], in_=st.ap)],
                           reads=[st], writes=[send1_t])
                if kind == "k" and ix == 1 and after_qk is not None:
                    ret["g1_t"] = after_qk(send1_t)
            elif kind == "v":
                st = vst[ix % 2]
                stv = st.ap.rearrange("p (b f) -> p b f", b=8)
                for tb in range(8):
                    ps = nb()
                    hf = tb // 4
                    for k in range(KC):
                        pg.op("pe", lambda e, k=k, ps=ps, tb=tb: e.matmul(
                            ps.ap, lhsT=hv[:, k, tb * 128:(tb + 1) * 128], rhs=wd[:, k, :],
                            start=(k == 0), stop=(k == KC - 1)), reads=[ht[k][hf], slot], writes=[ps])
                    pg.op("act", lambda e, ps=ps, tb=tb: e.activation(out=stv[:, tb, :], in_=ps.ap, func=AF.Copy), reads=[ps], writes=[st])
                for jj in range(4):
                    j = ix * 4 + jj
                    pg.dma("sp", [lambda e, j=j, jj=jj: e.dma_start(out=vsec[j], in_=stv[:, :, jj * 128:(jj + 1) * 128])],
                           reads=[st], writes=[send1v_t if fused else send1_t])
            elif kind == "pv":
                pass
            elif kind == "pg":
                slot_v, wd_v = dsts[i - 1]
                for mb in range(4):
                    m = ix * 4 + mb
                    for hf in range(2):
                        psv = nb()
                        psg = nb()
                        wdst_tile[0] = slot_v
                        mm_fm(psv, wd_v, KC, mb * 128, lambda k: (ht[k][hf], ht[k][hf].ap))
                        wdst_tile[0] = slot
                        mm_fm(psg, wd, KC, mb * 128, lambda k: (ht[k][hf], ht[k][hf].ap))
                        sg = nt()
                        pg.op("act", lambda e, psg=psg, sg=sg: e.activation(out=sg.ap, in_=psg.ap, func=AF.Sigmoid),
                              reads=[psg], writes=[sg])
                        pg.op("dve", lambda e, psv=psv, sg=sg, m=m, hf=hf: e.tensor_tensor(
                            out=zt[m][hf].ap, in0=psv.ap, in1=sg.ap, op=ALU.mult),
                            reads=[psv, sg], writes=[zt[m][hf]])
                if i + 2 < len(chunks):
                    issue(i + 2)
                if ix == 1:
                    pg.dma("sp", [lambda e: e.dma_start(out=shv, in_=zv[:, :, T:T + HALO])], reads=[zt[m][1] for m in range(8)], writes=[sendh_t])
                    if after_halo is not None:
                        ret["gh_t"] = after_halo(sendh_t)
                        ret["acct"] = conv_taps(l, zt, zhalo_t, gh, ret["gh_t"], xinh)
        if after_halo is not None:
            yat_ = [[Tile(yav[:, m, hf * 512:(hf + 1) * 512], inherit=xinh) for hf in range(2)] for m in range(8)]
            xa_tiles.extend(alltiles(yat_))
            conv_norm(l, zt, yat_)
            ret["yat"] = yat_
        return dict(gh_t=ret.get("gh_t"), yat=ret.get("yat"), acct=ret.get("acct"), g1_t=ret.get("g1_t"), send1v_t=send1v_t, zt=zt, zhalo_t=zhalo_t, send1_t=send1_t, sendh_t=sendh_t, slots=slots, xsp_t=xsp_t, qst=qst, vst=vst)

    def phase_B(l, A, g1_ap, gh_ap, g1_t, gh_t, send2_ap):
        win = W[("w_in", l)]
        zt, zhalo_t = A["zt"], A["zhalo_t"]
        pvl = l * PV_L
        xinh = arena_toks(alltiles(xt))
        yat = [[Tile(yav[:, m, hf * 512:(hf + 1) * 512], inherit=xinh) for hf in range(2)] for m in range(8)]
        xa_tiles.extend(alltiles(yat))
        if A.get("yat") is not None:
            yat = A["yat"]
            acct = A["acct"]
        else:
            acct = conv_taps(l, zt, zhalo_t, gh_ap, gh_t, xinh)
            conv_norm(l, zt, yat)
        zdead = [t for row in zt for t in row] + zhalo_t
        ybt = [Tile(ybv[:, :, tb * 128:(tb + 1) * 128], inherit=arena_toks(zdead)) for tb in range(8)]
        sgc_t = Tile(sguc, inherit=arena_toks(acct))
        xa_tiles.extend(ybt + [sgc_t])
        pg.dma("sp", [lambda e: e.dma_start(out=sguc[:, 0:3072], in_=sgu_d[l][:, 0:3072])], writes=[sgc_t])
        lng_bc = sguc[:, 0:1024]
        lnb_bc = sguc[:, 1024:2048]
        bs_bc = sguc[:, 2048:3072].rearrange("p (g t) -> p g t", g=8)
        wTm = sguc[:, 3072:3584].bitcast(BF16).rearrange("p (g t) -> p g t", g=8)
        wtmp = [LT[0], LT[1]]
        for i in range(2):
            pg.dma("sp", [lambda e, i=i: e.dma_start(out=wtmp[i].ap, in_=sgu_d[l][:, 3072 + i * 512: 3072 + (i + 1) * 512])], writes=[wtmp[i]])
            for gg in range(4):
                g = i * 4 + gg
                pg.op("dve", lambda e, i=i, gg=gg, g=g: e.tensor_tensor(out=wTm[:, g, :], in0=wtmp[i].ap[:, gg * 128:(gg + 1) * 128],
                                                                        in1=maskLE, op=ALU.mult), reads=[wtmp[i], cft], writes=[sgc_t])
        slots = A["slots"]
        uT = MA[:, 0:8192].rearrange("p (m t) -> p m t", m=8)
        stg = A["qst"] + A["vst"]
        ut = [[Tile(uT[:, m, hf * 512:(hf + 1) * 512], inherit=arena_toks(stg)) for hf in range(2)] for m in range(8)]
        vn_t = [Tile(MA[:, 8192 + i * 1024: 8192 + (i + 1) * 1024], inherit=arena_toks(stg)) for i in range(2)]
        vg_t = [Tile(MA[:, 10240 + i * 2048: 10240 + (i + 1) * 2048], inherit=arena_toks(stg)) for i in range(2)]
        chunks = [C_PB, C_PB + 512, C_PB + 1024, C_PB + 1536]
        dsts = {}
        for i in range(3):
            dsts[i] = (slots[i % 3], load_w(slots[i % 3], wview(win, KC, wc(chunks[i]), 512), KC, 512))
        for i in range(2):
            slot, wd = dsts[i]
            wdst_tile[0] = slot
            for mb in range(4):
                m = i * 4 + mb
                for hf in range(2):
                    ps = nb()
                    mm_fm(ps, wd, KC, mb * 128, lambda k: (ht[k][hf], ht[k][hf].ap))
                    pg.op("act", lambda e, ps=ps, m=m, hf=hf: e.activation(out=ut[m][hf].ap, in_=ps.ap, func=AF.Gelu_apprx_tanh),
                          reads=[ps], writes=[ut[m][hf]])
        dsts[3] = (slots[0], load_w(slots[0], wview(win, KC, wc(chunks[3]), 512), KC, 512))
        for tb in range(8):
            hf = tb // 4
            vg = vg_t[tb % 2]
            vgf = vg.ap.bitcast(F32)
            vn = vn_t[tb % 2]
            for i in range(2):
                slot, wd = dsts[2 + i]
                ps = nb()
                for k in range(KC):
                    pg.op("pe", lambda e, k=k, ps=ps, wd=wd: e.matmul(ps.ap, lhsT=hv[:, k, tb * 128:(tb + 1) * 128], rhs=wd[:, k, :],
                                                                      start=(k == 0), stop=(k == KC - 1)), reads=[ht[k][hf], slot], writes=[ps])
                pg.op("act", lambda e, ps=ps, i=i: e.activation(out=vgf[:, i * 512:(i + 1) * 512], in_=ps.ap, func=AF.Gelu_apprx_tanh,
                                                              accum_out=SM[:, i:i + 1]), reads=[ps], writes=[vg, smt])
                junk = nt()
                pg.op("act", lambda e, i=i, junk=junk: e.activation(out=junk.ap, in_=vgf[:, i * 512:(i + 1) * 512], func=AF.Square,
                                                                    accum_out=SM[:, 2 + i:3 + i]), reads=[vg], writes=[junk, smt])
            pg.op("dve", lambda e: e.tensor_tensor(out=SM[:, 4:5], in0=SM[:, 0:1], in1=SM[:, 1:2], op=ALU.add), reads=[smt], writes=[smt])
            pg.op("dve", lambda e: e.tensor_tensor(out=SM[:, 5:6], in0=SM[:, 2:3], in1=SM[:, 3:4], op=ALU.add), reads=[smt], writes=[smt])
            pg.op("dve", lambda e: e.tensor_scalar(out=SM[:, 4:6], in0=SM[:, 4:6], scalar1=1.0 / 1024, scalar2=None, op0=ALU.mult), reads=[smt], writes=[smt])
            pg.op("dve", lambda e: e.tensor_tensor(out=SM[:, 6:7], in0=SM[:, 4:5], in1=SM[:, 4:5], op=ALU.mult), reads=[smt], writes=[smt])
            pg.op("dve", lambda e: e.tensor_tensor(out=SM[:, 6:7], in0=SM[:, 5:6], in1=SM[:, 6:7], op=ALU.subtract), reads=[smt], writes=[smt])
            pg.op("act", lambda e: e.activation(out=SM[:, 7:8], in_=SM[:, 6:7], func=AF.Sqrt, bias=EPS, scale=1.0), reads=[smt], writes=[smt])
            pg.op("dve", lambda e: e.reciprocal(out=SM[:, 7:8], in_=SM[:, 7:8]), reads=[smt], writes=[smt])
            pg.op("dve", lambda e: e.tensor_scalar(out=vgf, in0=vgf, scalar1=SM[:, 4:5], scalar2=SM[:, 7:8], op0=ALU.subtract, op1=ALU.mult),
                  reads=[vg, smt], writes=[vg])
            pg.op("dve", lambda e: e.tensor_tensor(out=vgf, in0=vgf, in1=lng_bc, op=ALU.mult), reads=[vg, sgc_t], writes=[vg])
            pg.op("dve", lambda e: e.tensor_tensor(out=vn.ap, in0=vgf, in1=lnb_bc, op=ALU.add), reads=[vg, sgc_t], writes=[vn])
            b0, b1 = nb2()
            psq = PS[bank.index(b0) // 2][:, :].rearrange("p (g t) -> p g t", g=8)
            for g in range(8):
                bt = b0 if g < 4 else b1
                pg.op("pe", lambda e, g=g: e.matmul(psq[:, g, :], lhsT=vn.ap[:, g * 128:(g + 1) * 128], rhs=wTm[:, g, :], start=True, stop=True),
                      reads=[vn, sgc_t], writes=[bt])
            vg3 = vgf.rearrange("p (g t) -> p g t", g=8)
            pg.op("dve", lambda e: e.tensor_tensor(out=vg3, in0=psq, in1=bs_bc, op=ALU.add), reads=[b0, b1, sgc_t], writes=[vg])
            pg.op("dve", lambda e, tb=tb: e.tensor_tensor(out=ybt[tb].ap, in0=vg3, in1=uT[:, :, tb * 128:(tb + 1) * 128], op=ALU.mult),
                  reads=[vg] + [ut[m][hf] for m in range(8)], writes=[ybt[tb]])
        wold = arena_toks(slots)
        qT = WA[:, 0:8192]
        kT = WA[:, 8192:16384]
        Vv = WA[:, 16384:24576].rearrange("p (b f) -> p b f", b=64)
        qt_ = [Tile(qT[:, g * 512:(g + 1) * 512], inherit=wold) for g in range(16)]
        kt_ = [Tile(kT[:, r * 1024:(r + 1) * 1024], inherit=wold) for r in range(8)]
        vt_ = [Tile(Vv[:, r * 8:(r + 1) * 8, :], inherit=wold) for r in range(8)]
        g1f = g1_ap.rearrange("a b -> (a b)")
        off = bass.IndirectOffsetOnAxis(ap=IDX[:, 0:1], axis=0)
        qsrc0 = g1f[0:1048576].rearrange("(n t) -> n t", t=1024)
        if fused:
            vsrc0 = g1v.rearrange("a b -> (a b)")[0:1048576].rearrange("(n t) -> n t", t=1024)
            qk_stride, v_stride, v_off = 2 * 1048576, 1048576, 0
            g1v_t = A["g1v_t"]
        else:
            vsrc0 = qsrc0
            qk_stride, v_stride, v_off = X1, X1, 2 * 1048576
            g1v_t = g1_t
        for r in range(8):
            pg.dma("pool", [lambda e, r=r: e.indirect_dma_start(out=qT[:, r * 1024:(r + 1) * 1024], out_offset=None, in_=qsrc0, in_offset=off,
                                                                 element_offset=r * qk_stride)],
                   reads=[g1_t, idxt], writes=[qt_[2 * r], qt_[2 * r + 1]])
            pg.dma("pool", [lambda e, r=r: e.indirect_dma_start(out=kT[:, r * 1024:(r + 1) * 1024], out_offset=None, in_=qsrc0, in_offset=off,
                                                                 element_offset=r * qk_stride + 1048576)],
                   reads=[g1_t, idxt], writes=[kt_[r]])
        for r in range(8):
            pg.dma("pool", [lambda e, r=r: e.indirect_dma_start(out=WA[:, 16384 + r * 1024: 16384 + (r + 1) * 1024], out_offset=None, in_=vsrc0, in_offset=off,
                                                                 element_offset=r * v_stride + v_off)],
                   reads=[g1v_t, idxt], writes=[vt_[r]])
        mold = arena_toks(alltiles(ut) + vn_t + vg_t)
        SP2 = [Tile(MA[:, i * 1024:(i + 1) * 1024], inherit=mold) for i in range(3)]
        A2 = [Tile(MA[:, 3072 + i * 1024: 3072 + (i + 1) * 1024], inherit=mold) for i in range(3)]
        R2 = [Tile(MA[:, 6144 + i * 1024: 6144 + (i + 1) * 1024], inherit=mold) for i in range(2)]
        v3 = lambda t_: t_.ap.rearrange("p (h c) -> p h c", h=2)
        E2t = [[tmp[2 * i], tmp[2 * i + 1]] for i in range(3)]
        E2v = [TM[:, i * 1024:(i + 1) * 1024].rearrange("p (h c) -> p h c", h=2) for i in range(3)]
        Z2t = [[bank[2 * i], bank[2 * i + 1]] for i in range(3)]
        Z2v = [PS[i][:, :].rearrange("p (h c) -> p h c", h=2) for i in range(3)]
        psO = [bank[6], bank[7]]
        tiles = []
        for qg in range(16):
            for kb in range(4 * qg + 3, -1, -1):
                c0 = max(0, kb - 4 * qg) * 128
                tiles.append(dict(qg=qg, kb=kb, c0=c0, n=512 - c0, first=(kb == 4 * qg + 3), last=(kb == 0), diag=(kb >= 4 * qg)))
        NT = len(tiles)

        def S0(i):
            t = tiles[i]
            qg, kb, c0, n = t["qg"], t["kb"], t["c0"], t["n"]
            if t["first"]:
                po = psO[qg % 2]
                pg.op("pe", lambda e: e.matmul(po.ap, lhsT=zeros_b, rhs=qT[:, qg * 512:(qg + 1) * 512], start=True, stop=False),
                      reads=[cbt, qt_[qg]], writes=[po])
                pg.op("dve", lambda e: e.memset(R2[qg % 2].ap, 0.0), writes=[R2[qg % 2]])
            zt_, zv_ = Z2t[i % 3], Z2v[i % 3]
            for hh in range(2):
                pg.op("pe", lambda e, hh=hh: e.matmul(zv_[:, hh, 0:n], lhsT=kT[hh * 64:(hh + 1) * 64, kb * 128:(kb + 1) * 128],
                                                       rhs=qT[hh * 64:(hh + 1) * 64, qg * 512 + c0:(qg + 1) * 512], start=True, stop=False),
                      reads=[kt_[kb // 8], qt_[qg]], writes=[zt_[hh]])

        def S1(i):
            t = tiles[i]
            n = t["n"]
            zt_, zv_ = Z2t[i % 3], Z2v[i % 3]
            et_, ev_ = E2t[i % 3], E2v[i % 3]
            SP = SP2[i % 3]
            pg.op("act", lambda e: e.activation(out=ev_[:, :, 0:n], in_=zv_[:, :, 0:n], func=AF.Exp), reads=zt_, writes=et_)
            pg.op("act", lambda e: e.activation(out=v3(SP)[:, :, 0:n], in_=ev_[:, :, 0:n], func=AF.Ln, bias=1.0, scale=1.0), reads=et_, writes=[SP])
            if t["diag"]:
                for hh in range(2):
                    pg.op("dve", lambda e, hh=hh: e.tensor_tensor(out=v3(SP)[:, hh, 0:128], in0=v3(SP)[:, hh, 0:128], in1=maskLT_b, op=ALU.mult),
                          reads=[SP, cbt], writes=[SP])

        def S2(i):
            t = tiles[i]
            qg, c0, n = t["qg"], t["c0"], t["n"]
            zt_, zv_ = Z2t[i % 3], Z2v[i % 3]
            SP = SP2[i % 3]
            R = R2[qg % 2]
            for hh in range(2):
                pg.op("pe", lambda e, hh=hh: e.matmul(zv_[:, hh, 0:n], lhsT=negtri_b, rhs=v3(SP)[:, hh, 0:n], start=False, stop=t["first"]),
                      reads=[SP, cbt], writes=[zt_[hh]])
                if not t["first"]:
                    pg.op("pe", lambda e, hh=hh: e.matmul(zv_[:, hh, 0:n], lhsT=negones_b, rhs=v3(R)[:, hh, c0:512], start=False, stop=True),
                          reads=[R, cbt], writes=[zt_[hh]])
            if not t["last"]:
                pg.op("dve", lambda e: e.tensor_tensor(out=v3(R)[:, :, c0:512], in0=v3(R)[:, :, c0:512], in1=v3(SP)[:, :, 0:n], op=ALU.add),
                      reads=[R, SP], writes=[R])

        def S3(i):
            t = tiles[i]
            n = t["n"]
            zt_, zv_ = Z2t[i % 3], Z2v[i % 3]
            A_ = A2[i % 3]
            pg.op("act", lambda e: e.activation(out=v3(A_)[:, :, 0:n], in_=zv_[:, :, 0:n], func=AF.Exp), reads=zt_, writes=[A_])
            if t["diag"]:
                for hh in range(2):
                    pg.op("dve", lambda e, hh=hh: e.tensor_tensor(out=v3(A_)[:, hh, 0:128], in0=v3(A_)[:, hh, 0:128], in1=maskLT_b, op=ALU.mult),
                          reads=[A_, cbt], writes=[A_])

        def S4(i):
            t = tiles[i]
            qg, kb, c0, n = t["qg"], t["kb"], t["c0"], t["n"]
            A_ = A2[i % 3]
            po = psO[qg % 2]
            for hh in range(2):
                pg.op("pe", lambda e, hh=hh: e.matmul(po.ap[hh * 64:(hh + 1) * 64, c0:512], lhsT=Vv[:, kb, hh * 64:(hh + 1) * 64], rhs=v3(A_)[:, hh, 0:n],
                                                       start=False, stop=(t["last"])), reads=[A_, vt_[kb // 8]], writes=[po])
            if t["last"]:
                pg.op("act", lambda e: e.activation(out=qt_[qg].ap, in_=po.ap, func=AF.Copy), reads=[po], writes=[qt_[qg]])

        for it in range(NT + 4):
            if 0 <= it - 4 < NT:
                S4(it - 4)
            if 0 <= it - 3 < NT:
                S3(it - 3)
            if 0 <= it - 2 < NT:
                S2(it - 2)
            if 0 <= it - 1 < NT:
                S1(it - 1)
            if it < NT:
                S0(it)
        s2v = send2_ap.rearrange("a b -> (a b)").rearrange("(r p t) -> r p t", r=8, p=128)
        send2_t = Tile(None)
        for r in range(8):
            pg.dma("sp", [lambda e, r=r: e.dma_start(out=s2v[r], in_=qT[:, r * 1024:(r + 1) * 1024])],
                   reads=[qt_[2 * r], qt_[2 * r + 1]], writes=[send2_t])
        return dict(yat=yat, ybt=ybt, send2_t=send2_t, wa_tiles=qt_ + kt_ + vt_, ma_tiles=SP2 + A2 + R2, yc_inherit=arena_toks(zdead))

    def phase_C(l, Bst, g2_ap, g2_t, x_src, final):
        win = W[("w_in", l)]
        pvl = l * PV_L
        yat, ybt = Bst["yat"], Bst["ybt"]
        yct = [Tile(ycv[:, j, :], inherit=Bst.get("yc_inherit")) for j in range(8)]
        xa_tiles.extend(yct)
        g2f = g2_ap.rearrange("a b -> (a b)")
        off = bass.IndirectOffsetOnAxis(ap=IDX[:, 0:1], axis=0)
        wold = arena_toks(Bst["wa_tiles"])
        slots = [Tile(WA[:, i * 4096:(i + 1) * 4096], inherit=wold) for i in range(4)]
        gt = Tile(WA[:, 16384:20480], inherit=wold)
        mt = Tile(WA[:, 20480:24576], inherit=wold)
        gv = WA[:, 16384:20480].bitcast(F32).rearrange("p (m t) -> p m t", m=2)
        mv = WA[:, 20480:24576].bitcast(F32).rearrange("p (m t) -> p m t", m=2)
        mold = arena_toks(Bst["ma_tiles"])
        mxv = MA[:].rearrange("p (k t) -> p k t", k=KC)
        mxt = [[Tile(mxv[:, k, hf * 512:(hf + 1) * 512], inherit=mold) for hf in range(2)] for k in range(KC)]
        ysrc = [
            (lambda k, hf: (yat[k][hf], yat[k][hf].ap)),
            (lambda k, hf: (ybt[(hf * 4)], ybv[:, k, hf * 512:(hf + 1) * 512])),
            (lambda k, hf: (yct[k], ycv[:, k, hf * 512:(hf + 1) * 512])),
        ]
        yb_all = ybt
        wouts = [W[("w_out_conv", l)], W[("w_out_sgu", l)], W[("w_out_sb", l)]]
        chunks = []
        for mg in range(8):
            for br in range(3):
                chunks.append(("g", br, mg))
                chunks.append(("o", br, mg))
        dsts = {}

        def issue(i):
            kind, br, mg = chunks[i]
            if kind == "g":
                dsts[i] = (slots[i % 4], load_w(slots[i % 4], wview(win, KC, wc(C_G + br * 2048 + mg * 256), 256), KC, 256))
            else:
                dsts[i] = (slots[i % 4], load_w(slots[i % 4], wview(wouts[br], 8, mg * 256, 256), 8, 256))

        for i in range(3):
            issue(i)
        src0 = g2f[0:X2].rearrange("(n t) -> n t", t=1024)
        for j in range(8):
            pg.dma("pool", [lambda e, j=j: e.indirect_dma_start(out=ycv[:, j, :], out_offset=None, in_=src0, in_offset=off, element_offset=j * X2)],
                   reads=[g2_t, idxt], writes=[yct[j]])
        for i, (kind, br, mg) in enumerate(chunks):
            if i + 3 < len(chunks):
                issue(i + 3)
            slot, wd = dsts[i]
            wdst_tile[0] = slot
            for mb in range(2):
                m = mg * 2 + mb
                for hf in range(2):
                    ps = nb()
                    if kind == "g":
                        mm_fm(ps, wd, KC, mb * 128, lambda k: (ht[k][hf], ht[k][hf].ap))
                        bcol = pvl + PV_BG + br * 16 + m
                        pg.op("act", lambda e, ps=ps, mb=mb, hf=hf, bcol=bcol: e.activation(
                            out=gv[:, mb, hf * 512:(hf + 1) * 512], in_=ps.ap, func=AF.Sigmoid, bias=PV[:, bcol:bcol + 1], scale=1.0),
                            reads=[ps, pvt], writes=[gt])
                    else:
                        if br == 1:
                            for k in range(8):
                                pg.op("pe", lambda e, k=k, ps=ps: e.matmul(ps.ap, lhsT=wd[:, k, mb * 128:(mb + 1) * 128],
                                                                           rhs=ybv[:, k, hf * 512:(hf + 1) * 512], start=(k == 0), stop=(k == 7)),
                                      reads=[slot] + yb_all[hf * 4:(hf + 1) * 4], writes=[ps])
                        else:
                            mm_fm(ps, wd, 8, mb * 128, lambda k: ysrc[br](k, hf))
                        gsl = gv[:, mb, hf * 512:(hf + 1) * 512]
                        msl = mv[:, mb, hf * 512:(hf + 1) * 512]
                        if br == 0:
                            pg.op("dve", lambda e, ps=ps, gsl=gsl, msl=msl: e.tensor_tensor(out=msl, in0=ps.ap, in1=gsl, op=ALU.mult),
                                  reads=[ps, gt], writes=[mt])
                        else:
                            tt = nt()
                            pg.op("dve", lambda e, ps=ps, gsl=gsl, tt=tt: e.tensor_tensor(out=tt.ap, in0=ps.ap, in1=gsl, op=ALU.mult),
                                  reads=[ps, gt], writes=[tt])
                            if br == 1:
                                pg.op("dve", lambda e, msl=msl, tt=tt: e.tensor_tensor(out=msl, in0=msl, in1=tt.ap, op=ALU.add),
                                      reads=[mt, tt], writes=[mt])
                            else:
                                pg.op("dve", lambda e, msl=msl, tt=tt, m=m, hf=hf: e.tensor_tensor(out=mxt[m][hf].ap, in0=msl, in1=tt.ap, op=ALU.add),
                                      reads=[mt, tt], writes=[mxt[m][hf]])
        ydead = arena_toks(alltiles(yat) + ybt + yct + xa_tiles)
        for row in xt:
            for t_ in row:
                t_.r.extend(ydead)
        load_x(x_src)
        wold = arena_toks(slots + [gt, mt])
        slots = [Tile(WA[:, i * 8192:(i + 1) * 8192], inherit=wold) for i in range(3)]
        wo = W[("w_o", l)]
        dsts = {}
        for i in range(2):
            dsts[i] = (slots[i % 3], load_w(slots[i % 3], wview(wo, KC, i * 512, 512), KC, 512))
        for i in range(4):
            if i + 2 < 4:
                dsts[i + 2] = (slots[(i + 2) % 3], load_w(slots[(i + 2) % 3], wview(wo, KC, (i + 2) * 512, 512), KC, 512))
            slot, wd = dsts[i]
            wdst_tile[0] = slot
            for eb in range(4):
                e_ = i * 4 + eb
                for hf in range(2):
                    ps = nb()
                    mm_fm(ps, wd, KC, eb * 128, lambda k: (mxt[k][hf], mxt[k][hf].ap))
                    pg.op("dve", lambda e, ps=ps, e_=e_, hf=hf: e.tensor_tensor(out=xt[e_][hf].ap, in0=ps.ap, in1=xt[e_][hf].ap, op=ALU.add),
                          reads=[ps, xt[e_][hf]], writes=[xt[e_][hf]])
        rmsnorm(pvl + PV_G2, ht)
        w1, w2 = W[("w_ff1", l)], W[("w_ff2", l)]
        mdead = arena_toks(alltiles(mxt))
        fv = [MA[:, i * 4096:(i + 1) * 4096].rearrange("p (k t) -> p k t", k=4) for i in range(2)]
        ft = [[[Tile(fv[i][:, k, hf * 512:(hf + 1) * 512], inherit=mdead) for hf in range(2)] for k in range(4)] for i in range(2)]
        NG = 16
        dsts = {}

        def issue_f(i):
            fg, which = i // 2, i % 2
            if which == 0:
                dsts[i] = (slots[i % 3], load_w(slots[i % 3], wview(w1, KC, fg * 512, 512), KC, 512))
            else:
                src = w2.rearrange("(k p) c -> p k c", p=128)[:, fg * 4:(fg + 1) * 4, :]
                dsts[i] = (slots[i % 3], load_w(slots[i % 3], src, 4, 2048))

        issue_f(0)
        issue_f(1)
        for i in range(2 * NG):
            if i + 2 < 2 * NG:
                issue_f(i + 2)
            fg, which = i // 2, i % 2
            slot, wd = dsts[i]
            wdst_tile[0] = slot
            fcur = ft[fg % 2]
            if which == 0:
                for fb in range(4):
                    for hf in range(2):
                        ps = nb()
                        mm_fm(ps, wd, KC, fb * 128, lambda k: (ht[k][hf], ht[k][hf].ap))
                        rl = nt()
                        pg.op("act", lambda e, ps=ps, rl=rl: e.activation(out=rl.ap, in_=ps.ap, func=AF.Relu), reads=[ps], writes=[rl])
                        pg.op("dve", lambda e, ps=ps, rl=rl, fb=fb, hf=hf: e.tensor_tensor(out=fcur[fb][hf].ap, in0=ps.ap, in1=rl.ap, op=ALU.mult),
                              reads=[ps, rl], writes=[fcur[fb][hf]])
            else:
                for e_ in range(KC):
                    for hf in range(2):
                        ps = nb()
                        mm_fm(ps, wd, 4, e_ * 128, lambda k: (fcur[k][hf], fcur[k][hf].ap))
                        pg.op("dve", lambda e, ps=ps, e_=e_, hf=hf: e.tensor_tensor(out=xt[e_][hf].ap, in0=ps.ap, in1=xt[e_][hf].ap, op=ALU.add),
                              reads=[ps, xt[e_][hf]], writes=[xt[e_][hf]])
        if final:
            rmsnorm(PV_FG, None, dst_f32_out=out_d)
        del ma_last[:]
        ma_last.extend([t_ for a_ in ft for b_ in a_ for t_ in b_])
        return dict(slots=slots)

    def store(dst, src_ap, reads):
        n = src_ap.shape[1]
        q = n // 4
        fns = [lambda e, a=a: e.dma_start(out=dst[:, a * q:(a + 1) * q], in_=src_ap[:, a * q:(a + 1) * q]) for a in range(4)]
        final_toks.append(pg.dma("sp", fns, reads=reads))

    def load(dst_ap, src, writes):
        n = dst_ap.shape[1]
        q = n // 4
        fns = [lambda e, a=a: e.dma_start(out=dst_ap[:, a * q:(a + 1) * q], in_=src[:, a * q:(a + 1) * q]) for a in range(4)]
        pg.dma("sp", fns, writes=writes)

    def allgather(src, dst, src_t):
        dst_t = Tile(None)
        pg.cc(lambda e: e.collective_compute("AllGather", ALU.bypass, replica_groups=[list(range(NCORES))],
                                             ins=[src.opt()], outs=[dst.opt()]), reads=[src_t], writes=[dst_t])
        return dst_t

    load_consts()
    if fused:
        load_x(xT_d)
        inherit = []
        for li, l in enumerate(layers):
            A = phase_A(l, send1, sendh, inherit, after_halo=lambda st: allgather(sendh, gh, st),
                        after_qk=lambda st: allgather(send1, g1, st))
            gh_t = A["gh_t"]
            g1_t = A["g1_t"]
            A["g1v_t"] = allgather(send1v, g1v, A["send1v_t"])
            Bst = phase_B(l, A, g1, gh, g1_t, gh_t, send2)
            g2_t = allgather(send2, g2, Bst["send2_t"])
            Cst = phase_C(l, Bst, g2, g2_t, ("xsp", A["xsp_t"]), final=(li == len(layers) - 1))
            inherit = arena_toks(Cst["slots"])
    elif mode == "A":
        load_x(xT_d)
        A = phase_A(layer, send1, sendh, [])
        final_toks.extend(A["send1_t"].toks() + A["sendh_t"].toks())
        store(hT_o, HA[:], alltiles(ht))
        store(zT_o, XA[:, 0:8 * ZWP], alltiles(A["zt"]))
    elif mode == "B":
        load(HA[:], hT_i, alltiles(ht))
        zt = [[Tile(zv[:, m, HALO + hf * 512: HALO + (hf + 1) * 512]) for hf in range(2)] for m in range(8)]
        zhalo_t = [Tile(zv[:, m, 0:HALO]) for m in range(8)]
        load(XA[:, 0:8 * ZWP], zT_i, alltiles(zt) + zhalo_t)
        A = dict(zt=zt, zhalo_t=zhalo_t, slots=carve_w(3, 8192, []),
                 qst=[Tile(MA[:, i * 1024:(i + 1) * 1024]) for i in range(2)],
                 vst=[Tile(MA[:, 2048 + i * 4096: 2048 + (i + 1) * 4096]) for i in range(2)])
        g1_t, gh_t = Tile(None), Tile(None)
        Bst = phase_B(layer, A, g1, gh, g1_t, gh_t, send2)
        final_toks.extend(Bst["send2_t"].toks())
        store(ya_o, XAb[:, 17408:25600], alltiles(Bst["yat"]))
        store(yb_o, XAb[:, 0:8192], Bst["ybt"])
    elif mode == "C":
        load(HA[:], hT_i, alltiles(ht))
        yat = [[Tile(yav[:, m, hf * 512:(hf + 1) * 512]) for hf in range(2)] for m in range(8)]
        ybt = [Tile(ybv[:, :, tb * 128:(tb + 1) * 128]) for tb in range(8)]
        load(XAb[:, 17408:25600], ya_i, alltiles(yat))
        load(XAb[:, 0:8192], yb_i, ybt)
        Bst = dict(yat=yat, ybt=ybt, wa_tiles=[], ma_tiles=[], yc_inherit=[])
        g2_t = Tile(None)
        phase_C(layer, Bst, g2, g2_t, xT_d, final=last)
        if not last:
            xo = out_d.rearrange("(k p) t -> p k t", p=128)
            fns = [lambda e, k0=k0: e.dma_start(out=xo[:, k0:k0 + 4, :], in_=xv[:, k0:k0 + 4, :]) for k0 in range(0, KC, 4)]
            final_toks.append(pg.dma("sp", fns, reads=alltiles(xt)))
    pg._wait("sp", final_toks)
    stack.close()
    return nc


def _pack_pv(inp, lmap):
    pv = np.zeros((128, PV_N), np.float32)
    for slot, l in lmap.items():
        o = slot * PV_L
        pv[:, o + PV_G1:o + PV_G1 + 16] = np.asarray(inp["attn_norm_g"][l]).reshape(16, 128).T
        pv[:, o + PV_G2:o + PV_G2 + 16] = np.asarray(inp["mlp_norm_g"][l]).reshape(16, 128).T
        pv[:, o + PV_BG:o + PV_BG + 48] = np.asarray(inp["b_gate"][l]).reshape(48, 128).T
        cw = np.asarray(inp["conv_w"][l])
        pv[:, o + PV_CW:o + PV_CW + 248] = cw.reshape(CW, 8, 128).transpose(2, 1, 0).reshape(128, 248)
        pv[:, o + PV_CB:o + PV_CB + 8] = np.asarray(inp["conv_b"][l]).reshape(8, 128).T
        pv[:, o + PV_LG:o + PV_LG + 8] = np.asarray(inp["conv_ln_g"][l]).reshape(8, 128).T
        pv[:, o + PV_LB:o + PV_LB + 8] = np.asarray(inp["conv_ln_b"][l]).reshape(8, 128).T
    pv[:, PV_FG:PV_FG + 16] = np.asarray(inp["final_norm_g"]).reshape(16, 128).T
    return pv


def _consts():
    i = np.arange(128)
    c = np.zeros((128, 5 * 128), np.float32)
    c[:, 0:128] = 1.0
    c[:, 128:256] = (i[:, None] <= i[None, :])
    c[:, 256:384] = (i[:, None] < i[None, :])
    c[:, 384:512] = -1.0 * (i[:, None] >= i[None, :])
    c[:, 512:640] = -1.0
    return c


def _pack_sgu(inp, l):
    a = np.zeros((128, 4096), np.float32)
    a[:, 0:1024] = np.asarray(inp["sgu_ln_g"][l])[None, :]
    a[:, 1024:2048] = np.asarray(inp["sgu_ln_b"][l])[None, :]
    a[:, 2048:3072] = np.asarray(inp["sgu_b"][l]).reshape(1, 1024)
    a[:, 3072:4096] = np.asarray(inp["sgu_w"][l]).transpose(2, 0, 1).reshape(128, 1024)
    return a


def _core_small(c):
    p = np.arange(128, dtype=np.int32)
    idx = np.stack([c * 128 + p, max(c - 1, 0) * 128 + p], axis=1).astype(np.int32)
    flag = np.full((128, 1), 0.0 if c == 0 else 1.0, np.float32)
    return idx, flag


_CACHE = {}


def _get(mode, last=False):
    key = (mode, last)
    if key not in _CACHE:
        _CACHE[key] = build(mode, 0, last)
    return _CACHE[key]


WNAMES = ["w_in", "w_out_conv", "w_out_sgu", "w_out_sb", "w_o", "w_ff1", "w_ff2"]


def _run(nc, maps):
    res = run_bass_kernel_spmd(nc, maps, core_ids=list(range(NCORES)))
    return res.results


def kernel_unfused(inp, nlayers=2, debug=None):
    x = np.asarray(inp["x"], np.float32)[0]
    xs = [np.ascontiguousarray(x[c * T:(c + 1) * T].T) for c in range(NCORES)]
    cst = _consts()
    small = [_core_small(c) for c in range(NCORES)]
    for l in range(nlayers):
        pv = _pack_pv(inp, {0: l})
        sgu = _pack_sgu(inp, l)
        wl = {nm: np.ascontiguousarray(np.asarray(inp[nm][l], np.float32)) for nm in WNAMES}
        base = [dict(pv=pv, cst=cst, idx=small[c][0], flag=small[c][1]) for c in range(NCORES)]
        w_in = wl["w_in"]
        w_inA = np.ascontiguousarray(np.concatenate([w_in[:, 0:2048], w_in[:, 4096:7168]], axis=1))
        w_inB = np.ascontiguousarray(w_in[:, 2048:4096])
        w_inC = np.ascontiguousarray(w_in[:, C_G:])
        maps = [dict(base[c], xT=xs[c], w_in0=w_inA) for c in range(NCORES)]
        ra = _run(_get("A"), maps)
        g1 = np.concatenate([ra[c]["send1"] for c in range(NCORES)], axis=0)
        gh = np.concatenate([ra[c]["sendh"] for c in range(NCORES)], axis=0)
        if debug is not None:
            debug[f"A{l}"] = ra
        maps = [dict(base[c], w_in0=w_inB, sgu0=sgu, g1=g1, gh=gh, hT_i=ra[c]["hT_o"], zT_i=ra[c]["zT_o"]) for c in range(NCORES)]
        rb = _run(_get("B"), maps)
        g2 = np.concatenate([rb[c]["send2"] for c in range(NCORES)], axis=0)
        if debug is not None:
            debug[f"B{l}"] = rb
        last = (l == nlayers - 1)
        maps = [dict(base[c], xT=xs[c], g2=g2, hT_i=ra[c]["hT_o"], ya_i=rb[c]["ya_o"], yb_i=rb[c]["yb_o"],
                     **{f"{nm}0": (w_inC if nm == "w_in" else wl[nm]) for nm in WNAMES}) for c in range(NCORES)]
        rc = _run(_get("C", last), maps)
        xs = [rc[c]["out"] for c in range(NCORES)]
    out = np.concatenate([xs[c].T for c in range(NCORES)], axis=0)[None]
    return np.ascontiguousarray(out.astype(np.float32))


def kernel_fused(inp):
    x = np.asarray(inp["x"], np.float32)[0]
    cst = _consts()
    pv = _pack_pv(inp, {0: 0, 1: 1})
    shared = dict(pv=pv, cst=cst)
    for l in range(2):
        shared[f"sgu{l}"] = _pack_sgu(inp, l)
        for nm in WNAMES:
            shared[f"{nm}{l}"] = np.ascontiguousarray(np.asarray(inp[nm][l], np.float32))
    maps = []
    for c in range(NCORES):
        idx, flag = _core_small(c)
        maps.append(dict(shared, idx=idx, flag=flag, xT=np.ascontiguousarray(x[c * T:(c + 1) * T].T)))
    res = _run(_get("F"), maps)
    out = np.concatenate([res[c]["out"].T for c in range(NCORES)], axis=0)[None]
    return np.ascontiguousarray(out.astype(np.float32))


def kernel(**inputs):
    if os.environ.get("KFUSED", "1") == "1":
        return kernel_fused(inputs)
    return kernel_unfused(inputs)
```
